# Optimizing a Trainium2 kernel written in Bass

```python
import math
import jax, jax.numpy as jnp
from jax import lax
import numpy as np

D_MODEL = 4096
BATCH = 4
SEQ = 2048
DEPTH = 4

N_FOURIER_GROUPS = 4
FOURIER_GROUP_DIM = D_MODEL // 16
D_FOURIER = N_FOURIER_GROUPS * FOURIER_GROUP_DIM
ATTN_HEAD_DIM = 128
N_ATTN_HEADS = (D_MODEL - D_FOURIER) // (2 * ATTN_HEAD_DIM)
D_ATTN = N_ATTN_HEADS * 2 * ATTN_HEAD_DIM
D_IN = D_FOURIER + 3 * D_ATTN
N_BRANCHES = 2
D_FF = 2 * D_MODEL
CONV_WIDTH = 3
N_REL_BUCKETS = 32
REL_MAX_DISTANCE = 128
Q_BLOCK = 128
EPS = 1e-6

kernel_name = "hybrid_fnet_diffattn_convglu_encoder"


def rms_norm(x, gain):
    xf = x.astype(jnp.float32)
    y = xf * lax.rsqrt(jnp.mean(xf * xf, axis=-1, keepdims=True) + EPS)
    return (y * gain.astype(jnp.float32)).astype(x.dtype)


def rel_bucket(rel):
    n = -rel
    half = N_REL_BUCKETS // 2
    ret = (n < 0).astype(jnp.int32) * half
    n = jnp.abs(n)
    max_exact = half // 2
    is_small = n < max_exact
    nf = jnp.maximum(n, 1).astype(jnp.float32)
    large = max_exact + (jnp.log(nf / max_exact) / math.log(REL_MAX_DISTANCE / max_exact)
                         * (half - max_exact)).astype(jnp.int32)
    large = jnp.minimum(large, half - 1)
    return ret + jnp.where(is_small, n, large)


def fourier_mix(u):
    b, s, _ = u.shape
    ug = u.reshape(b, s, N_FOURIER_GROUPS, FOURIER_GROUP_DIM).astype(jnp.float32)
    y = jnp.fft.fft2(ug, axes=(1, 3), norm="ortho").real
    return y.reshape(b, s, D_FOURIER).astype(u.dtype)


def diff_attention(q, k, v, rel_table, lam, sub_gain, lam_init):
    b, s, h, _, d = q.shape
    nblk = s // Q_BLOCK
    q = q * (d ** -0.5)
    qb = q.reshape(b, nblk, Q_BLOCK, h, 2, d).transpose(1, 0, 2, 3, 4, 5)
    key_pos = jnp.arange(s, dtype=jnp.int32)
    starts = jnp.arange(nblk, dtype=jnp.int32) * Q_BLOCK

    def block(args):
        q_blk, start = args
        q_pos = start + jnp.arange(Q_BLOCK, dtype=jnp.int32)
        bias = rel_table[rel_bucket(key_pos[None, :] - q_pos[:, None])]
        bias = bias.transpose(2, 0, 1).astype(jnp.float32)
        logits = jnp.einsum('bqhmd,bkhmd->bmhqk', q_blk, k,
                            preferred_element_type=jnp.float32) + bias[None, None]
        p = jax.nn.softmax(logits, axis=-1)
        attn = p[:, 0] - lam * p[:, 1]
        return jnp.einsum('bhqk,bkhe->bqhe', attn.astype(v.dtype), v)

    out = lax.map(block, (qb, starts))
    out = out.transpose(1, 0, 2, 3, 4).reshape(b, s, h, 2 * d)
    out = rms_norm(out, sub_gain) * (1.0 - lam_init)
    return out.reshape(b, s, h * 2 * d)


def conv_glu(h, w_up, conv_w, conv_b, w_down):
    s = h.shape[1]
    a = h @ w_up
    gate, val = a[..., :D_FF], a[..., D_FF:]
    pad = CONV_WIDTH // 2
    gp = jnp.pad(gate, ((0, 0), (pad, pad), (0, 0)))
    gate = sum(gp[:, j:j + s] * conv_w[j] for j in range(CONV_WIDTH)) + conv_b
    return (jax.nn.gelu(gate, approximate=False) * val) @ w_down


def setup_inputs(seed: int = 0) -> dict:
    key = jax.random.key(seed)
    ks = jax.random.split(key, 20)
    f32 = jnp.float32
    nrm = lambda k, shape, scale: jax.random.normal(k, shape, f32) * scale
    return {
        "x": nrm(ks[0], (BATCH, SEQ, D_MODEL), 1.0),
        "norm1_gain": 1.0 + nrm(ks[1], (DEPTH, D_MODEL), 0.02),
        "w_in": nrm(ks[2], (DEPTH, D_MODEL, D_IN), D_MODEL ** -0.5),
        "w_fourier_out": nrm(ks[3], (DEPTH, D_FOURIER, D_MODEL), D_FOURIER ** -0.5),
        "lambdas": nrm(ks[4], (DEPTH, 4, ATTN_HEAD_DIM), 0.1),
        "subln_gain": 1.0 + nrm(ks[5], (DEPTH, 2 * ATTN_HEAD_DIM), 0.02),
        "rel_bias_table": nrm(ks[6], (N_REL_BUCKETS, N_ATTN_HEADS), 0.5),
        "w_attn_out": nrm(ks[7], (DEPTH, D_ATTN, D_MODEL), D_ATTN ** -0.5),
        "w_gate": nrm(ks[8], (DEPTH, D_MODEL, N_BRANCHES * D_MODEL), D_MODEL ** -0.5),
        "b_gate": nrm(ks[9], (DEPTH, N_BRANCHES * D_MODEL), 0.02),
        "w_o": nrm(ks[10], (DEPTH, D_MODEL, D_MODEL), D_MODEL ** -0.5),
        "norm2_gain": 1.0 + nrm(ks[11], (DEPTH, D_MODEL), 0.02),
        "w_up": nrm(ks[12], (DEPTH, D_MODEL, 2 * D_FF), D_MODEL ** -0.5),
        "conv_w": nrm(ks[13], (DEPTH, CONV_WIDTH, D_FF), CONV_WIDTH ** -0.5),
        "conv_b": nrm(ks[14], (DEPTH, D_FF), 0.02),
        "w_down": nrm(ks[15], (DEPTH, D_FF, D_MODEL), D_FF ** -0.5),
        "final_norm_gain": 1.0 + nrm(ks[16], (D_MODEL,), 0.02),
    }


def reference(x, norm1_gain, w_in, w_fourier_out, lambdas, subln_gain, rel_bias_table, w_attn_out,
              w_gate, b_gate, w_o, norm2_gain, w_up, conv_w, conv_b, w_down, final_norm_gain):
    b, s, _ = x.shape
    for l in range(DEPTH):
        lam_init = 0.8 - 0.6 * math.exp(-0.3 * l)
        h = rms_norm(x, norm1_gain[l])
        u = h @ w_in[l]
        u_f = u[..., :D_FOURIER]
        q = u[..., D_FOURIER:D_FOURIER + D_ATTN].reshape(b, s, N_ATTN_HEADS, 2, ATTN_HEAD_DIM)
        k = u[..., D_FOURIER + D_ATTN:D_FOURIER + 2 * D_ATTN].reshape(b, s, N_ATTN_HEADS, 2, ATTN_HEAD_DIM)
        v = u[..., D_FOURIER + 2 * D_ATTN:].reshape(b, s, N_ATTN_HEADS, 2 * ATTN_HEAD_DIM)
        y_f = fourier_mix(u_f) @ w_fourier_out[l]
        lam_p = lambdas[l].astype(jnp.float32)
        lam = (jnp.exp(jnp.sum(lam_p[0] * lam_p[1])) - jnp.exp(jnp.sum(lam_p[2] * lam_p[3]))
               + lam_init)
        y_a = diff_attention(q, k, v, rel_bias_table, lam, subln_gain[l], lam_init) @ w_attn_out[l]
        g = jax.nn.sigmoid(h @ w_gate[l] + b_gate[l]).reshape(b, s, N_BRANCHES, D_MODEL)
        mixed = g[:, :, 0] * y_f + g[:, :, 1] * y_a
        x = x + mixed @ w_o[l]
        h2 = rms_norm(x, norm2_gain[l])
        x = x + conv_glu(h2, w_up[l], conv_w[l], conv_b[l], w_down[l])
    return rms_norm(x, final_norm_gain)
```

```python
import math
import numpy as np
import ml_dtypes
import concourse.bass as bass
import concourse.mybir as mybir
from concourse.bass_utils import run_bass_kernel_spmd

F32 = mybir.dt.float32
BF16 = mybir.dt.bfloat16
ACTF = mybir.ActivationFunctionType
ALU = mybir.AluOpType
AX = mybir.AxisListType
EPS = 1e-6
AG_MODE = "2stage"
CC_MAX_OUT = 4 * 1024 * 1024


def chunk_rows(rows, rowbytes, nr, mult=1, limit=CC_MAX_OUT):
    best = None
    for cr in range(mult, rows + 1, mult):
        if rows % cr == 0 and cr * rowbytes * nr <= limit:
            best = cr
    assert best is not None, (rows, rowbytes, nr, mult, limit)
    return best


def ag_row(local, r, cr, nr):
    return (local // cr) * nr * cr + r * cr + (local % cr)


def weight_perm(rr, cra, crb):
    shard = [np.arange(rr) + c * rr for c in range(8)]
    mid = []
    for c in range(8):
        a, b = shard[c % 4], shard[c % 4 + 4]
        mid.append(np.concatenate([np.concatenate([a[i * cra:(i + 1) * cra], b[i * cra:(i + 1) * cra]]) for i in range(rr // cra)]))
    full = []
    for c in range(8):
        g = (c // 4) * 4
        full.append(np.concatenate([np.concatenate([mid[g + r][j * crb:(j + 1) * crb] for r in range(4)]) for j in range(2 * rr // crb)]))
    for c in range(1, 8):
        assert np.array_equal(full[c], full[0])
    return full[0]
NREL = 32
RELMAX = 128


class Cfg:
    def __init__(self, D=4096, S=2048, L=4, TB=512, cc_max=CC_MAX_OUT):
        self.D, self.S, self.L, self.TB = D, S, L, TB
        self.cc_max = cc_max
        self.B = 4
        self.T = S // 2
        self.DC = D // 128
        self.NF = D // 4
        self.GC = D // 16
        self.GCN = self.GC // 128
        self.NFB = self.NF // 128
        self.H = (D - self.NF) // 256
        self.DA = self.H * 256
        self.NAB = self.DA // 128
        self.DIN = self.NF + 3 * self.DA
        self.DFF = 2 * D
        self.FB = self.DFF // 128
        self.NTB = self.T // TB
        self.TT = self.T // 128
        self.KCH = 2 * self.T // 128
        self.LB = 3 * self.T
        self.WL = 3 * self.T - 128
        self.NPART = self.FB // 16
        self.mats = {
            "w_in": (D, self.DIN), "w_gate": (D, 2 * D), "w_fo": (self.NF, D), "w_ao": (self.DA, D),
            "w_o": (D, D), "w_up": (D, 2 * self.DFF), "w_down": (self.DFF, D),
        }
        assert self.GC % 128 == 0 and self.T % TB == 0 and self.FB % 16 == 0

    def kc(self, K):
        n = K // 128
        for c in (8, 4, 2, 1):
            if n % c == 0:
                return c

    def rows(self, name):
        K, N = self.mats[name]
        tot = K * N // 2048
        assert tot % 8 == 0
        return tot // 8

    def wchunks(self, name):
        rr = self.rows(name)
        cra = chunk_rows(rr, 4096, 2, limit=self.cc_max)
        crb = chunk_rows(2 * rr, 4096, 4, limit=self.cc_max)
        return rr, cra, crb

    def np_cols(self):
        c = self
        o = {}
        p = 0
        for nm, n in (("g1", c.L * c.DC), ("g2", c.L * c.DC), ("gf", c.DC), ("bg", c.L * 2 * c.DC),
                      ("cw", c.L * 3 * c.FB), ("cb", c.L * c.FB), ("sg", c.L * 2), ("mk", 2)):
            o[nm] = p
            p += n
        o["_n"] = p
        return o


class Buf:
    __slots__ = ("name", "lw", "rd", "rdd", "lastdma", "dcount")

    def __init__(self, name):
        self.name = name
        self.lw = None
        self.rd = {}
        self.rdd = []
        self.lastdma = {}
        self.dcount = {}


class Op:
    __slots__ = ("idx", "eng", "fn", "deps", "kind", "semkey", "val", "sig", "epoch", "n")


class Sched:
    def __init__(self, nc, safe_same=True):
        self.nc = nc
        self.ops = []
        self.epoch = 0
        self.safe_same = safe_same
        self.sems = {}
        self.cccount = 0

    def _add(self, o, r, w):
        deps = {}

        def add(d):
            if d is None:
                return
            if d.kind == "c":
                k = d.eng
                if k not in deps or deps[k].idx < d.idx:
                    deps[k] = d
            else:
                deps[("x", d.idx)] = d
        for b in r:
            add(b.lw)
        for b in w:
            add(b.lw)
            for d in b.rd.values():
                add(d)
            for d in b.rdd:
                add(d)
        return deps

    def _fin(self, o, deps, r, w):
        o.deps = list(deps.values())
        o.idx = len(self.ops)
        o.sig = False
        o.epoch = self.epoch
        for b in r:
            if o.kind == "c":
                b.rd[o.eng] = o
            else:
                b.rdd.append(o)
        for b in w:
            b.lw = o
            b.rd = {}
            b.rdd = []
        self.ops.append(o)
        return o

    def op(self, eng, fn, r=(), w=()):
        o = Op()
        o.eng, o.fn, o.kind = eng, fn, "c"
        deps = self._add(o, r, w)
        o.idx = len(self.ops)
        return self._fin(o, deps, r, w)

    def dma(self, eng, fn, dbuf, r=(), w=(), n=1):
        o = Op()
        o.eng, o.fn, o.kind, o.n = eng, fn, "d", n
        deps = self._add(o, r, w)
        prev = dbuf.lastdma.get(eng)
        if prev is not None:
            deps[("x", prev.idx)] = prev
        dbuf.lastdma[eng] = o
        dbuf.dcount[eng] = dbuf.dcount.get(eng, 0) + 16 * n
        o.semkey = ("d", id(dbuf), eng)
        o.val = dbuf.dcount[eng]
        return self._fin(o, deps, r, w)

    def cc(self, fn, r=(), w=()):
        o = Op()
        o.eng, o.fn, o.kind = "pool", fn, "cc"
        deps = self._add(o, r, w)
        self.cccount += 1
        o.semkey = ("cc",)
        o.val = self.cccount
        return self._fin(o, deps, r, w)

    def sem(self, key):
        if key not in self.sems:
            self.sems[key] = self.nc.alloc_semaphore("sm%d" % len(self.sems))
        return self.sems[key]

    def emit(self):
        nc = self.nc
        engs = {"pe": nc.tensor, "act": nc.scalar, "dve": nc.vector, "pool": nc.gpsimd, "sp": nc.sync}
        for o in self.ops:
            for d in o.deps:
                if d.kind == "c":
                    if d.eng == o.eng and (d.eng == "pe" or not self.safe_same):
                        continue
                    d.sig = True
        cnt = {}
        for o in self.ops:
            if o.kind == "c" and o.sig:
                k = ("c", o.eng, o.epoch)
                cnt[k] = cnt.get(k, 0) + 1
                o.semkey = k
                o.val = cnt[k]
        waited = {}
        nwait = 0
        for o in self.ops:
            E = engs[o.eng]
            for d in sorted(o.deps, key=lambda d: -(d.val if (d.kind != "c" or d.sig) else 0)):
                if d.kind == "c" and not d.sig:
                    continue
                if d.kind == "c" and d.eng == o.eng and (d.eng == "pe" or not self.safe_same):
                    continue
                wk = (o.eng, d.semkey)
                if waited.get(wk, 0) >= d.val:
                    continue
                E.wait_ge(self.sem(d.semkey), d.val)
                waited[wk] = d.val
                nwait += 1
            if o.fn is None:
                continue
            res = o.fn(E)
            if o.kind == "c":
                if o.sig:
                    inst = res[-1] if isinstance(res, (list, tuple)) else res
                    inst.then_inc(self.sem(o.semkey), 1)
            elif o.kind == "d":
                lst = res if isinstance(res, (list, tuple)) else [res]
                assert len(lst) == o.n
                for inst in lst:
                    inst.then_inc(self.sem(o.semkey), 16)
            else:
                res.then_inc(self.sem(o.semkey))
        return nwait


def build(cfg, safe_same=True):
    c = cfg
    D, T, L, TB, DC, NTB, TT = c.D, c.T, c.L, c.TB, c.DC, c.NTB, c.TT
    NF, NFB, GC, GCN, H, DA, NAB, DIN, DFF, FB = c.NF, c.NFB, c.GC, c.GCN, c.H, c.DA, c.NAB, c.DIN, c.DFF, c.FB
    KCH, LB, WL = c.KCH, c.LB, c.WL
    pc = c.np_cols()
    NP = pc["_n"]
    nc = bass.Bass("TRN2", target_bir_lowering=False)
    S = Sched(nc, safe_same=safe_same)

    xT_in = nc.dram_tensor("xT", [D, T], F32, kind="ExternalInput")
    outT = nc.dram_tensor("outT", [D, T], F32, kind="ExternalOutput")
    win = {nm: nc.dram_tensor(nm, [L * c.rows(nm), 2048], F32, kind="ExternalInput") for nm in c.mats}
    params_d = nc.dram_tensor("params", [128, NP], F32, kind="ExternalInput")
    lamb_d = nc.dram_tensor("lamb", [128, L * 512], F32, kind="ExternalInput")
    csc_d = nc.dram_tensor("csc", [128, GCN * 2 * GC], BF16, kind="ExternalInput")
    ct_d = nc.dram_tensor("ct", [2 * 2 * T, T], BF16, kind="ExternalInput")
    tr_d = nc.dram_tensor("tr", [H, LB], F32, kind="ExternalInput")

    XS = nc.dram_tensor("XS", [D, T], F32)
    wsh = {(l, nm): nc.dram_tensor("wsh_%d_%s" % (l, nm), [c.rows(nm), 2048], BF16) for l in range(L) for nm in c.mats}
    wfull = {(l, nm): nc.dram_tensor("wf_%d_%s" % (l, nm), [8 * c.rows(nm), 2048], BF16) for l in range(L) for nm in c.mats}
    wmid = {(l, nm): nc.dram_tensor("wm_%d_%s" % (l, nm), [2 * c.rows(nm), 2048], BF16) for l in range(L) for nm in c.mats}
    QX = nc.dram_tensor("QX", [NAB * 128, T], BF16)
    KX = nc.dram_tensor("KX", [NAB * 128, T], BF16)
    KG = nc.dram_tensor("KG", [2 * NAB * 128, T], BF16)
    VX = nc.dram_tensor("VX", [T, DA], BF16)
    VG = nc.dram_tensor("VG", [2 * T, DA], BF16)
    AFX = nc.dram_tensor("AFX", [T, 2 * NF], BF16)
    AFG = nc.dram_tensor("AFG", [2 * T, 2 * NF], BF16)
    GS = nc.dram_tensor("GS", [2 * DC * 128, T], BF16)
    HX = nc.dram_tensor("HX", [128, 2 * DC], BF16)
    HG = nc.dram_tensor("HG", [256, 2 * DC], BF16)
    MB = nc.dram_tensor("MB", [H * 128, LB], F32)

    b_XS = [Buf("XS%d" % i) for i in range(DC)]
    b_wfull = {}
    crK = chunk_rows(NAB * 128, T * 2, 2, mult=128, limit=c.cc_max)
    crV = chunk_rows(T, DA * 2, 2, mult=128, limit=c.cc_max)
    crA = chunk_rows(T, 2 * NF * 2, 2, mult=128, limit=c.cc_max)
    b_wsh = {k: Buf("wsh%s" % (k,)) for k in wsh}
    b_QX, b_KX, b_VX, b_AFX = (Buf(n) for n in ("QX", "KX", "VX", "AFX"))
    b_KG = [Buf("KG%d" % i) for i in range(NAB * 128 // crK)]
    b_VG = [Buf("VG%d" % i) for i in range(T // crV)]
    b_AFG = [Buf("AFG%d" % i) for i in range(T // crA)]
    b_GS = [Buf("GS%d" % i) for i in range(2 * DC)]
    b_HX, b_HG, b_MB, b_out = Buf("HX"), [Buf("HG")], [Buf("MB%d" % h) for h in range(H)], [Buf("out%d" % i) for i in range(DC)]

    def sb(name, shape, dt):
        return nc.alloc_sbuf_tensor("s_" + name, shape, dt)
    R1 = sb("R1", [128, DC, T], BF16)
    b_R1 = [Buf("R1_%d" % i) for i in range(DC)]
    R2 = sb("R2", [128, 16, T], BF16)
    b_R2 = [Buf("R2_%d" % i) for i in range(16)]
    NSLOT = 5
    SLOTE = 4096
    slots = [sb("slot%d" % i, [128, SLOTE], BF16) for i in range(NSLOT)]
    b_slots = [Buf("slot%d" % i) for i in range(NSLOT)]
    slot_ctr = [0]

    def next_slot():
        i = slot_ctr[0] % NSLOT
        slot_ctr[0] += 1
        return slots[i], b_slots[i]
    params = sb("params", [128, NP], F32)
    b_params = Buf("params")
    ones_bf = sb("ones", [128, 128], BF16)
    b_ones = Buf("ones")
    csc = sb("csc", [128, GCN, 2 * GC], BF16)
    b_csc = Buf("csc")
    NXS = 3
    XW = sb("XW", [128, NXS * T], F32)
    xst = [XW[:, i * T:(i + 1) * T] for i in range(NXS)]
    b_xst = [Buf("xst%d" % i) for i in range(NXS)]
    xctr = [0]

    def next_xst():
        i = xctr[0] % NXS
        xctr[0] += 1
        return xst[i], b_xst[i]
    NST = 4
    st16 = [sb("st16_%d" % i, [128, T], BF16) for i in range(NST)]
    b_st16 = [Buf("st16_%d" % i) for i in range(NST)]
    sctr = [0]

    def next_st16():
        i = sctr[0] % NST
        sctr[0] += 1
        return st16[i], b_st16[i]
    gt = [sb("gt%d" % i, [128, T], F32) for i in range(6)]
    b_gt = [Buf("gt%d" % i) for i in range(6)]
    rstd = gt[5]
    b_rstd = b_gt[5]
    Wb = XW
    assert LB == NXS * T
    lamt = sb("lamt", [128, 512], F32)
    b_lamt = Buf("lamt")
    lsc = sb("lsc", [128, 8], F32)
    b_lsc = Buf("lsc")
    ltmp = sb("ltmp", [128, 128], F32)
    b_ltmp = Buf("ltmp")
    tmpS = [gt[4], gt[5]]
    b_tmpS = [b_gt[4], b_gt[5]]
    Pt = [sb("Pt%d" % i, [128, TB], BF16) for i in range(3)]
    b_Pt = [Buf("Pt%d" % i) for i in range(3)]
    rs = sb("rs", [128, TB], F32)
    b_rs = Buf("rs")
    on = [[gt[m * 2 + j] for j in range(2)] for m in range(2)]
    b_on = [[b_gt[m * 2 + j] for j in range(2)] for m in range(2)]
    sq16 = [sb("sq16_%d" % i, [128, TB], BF16) for i in range(2)]
    b_sq16 = [Buf("sq16_%d" % i) for i in range(2)]
    tk16 = [sb("tk16_%d" % i, [128, 512], BF16) for i in range(3)]
    b_tk16 = [Buf("tk16_%d" % i) for i in range(3)]
    tkc = [0]

    def next_tk():
        i = tkc[0] % 3
        tkc[0] += 1
        return tk16[i], b_tk16[i]
    assert NFB <= 16
    ufT = R2
    b_ufT = b_R2
    yf32 = gt[0:4]
    b_yf32 = b_gt[0:4]
    gsb = sb("gsb", [128, T + 2], F32)
    b_gsb = Buf("gsb")
    cva, b_cva = gt[4], b_gt[4]
    cvb, b_cvb = gt[5], b_gt[5]
    gg = st16
    b_gg = b_st16
    hho = sb("hho", [128, 2, DC], BF16)
    b_hho = Buf("hho")
    hh2 = sb("hh2", [128, 2, DC], BF16)
    b_hh2 = Buf("hh2")
    ghal = sb("ghal", [128, FB, 2], F32)
    b_ghal = Buf("ghal")

    banks = [nc.alloc_psum_tensor("bank%d" % i, [128, 512], F32) for i in range(8)]
    b_banks = [Buf("bank%d" % i) for i in range(8)]

    def pcol(nm, off, n=1):
        return params[:, pc[nm] + off: pc[nm] + off + n]

    evac_rr = [0]

    def evac_eng():
        evac_rr[0] += 1
        return "act" if evac_rr[0] % 2 == 0 else "dve"

    def copy_op(eng, out, in_, r, w, scale=None):
        if eng == "act":
            if scale is None:
                S.op("act", lambda E: E.activation(out, in_, ACTF.Copy), r=r, w=w)
            else:
                S.op("act", lambda E: E.activation(out, in_, ACTF.Copy, scale=scale), r=r, w=w)
        else:
            if scale is None:
                S.op("dve", lambda E: E.tensor_copy(out, in_), r=r, w=w)
            else:
                S.op("dve", lambda E: E.tensor_scalar(out, in_, scale, None, ALU.mult), r=r, w=w)

    def slab_src(l, nm, ng, kg):
        K, N = c.mats[nm]
        kc = c.kc(K)
        KG = K // 128 // kc
        srows = kc * 32
        r0 = (ng * KG + kg) * srows
        t = wfull[(l, nm)]
        return bass.AP(t, r0 * 2048, [(kc * 512, 128), (1, kc * 512)]), kc

    def load_slab(l, nm, ng, kg):
        src, kc = slab_src(l, nm, ng, kg)
        sl, bsl = next_slot()
        dst = sl[:, 0:kc * 512]
        S.dma("sp", lambda E: E.dma_start(out=dst, in_=src), bsl, r=b_wfull[(l, nm)], w=[bsl])
        return sl, bsl, kc

    def proj_group(l, nm, ng, kgs, rhs_fn, rbufs_fn):
        K, N = c.mats[nm]
        first = True
        for kgi, kg in enumerate(kgs):
            sl, bsl, kc = load_slab(l, nm, ng, kg)
            slv = sl[:, 0:kc * 512].rearrange("p (c n) -> p c n", c=kc)
            for b in range(4):
                for tb in range(NTB):
                    bk = banks[b * NTB + tb]
                    bb = b_banks[b * NTB + tb]

                    def fn(E, b=b, tb=tb, bk=bk, slv=slv, kc=kc, kg=kg, kgi=kgi):
                        last = None
                        for cc in range(kc):
                            cg = kg * kc + cc
                            last = E.matmul(bk[:, 0:TB], lhsT=slv[:, cc, b * 128:(b + 1) * 128], rhs=rhs_fn(cg, tb),
                                            start=(kgi == 0 and cc == 0), stop=(kgi == len(kgs) - 1 and cc == kc - 1))
                        return last
                    rb = [bsl]
                    for cc in range(kc):
                        rb += rbufs_fn(kg * kc + cc)
                    S.op("pe", fn, r=rb, w=[bb])

    S.dma("sp", lambda E: E.dma_start(out=params[:, :], in_=params_d[:, :]), b_params, w=[b_params])
    S.dma("sp", lambda E: E.dma_start(out=csc[:, :, :], in_=csc_d[:, :].rearrange("p (c n) -> p c n", c=GCN)), b_csc, w=[b_csc])
    S.op("dve", lambda E: E.memset(ones_bf[:, :], 1.0), w=[b_ones])
    S.dma("pool", lambda E: E.dma_start(out=XS[:, :], in_=xT_in[:, :]), b_XS[0], w=b_XS)
    for h in range(H):
        S.dma("sp", lambda E, h=h: E.dma_start(out=Wb[:, :], in_=tr_d[h:h + 1, :].partition_broadcast(128)), b_xst[0], w=b_xst)
        S.dma("sp", lambda E, h=h: E.dma_start(out=MB[h * 128:(h + 1) * 128, :], in_=Wb[:, :]), b_xst[0], r=b_xst, w=[b_MB[h]])

    def chunked_ag(src, dst, rows, cr, groups, nr, rbufs, wbufs):
        assert rows % cr == 0 and len(wbufs) == rows // cr
        for i in range(rows // cr):
            S.cc(lambda E, i=i: E.collective_compute("AllGather", ALU.bypass, replica_groups=groups,
                                                     ins=[src[i * cr:(i + 1) * cr, :]], outs=[dst[i * nr * cr:(i + 1) * nr * cr, :]]),
                 r=rbufs, w=[wbufs[i]])

    castsem = [Buf("castsem%d" % i) for i in range(2)]
    castctr = [0]

    def prologue(l, names):
        for nm in names:
            rr = c.rows(nm)
            step = 1024
            pieces = []
            for r0 in range(0, rr, step):
                r1 = min(rr, r0 + step)
                bpiece = Buf("wshp")
                pieces.append(bpiece)
                cs_ = castsem[castctr[0] % 2]
                castctr[0] += 1
                S.dma("pool", lambda E, nm=nm, r0=r0, r1=r1: E.dma_start(
                    out=wsh[(l, nm)][r0:r1, :], in_=win[nm][l * c.rows(nm) + r0: l * c.rows(nm) + r1, :]),
                    cs_, w=[bpiece])
            if AG_MODE == "8":
                b_wfull[(l, nm)] = [Buf("wf")]
                S.cc(lambda E, nm=nm: E.collective_compute("AllGather", ALU.bypass, replica_groups=[list(range(8))],
                                                           ins=[wsh[(l, nm)].ap().opt()], outs=[wfull[(l, nm)].ap().opt()]),
                     r=pieces, w=b_wfull[(l, nm)])
            else:
                _, cra, crb = c.wchunks(nm)
                bmid = [Buf("wmid") for _ in range(rr // cra)]
                chunked_ag(wsh[(l, nm)], wmid[(l, nm)], rr, cra, [[0, 4], [1, 5], [2, 6], [3, 7]], 2, pieces, bmid)
                b_wfull[(l, nm)] = [Buf("wf") for _ in range(2 * rr // crb)]
                chunked_ag(wmid[(l, nm)], wfull[(l, nm)], 2 * rr, crb, [[0, 1, 2, 3], [4, 5, 6, 7]], 4, bmid, b_wfull[(l, nm)])

    PAIRS = [[0, 1], [2, 3], [4, 5], [6, 7]]

    def pair_ag(src, dst, bsrc, bdst, rows, cr):
        chunked_ag(src, dst, rows, cr, PAIRS, 2, [bsrc], bdst)

    def rmsnorm(gain_col, dst_ap_fn, dst_buf_fn, to_dram=None):
        for ci in range(DC):
            xt, bxt = next_xst()
            S.dma("sp", lambda E, ci=ci, xt=xt: E.dma_start(out=xt[:, :], in_=XS[ci * 128:(ci + 1) * 128, :]), bxt,
                  r=[b_XS[ci]], w=[bxt])
            sq, bsq = next_st16()
            S.op("act", lambda E, xt=xt, sq=sq: E.activation(sq[:, :], xt[:, :], ACTF.Square), r=[bxt], w=[bsq])
            for tb in range(NTB):
                S.op("pe", lambda E, sq=sq, tb=tb, ci=ci: E.matmul(banks[tb][:, 0:TB], lhsT=ones_bf[:, :], rhs=sq[:, tb * TB:(tb + 1) * TB],
                                                                start=(ci == 0), stop=(ci == DC - 1)),
                     r=[bsq, b_ones], w=[b_banks[tb]])
        for tb in range(NTB):
            S.op("act", lambda E, tb=tb: E.activation(rstd[:, tb * TB:(tb + 1) * TB], banks[tb][:, 0:TB], ACTF.Sqrt,
                                                      bias=eps_t[:, 0:1], scale=1.0 / D), r=[b_banks[tb], b_eps], w=[b_rstd])
        S.op("dve", lambda E: E.reciprocal(rstd[:, :], rstd[:, :]), r=[b_rstd], w=[b_rstd])
        for ci in range(DC):
            xt, bxt = next_xst()
            S.dma("sp", lambda E, ci=ci, xt=xt: E.dma_start(out=xt[:, :], in_=XS[ci * 128:(ci + 1) * 128, :]), bxt,
                  r=[b_XS[ci]], w=[bxt])
            if to_dram is None:
                S.op("dve", lambda E, ci=ci, xt=xt: E.scalar_tensor_tensor(dst_ap_fn(ci), xt[:, :], gain_col(ci), rstd[:, :],
                                                                         ALU.mult, ALU.mult),
                     r=[bxt, b_rstd, b_params], w=[dst_buf_fn(ci)])
            else:
                S.op("dve", lambda E, ci=ci, xt=xt: E.scalar_tensor_tensor(xt[:, :], xt[:, :], gain_col(ci), rstd[:, :],
                                                                         ALU.mult, ALU.mult),
                     r=[bxt, b_rstd, b_params], w=[bxt])
                S.dma("pool", lambda E, ci=ci, xt=xt: E.dma_start(out=to_dram[ci * 128:(ci + 1) * 128, :], in_=xt[:, :]), bxt,
                      r=[bxt], w=[b_out[ci]])

    eps_t = sb("eps_t", [128, 1], F32)
    b_eps = Buf("eps")
    S.op("dve", lambda E: E.memset(eps_t[:, :], EPS), w=[b_eps])

    def x_rmw(j, bank_ids):
        xt, bxt = next_xst()
        S.dma("sp", lambda E, xt=xt: E.dma_start(out=xt[:, :], in_=XS[j * 128:(j + 1) * 128, :]), bxt, r=[b_XS[j]], w=[bxt])
        for tb in range(NTB):
            bi = bank_ids[tb]
            S.op("dve", lambda E, xt=xt, tb=tb, bi=bi: E.tensor_tensor(xt[:, tb * TB:(tb + 1) * TB], xt[:, tb * TB:(tb + 1) * TB],
                                                                     banks[bi][:, 0:TB], ALU.add), r=[bxt, b_banks[bi]], w=[bxt])
        S.dma("pool", lambda E, xt=xt: E.dma_start(out=XS[j * 128:(j + 1) * 128, :], in_=xt[:, :]), bxt, r=[bxt], w=[b_XS[j]])

    prologue(0, ["w_in", "w_gate", "w_fo", "w_ao", "w_o", "w_up", "w_down"])

    for l in range(L):
        S.epoch = l
        lam_init = 0.8 - 0.6 * math.exp(-0.3 * l)
        S.dma("sp", lambda E, l=l: E.dma_start(out=lamt[:, :], in_=lamb_d[:, l * 512:(l + 1) * 512]), b_lamt, w=[b_lamt])
        for i in range(2):
            S.op("dve", lambda E, i=i: E.tensor_tensor(ltmp[:, :], lamt[:, (2 * i) * 128:(2 * i + 1) * 128],
                                                     lamt[:, (2 * i + 1) * 128:(2 * i + 2) * 128], ALU.mult), r=[b_lamt], w=[b_ltmp])
            S.op("dve", lambda E, i=i: E.tensor_reduce(lsc[:, i:i + 1], ltmp[:, :], AX.X, ALU.add), r=[b_ltmp], w=[b_lsc])
        S.op("act", lambda E: E.activation(lsc[:, 2:4], lsc[:, 0:2], ACTF.Exp), r=[b_lsc], w=[b_lsc])
        S.op("dve", lambda E, li=lam_init: E.scalar_tensor_tensor(lsc[:, 4:5], lsc[:, 3:4], -li, lsc[:, 2:3], ALU.add, ALU.subtract),
             r=[b_lsc], w=[b_lsc])
        S.op("dve", lambda E, l=l, li=lam_init: E.tensor_scalar(lsc[:, 5:7], pcol("sg", l * 2, 2), 1.0 - li, None, ALU.mult),
             r=[b_params], w=[b_lsc])

        rmsnorm(lambda ci, l=l: pcol("g1", l * DC + ci), lambda ci: R1[:, ci, :], lambda ci: b_R1[ci])

        hr = lambda cg, tb: R1[:, cg, tb * TB:(tb + 1) * TB]
        hb = lambda cg: [b_R1[cg]]
        KGin = DC // c.kc(D)
        allkg = list(range(KGin))
        n_uf, n_q = NF // 512, DA // 512
        for ng in range(DIN // 512):
            if ng < n_uf + 2 * n_q:
                proj_group(l, "w_in", ng, allkg, hr, hb)
                for b in range(4):
                    if ng < n_uf:
                        fbk = ng * 4 + b
                        for tb in range(NTB):
                            bi = b * NTB + tb
                            copy_op(evac_eng(), ufT[:, fbk, tb * TB:(tb + 1) * TB], banks[bi][:, 0:TB], [b_banks[bi]], [b_ufT[fbk]])
                    else:
                        isq = ng < n_uf + n_q
                        blk = (ng - n_uf - (0 if isq else n_q)) * 4 + b
                        st, bst = next_st16()
                        for tb in range(NTB):
                            bi = b * NTB + tb
                            copy_op(evac_eng(), st[:, tb * TB:(tb + 1) * TB], banks[bi][:, 0:TB], [b_banks[bi]], [bst],
                                    scale=(128 ** -0.5) if isq else None)
                        dstT = QX if isq else KX
                        S.dma("pool", lambda E, st=st, blk=blk, dstT=dstT: E.dma_start(out=dstT[blk * 128:(blk + 1) * 128, :], in_=st[:, :]),
                              bst, r=[bst], w=[b_QX if isq else b_KX])
                if ng == n_uf - 1:
                    for tt in range(TT):
                        for g in range(4):
                            bi = (tt * 4 + g) % 8

                            def fn(E, tt=tt, g=g, bi=bi):
                                last = None
                                for cc in range(GCN):
                                    last = E.matmul(banks[bi][:, 0:2 * GC], lhsT=ufT[:, g * GCN + cc, tt * 128:(tt + 1) * 128],
                                                    rhs=csc[:, cc, :], start=(cc == 0), stop=(cc == GCN - 1))
                                return last
                            S.op("pe", fn, r=[b_ufT[g * GCN + cc] for cc in range(GCN)] + [b_csc], w=[b_banks[bi]])
                            tk, btk = next_tk()
                            copy_op(evac_eng(), tk[:, 0:2 * GC], banks[bi][:, 0:2 * GC], [b_banks[bi]], [btk])
                            S.dma("pool", lambda E, tk=tk, tt=tt, g=g: E.dma_start(
                                out=AFX[tt * 128:(tt + 1) * 128, g * 2 * GC:(g + 1) * 2 * GC], in_=tk[:, 0:2 * GC]), btk,
                                r=[btk], w=[b_AFX])
                    pair_ag(AFX, AFG, b_AFX, b_AFG, T, crA)
                if ng == n_uf + 2 * n_q - 1:
                    pair_ag(KX, KG, b_KX, b_KG, NAB * 128, crK)
            else:
                gv = ng - n_uf - 2 * n_q
                for kgi, kg in enumerate(allkg):
                    sl, bsl, kc = load_slab(l, "w_in", ng, kg)
                    slv = sl[:, 0:kc * 512].rearrange("p (c n) -> p c n", c=kc)
                    for tt in range(TT):
                        def fn(E, tt=tt, slv=slv, kc=kc, kg=kg, kgi=kgi):
                            last = None
                            for cc in range(kc):
                                cg = kg * kc + cc
                                last = E.matmul(banks[tt][:, :], lhsT=R1[:, cg, tt * 128:(tt + 1) * 128], rhs=slv[:, cc, :],
                                                start=(kgi == 0 and cc == 0), stop=(kgi == len(allkg) - 1 and cc == kc - 1))
                            return last
                        S.op("pe", fn, r=[bsl] + [b_R1[kg * kc + cc] for cc in range(kc)], w=[b_banks[tt]])
                for tt in range(TT):
                    tk, btk = next_tk()
                    copy_op(evac_eng(), tk[:, :], banks[tt][:, :], [b_banks[tt]], [btk])
                    S.dma("pool", lambda E, tk=tk, tt=tt, gv=gv: E.dma_start(out=VX[tt * 128:(tt + 1) * 128, gv * 512:(gv + 1) * 512], in_=tk[:, :]),
                          btk, r=[btk], w=[b_VX])
        pair_ag(VX, VG, b_VX, b_VG, T, crV)

        for ng in range(2 * D // 512):
            proj_group(l, "w_gate", ng, allkg, hr, hb)
            for b in range(4):
                blk = ng * 4 + b
                st, bst = next_st16()
                for tb in range(NTB):
                    bi = b * NTB + tb
                    S.op("act", lambda E, st=st, tb=tb, bi=bi, blk=blk, l=l: E.activation(
                        st[:, tb * TB:(tb + 1) * TB], banks[bi][:, 0:TB], ACTF.Sigmoid, bias=pcol("bg", l * 2 * DC + blk)),
                        r=[b_banks[bi], b_params], w=[bst])
                S.dma("pool", lambda E, st=st, blk=blk: E.dma_start(out=GS[blk * 128:(blk + 1) * 128, :], in_=st[:, :]), bst,
                      r=[bst], w=[b_GS[blk]])

        if l + 1 < L:
            prologue(l + 1, ["w_in", "w_gate", "w_fo", "w_ao", "w_o"])

        for sb_ in range(NTB):
            for sc in range(KCH):
                sl, bsl = next_slot()
                afv = sl[:, 0:2 * NF]
                arow = ag_row((sc % TT) * 128, sc // TT, crA, 2)
                S.dma("sp", lambda E, afv=afv, arow=arow: E.dma_start(out=afv, in_=AFG[arow:arow + 128, :]), bsl, r=b_AFG, w=[bsl])
                tl, btl = next_slot()
                tv = tl[:, 0:2 * TB].rearrange("p (a n) -> p a n", a=2)
                src = bass.AP(ct_d, sc * 128 * T + sb_ * TB, [(T, 128), (2 * T * T, 2), (1, TB)])
                S.dma("sp", lambda E, tv=tv, src=src: E.dma_start(out=tv, in_=src), btl, w=[btl])
                for fb in range(NFB):
                    g, j = fb // GCN, fb % GCN
                    c0 = g * 2 * GC + j * 128

                    def fn(E, fb=fb, c0=c0, afv=afv, tv=tv, sc=sc):
                        E.matmul(banks[fb][:, 0:TB], lhsT=afv[:, c0:c0 + 128], rhs=tv[:, 0, :], start=(sc == 0), stop=False)
                        return E.matmul(banks[fb][:, 0:TB], lhsT=afv[:, c0 + GC:c0 + GC + 128], rhs=tv[:, 1, :], start=False,
                                        stop=(sc == KCH - 1))
                    S.op("pe", fn, r=[bsl, btl], w=[b_banks[fb]])
            for fb in range(NFB):
                copy_op(evac_eng(), R1[:, NAB + fb, sb_ * TB:(sb_ + 1) * TB], banks[fb][:, 0:TB], [b_banks[fb]], [b_R1[NAB + fb]])

        for h in range(H):
            S.dma("sp", lambda E, h=h: E.dma_start(out=Wb[:, 0:WL], in_=bass.AP(MB, h * 128 * LB + 127, [(LB - 1, 128), (1, WL)])),
                  b_xst[0], r=[b_MB[h]], w=b_xst)
            ksl, bks = next_slot()
            kv = ksl[:, 0:2 * 2 * T].rearrange("p (m r t) -> p m r t", m=2, r=2)
            for m in range(2):
                krow = ag_row((2 * h + m) * 128, 0, crK, 2)
                src = bass.AP(KG, krow * T, [(T, 128), (crK * T, 2), (1, T)])
                S.dma("sp", lambda E, m=m, src=src, kv=kv: E.dma_start(out=kv[:, m, :, :], in_=src), bks, r=b_KG, w=[bks])
            kfl = ksl[:, 0:2 * 2 * T].rearrange("p (m k) -> p m k", m=2)
            vsl, bvs = next_slot()
            vv = vsl[:, 0:KCH * 256].rearrange("p (k e) -> p k e", k=KCH)
            nvc = crV // 128

            def vload(E, h=h, vv=vv):
                res = []
                for r_ in range(2):
                    for i_ in range(T // crV):
                        vrow = ag_row(i_ * crV, r_, crV, 2)
                        k0 = r_ * TT + i_ * nvc
                        res.append(E.dma_start(out=vv[:, k0:k0 + nvc, :],
                                               in_=bass.AP(VG, vrow * DA + h * 256, [(DA, 128), (128 * DA, nvc), (1, 256)])))
                return res
            S.dma("sp", vload, bvs, r=b_VG, w=[bvs], n=2 * (T // crV))
            qsl, bqs = next_slot()
            qv = qsl[:, 0:2 * T].rearrange("p (m t) -> p m t", m=2)
            S.dma("sp", lambda E, h=h, qv=qv: E.dma_start(out=qv, in_=bass.AP(QX, 2 * h * 128 * T, [(T, 128), (128 * T, 2), (1, T)])),
                  bqs, r=[b_QX], w=[bqs])
            for qb in range(NTB):
                for m in range(2):
                    acc = [m * 3 + 0, m * 3 + 1, m * 3 + 2]
                    pend = None
                    for kc_ in range(KCH + 1):
                        if kc_ < KCH:
                            sbk = 6 + (kc_ % 2)
                            S.op("pe", lambda E, m=m, kc_=kc_, sbk=sbk, qb=qb, kfl=kfl, qv=qv: E.matmul(
                                banks[sbk][:, 0:TB], lhsT=kfl[:, m, kc_ * 128:(kc_ + 1) * 128], rhs=qv[:, m, qb * TB:(qb + 1) * TB],
                                start=True, stop=True), r=[bks, bqs], w=[b_banks[sbk]])
                            ti = kc_ % 2
                            jb = 2 * T - kc_ * 128 + qb * TB - 128
                            S.op("dve", lambda E, ti=ti, sbk=sbk, jb=jb: E.tensor_tensor(tmpS[ti][:, 0:TB], banks[sbk][:, 0:TB],
                                                                                        Wb[:, jb:jb + TB], ALU.add),
                                 r=[b_banks[sbk]] + b_xst, w=[b_tmpS[ti]])
                            pi = kc_ % 3
                            S.op("act", lambda E, ti=ti, pi=pi: E.activation(Pt[pi][:, :], tmpS[ti][:, 0:TB], ACTF.Exp),
                                 r=[b_tmpS[ti]], w=[b_Pt[pi]])
                        if pend is not None:
                            pk, ppi = pend

                            def fn(E, pk=pk, ppi=ppi, acc=acc, vv=vv):
                                E.matmul(banks[acc[0]][:, 0:TB], lhsT=vv[:, pk, 0:128], rhs=Pt[ppi][:, :], start=(pk == 0), stop=(pk == KCH - 1))
                                E.matmul(banks[acc[1]][:, 0:TB], lhsT=vv[:, pk, 128:256], rhs=Pt[ppi][:, :], start=(pk == 0), stop=(pk == KCH - 1))
                                return E.matmul(banks[acc[2]][:, 0:TB], lhsT=ones_bf[:, :], rhs=Pt[ppi][:, :], start=(pk == 0), stop=(pk == KCH - 1))
                            S.op("pe", fn, r=[bvs, b_Pt[ppi], b_ones], w=[b_banks[a] for a in acc])
                        pend = (kc_, kc_ % 3) if kc_ < KCH else None
                    S.op("dve", lambda E, acc=acc: E.reciprocal(rs[:, :], banks[acc[2]][:, 0:TB]), r=[b_banks[acc[2]]], w=[b_rs])
                    for j in range(2):
                        S.op("dve", lambda E, m=m, j=j, acc=acc: E.tensor_tensor(on[m][j][:, 0:TB], banks[acc[j]][:, 0:TB], rs[:, :], ALU.mult),
                             r=[b_banks[acc[j]], b_rs], w=[b_on[m][j]])
                for j in range(2):
                    S.op("dve", lambda E, j=j: E.scalar_tensor_tensor(on[0][j][:, 0:TB], on[1][j][:, 0:TB], lsc[:, 4:5], on[0][j][:, 0:TB],
                                                                    ALU.mult, ALU.add), r=[b_on[1][j], b_on[0][j], b_lsc], w=[b_on[0][j]])
                    S.op("act", lambda E, j=j: E.activation(sq16[j][:, :], on[0][j][:, 0:TB], ACTF.Square), r=[b_on[0][j]], w=[b_sq16[j]])
                    S.op("pe", lambda E, j=j: E.matmul(banks[2][:, 0:TB], lhsT=ones_bf[:, :], rhs=sq16[j][:, :], start=(j == 0), stop=(j == 1)),
                         r=[b_sq16[j], b_ones], w=[b_banks[2]])
                S.op("act", lambda E: E.activation(rs[:, :], banks[2][:, 0:TB], ACTF.Sqrt, bias=eps_t[:, 0:1], scale=1.0 / 256),
                     r=[b_banks[2], b_eps], w=[b_rs])
                S.op("dve", lambda E: E.reciprocal(rs[:, :], rs[:, :]), r=[b_rs], w=[b_rs])
                for j in range(2):
                    S.op("dve", lambda E, j=j, h=h, qb=qb: E.scalar_tensor_tensor(R1[:, 2 * h + j, qb * TB:(qb + 1) * TB], on[0][j][:, 0:TB],
                                                                               lsc[:, 5 + j:6 + j], rs[:, :], ALU.mult, ALU.mult),
                         r=[b_on[0][j], b_rs, b_lsc], w=[b_R1[2 * h + j]])

        KCo = c.kc(D)
        KGo = DC // KCo
        hk = DC // 2
        for kh in range(2):
            for jg in range(hk // 4):
                ng = kh * (hk // 4) + jg
                fr = lambda cg, tb: R1[:, NAB + cg, tb * TB:(tb + 1) * TB]
                fbuf = lambda cg: [b_R1[NAB + cg]]
                proj_group(l, "w_fo", ng, list(range(NFB // c.kc(NF))), fr, fbuf)
                for b in range(4):
                    j = ng * 4 + b
                    st, bst = next_st16()
                    S.dma("sp", lambda E, st=st, j=j: E.dma_start(out=st[:, :], in_=GS[j * 128:(j + 1) * 128, :]), bst, r=[b_GS[j]], w=[bst])
                    for tb in range(NTB):
                        bi = b * NTB + tb
                        S.op("dve", lambda E, b=b, tb=tb, bi=bi, st=st: E.tensor_tensor(yf32[b][:, tb * TB:(tb + 1) * TB], banks[bi][:, 0:TB],
                                                                                      st[:, tb * TB:(tb + 1) * TB], ALU.mult),
                             r=[b_banks[bi], bst], w=[b_yf32[b]])
                ar = lambda cg, tb: R1[:, cg, tb * TB:(tb + 1) * TB]
                abuf = lambda cg: [b_R1[cg]]
                proj_group(l, "w_ao", ng, list(range(NAB // c.kc(DA))), ar, abuf)
                for b in range(4):
                    j = ng * 4 + b
                    st, bst = next_st16()
                    S.dma("sp", lambda E, st=st, j=j: E.dma_start(out=st[:, :], in_=GS[(DC + j) * 128:(DC + j + 1) * 128, :]), bst,
                          r=[b_GS[DC + j]], w=[bst])
                    for tb in range(NTB):
                        bi = b * NTB + tb
                        S.op("dve", lambda E, tb=tb, bi=bi, st=st: E.tensor_tensor(cva[:, tb * TB:(tb + 1) * TB], banks[bi][:, 0:TB],
                                                                                 st[:, tb * TB:(tb + 1) * TB], ALU.mult),
                             r=[b_banks[bi], bst], w=[b_cva])
                        S.op("dve", lambda E, b=b, tb=tb, jg=jg: E.tensor_tensor(R2[:, jg * 4 + b, tb * TB:(tb + 1) * TB], cva[:, tb * TB:(tb + 1) * TB],
                                                                               yf32[b][:, tb * TB:(tb + 1) * TB], ALU.add),
                             r=[b_cva, b_yf32[b]], w=[b_R2[jg * 4 + b]])
            mr = lambda cg, tb, kh=kh: R2[:, cg - kh * hk, tb * TB:(tb + 1) * TB]
            mbuf = lambda cg, kh=kh: [b_R2[cg - kh * hk]]
            kgs = [kh * (hk // KCo) + i for i in range(hk // KCo)]
            for og in range(D // 512):
                proj_group(l, "w_o", og, kgs, mr, mbuf)
                for b in range(4):
                    x_rmw(og * 4 + b, [b * NTB + tb for tb in range(NTB)])

        rmsnorm(lambda ci, l=l: pcol("g2", l * DC + ci), lambda ci: R1[:, ci, :], lambda ci: b_R1[ci])
        S.op("dve", lambda E: E.tensor_copy(hho[:, 0, :], R1[:, :, 0]), r=b_R1, w=[b_hho])
        S.op("dve", lambda E: E.tensor_copy(hho[:, 1, :], R1[:, :, T - 1]), r=b_R1, w=[b_hho])
        S.dma("pool", lambda E: E.dma_start(out=HX[:, :], in_=hho[:, :, :].rearrange("p a c -> p (a c)")), b_hho, r=[b_hho], w=[b_HX])
        pair_ag(HX, HG, b_HX, b_HG, 128, 128)
        S.dma("sp", lambda E: [E.dma_start(out=hh2[:, 0, :], in_=HG[0:128, DC:2 * DC]),
                               E.dma_start(out=hh2[:, 1, :], in_=HG[128:256, 0:DC])], b_hh2, r=b_HG, w=[b_hh2], n=2)
        if l + 1 < L:
            prologue(l + 1, ["w_up", "w_down"])

        first_f0 = [True]
        for ng in range(DFF // 512):
            for kgi, kg in enumerate(allkg):
                sl, bsl, kc = load_slab(l, "w_up", ng, kg)
                slv = sl[:, 0:kc * 512].rearrange("p (c n) -> p c n", c=kc)

                def fn(E, ng=ng, kg=kg, kc=kc, slv=slv, first_f0=first_f0):
                    last = None
                    for b in range(4):
                        blk = ng * 4 + b
                        for cc in range(kc):
                            cg = kg * kc + cc
                            st_ = first_f0[0]
                            first_f0[0] = False
                            last = E.matmul(banks[7][:, blk * 2:blk * 2 + 2], lhsT=slv[:, cc, b * 128:(b + 1) * 128], rhs=hh2[:, :, cg],
                                            start=st_, stop=(cg == DC - 1), skip_group_check=True)
                    return last
                S.op("pe", fn, r=[bsl, b_hh2], w=[b_banks[7]])
        bk7 = banks[7][:, 0:FB * 2].rearrange("p (f a) -> p f a", a=2)
        for a in range(2):
            S.op("dve", lambda E, a=a: E.tensor_scalar(ghal[:, :, a], bk7[:, :, a], pcol("mk", a), None, ALU.mult),
                 r=[b_banks[7], b_params], w=[b_ghal])

        KCd = c.kc(DFF)
        for part in range(c.NPART):
            for fg in range(4):
                ngg = part * 4 + fg
                proj_group(l, "w_up", ngg, allkg, hr, hb)
                for b in range(4):
                    fblk = ngg * 4 + b
                    for tb in range(NTB):
                        bi = b * NTB + tb
                        S.op("act", lambda E, tb=tb, bi=bi: E.activation(gsb[:, 1 + tb * TB:1 + (tb + 1) * TB], banks[bi][:, 0:TB], ACTF.Copy),
                             r=[b_banks[bi]], w=[b_gsb])
                    S.op("dve", lambda E, fblk=fblk: E.tensor_copy(gsb[:, 0:1], ghal[:, fblk, 0:1]), r=[b_ghal], w=[b_gsb])
                    S.op("dve", lambda E, fblk=fblk: E.tensor_copy(gsb[:, T + 1:T + 2], ghal[:, fblk, 1:2]), r=[b_ghal], w=[b_gsb])
                    cw = lambda j, fblk=fblk, l=l: pcol("cw", (l * 3 + j) * FB + fblk)
                    S.op("dve", lambda E, fblk=fblk, l=l, cw=cw: E.tensor_scalar(cva[:, :], gsb[:, 0:T], cw(0), pcol("cb", l * FB + fblk),
                                                                               ALU.mult, ALU.add), r=[b_gsb, b_params], w=[b_cva])
                    S.op("dve", lambda E, cw=cw: E.scalar_tensor_tensor(cvb[:, :], gsb[:, 1:T + 1], cw(1), cva[:, :], ALU.mult, ALU.add),
                         r=[b_gsb, b_cva, b_params], w=[b_cvb])
                    S.op("dve", lambda E, cw=cw: E.scalar_tensor_tensor(cva[:, :], gsb[:, 2:T + 2], cw(2), cvb[:, :], ALU.mult, ALU.add),
                         r=[b_gsb, b_cvb, b_params], w=[b_cva])
                    S.op("act", lambda E, b=b: E.activation(gg[b][:, :], cva[:, :], ACTF.Gelu), r=[b_cva], w=[b_gg[b]])
                ngv = DFF // 512 + ngg
                proj_group(l, "w_up", ngv, allkg, hr, hb)
                for b in range(4):
                    for tb in range(NTB):
                        bi = b * NTB + tb
                        S.op("dve", lambda E, b=b, tb=tb, bi=bi, fg=fg: E.tensor_tensor(R2[:, fg * 4 + b, tb * TB:(tb + 1) * TB], banks[bi][:, 0:TB],
                                                                                      gg[b][:, tb * TB:(tb + 1) * TB], ALU.mult),
                             r=[b_banks[bi], b_gg[b]], w=[b_R2[fg * 4 + b]])
            ar2 = lambda cg, tb, part=part: R2[:, cg - part * 16, tb * TB:(tb + 1) * TB]
            ab2 = lambda cg, part=part: [b_R2[cg - part * 16]]
            kgs = [part * (16 // KCd) + i for i in range(16 // KCd)]
            for og in range(D // 512):
                proj_group(l, "w_down", og, kgs, ar2, ab2)
                for b in range(4):
                    x_rmw(og * 4 + b, [b * NTB + tb for tb in range(NTB)])

    S.epoch = L
    rmsnorm(lambda ci: pcol("gf", ci), None, None, to_dram=outT)
    S.op("sp", None, r=b_out + b_XS, w=[])
    S.op("pool", None, r=b_out + b_XS, w=[])
    nw = S.emit()
    return nc, (len(S.ops), nw)


def rel_bucket_np(rel):
    n = -rel
    half = NREL // 2
    ret = (n < 0).astype(np.int32) * half
    n = np.abs(n)
    max_exact = half // 2
    is_small = n < max_exact
    nf = np.maximum(n, 1).astype(np.float32)
    large = max_exact + (np.log(nf / np.float32(max_exact)) / np.float32(math.log(RELMAX / max_exact))
                         * np.float32(half - max_exact)).astype(np.int32)
    large = np.minimum(large, half - 1)
    return ret + np.where(is_small, n, large)


def stream_layout(cfg, W, K, N):
    kc = cfg.kc(K)
    KG = K // 128 // kc
    NG = N // 512
    a = W.reshape(KG, kc, 128, NG, 512).transpose(3, 0, 2, 1, 4)
    return np.ascontiguousarray(a).reshape(8, -1, 2048)


def prep_inputs(cfg, x, norm1_gain, w_in, w_fourier_out, lambdas, subln_gain, rel_bias_table, w_attn_out,
                w_gate, b_gate, w_o, norm2_gain, w_up, conv_w, conv_b, w_down, final_norm_gain):
    c = cfg
    L, D, T, DC, FB = c.L, c.D, c.T, c.DC, c.FB
    f32 = np.float32
    x = np.asarray(x, f32)
    src = {"w_in": w_in, "w_gate": w_gate, "w_fo": w_fourier_out, "w_ao": w_attn_out, "w_o": w_o, "w_up": w_up, "w_down": w_down}
    wsh = {}
    for nm, (K, N) in c.mats.items():
        rr, cra, crb = c.wchunks(nm)
        perm = weight_perm(rr, cra, crb) if AG_MODE != "8" else np.arange(8 * rr)
        per_layer = []
        for l in range(L):
            st = stream_layout(c, np.asarray(src[nm][l], f32), K, N).reshape(8 * rr, 2048)
            sh = np.empty_like(st)
            sh[perm] = st
            per_layer.append(sh.reshape(8, rr, 2048))
        wsh[nm] = np.stack(per_layer, axis=1)
    pc = c.np_cols()

    def fm(a, nblk):
        a = np.asarray(a, f32)
        lead = a.shape[:-1]
        a = a.reshape(*lead, nblk, 128)
        return np.moveaxis(a, -1, 0).reshape(128, -1)
    base = np.zeros((128, pc["_n"]), f32)
    base[:, pc["g1"]:pc["g1"] + L * DC] = fm(norm1_gain, DC)
    base[:, pc["g2"]:pc["g2"] + L * DC] = fm(norm2_gain, DC)
    base[:, pc["gf"]:pc["gf"] + DC] = fm(final_norm_gain, DC)
    base[:, pc["bg"]:pc["bg"] + L * 2 * DC] = fm(b_gate, 2 * DC)
    base[:, pc["cw"]:pc["cw"] + L * 3 * FB] = fm(conv_w, FB)
    base[:, pc["cb"]:pc["cb"] + L * FB] = fm(conv_b, FB)
    base[:, pc["sg"]:pc["sg"] + L * 2] = fm(subln_gain, 2)
    lamb = np.ascontiguousarray(np.broadcast_to(np.asarray(lambdas, f32).reshape(1, L * 512), (128, L * 512)))
    GC, GCN, S_ = c.GC, c.GCN, c.S
    cidx = np.arange(GC)
    ang = 2.0 * np.pi * ((cidx[:, None] * cidx[None, :]) % GC) / GC
    cs = np.concatenate([np.cos(ang), np.sin(ang)], axis=1)
    csc = cs.reshape(GCN, 128, 2 * GC).transpose(1, 0, 2).reshape(128, GCN * 2 * GC).astype(ml_dtypes.bfloat16)
    tbl = np.asarray(rel_bias_table, f32)
    in_maps = []
    for core in range(8):
        b, hf = core // 2, core % 2
        m = {}
        m["xT"] = np.ascontiguousarray(x[b, hf * T:(hf + 1) * T, :].T)
        for nm in c.mats:
            m[nm] = wsh[nm][core].reshape(-1, 2048)
        p = base.copy()
        p[:, pc["mk"]] = 1.0 if hf == 1 else 0.0
        p[:, pc["mk"] + 1] = 1.0 if hf == 0 else 0.0
        m["params"] = p
        m["lamb"] = lamb
        m["csc"] = csc
        s = np.arange(2 * T)
        sp = hf * T + np.arange(T)
        a2 = 2.0 * np.pi * ((s[:, None] * sp[None, :]) % S_) / S_
        nrm = 1.0 / math.sqrt(S_ * GC)
        m["ct"] = np.concatenate([np.cos(a2) * nrm, -np.sin(a2) * nrm], axis=0).astype(ml_dtypes.bfloat16)
        N_ = 3 * T - 1
        i = np.arange(N_)
        rel = i - (T - 1) - hf * T
        TD = tbl[rel_bucket_np(rel)]
        TRv = np.zeros((c.H, c.LB), f32)
        TRv[:, :N_] = TD[::-1, :].T
        m["tr"] = TRv
        in_maps.append(m)
    return in_maps


_CACHE = {}


def run(cfg, inputs, safe_same=True):
    key = (cfg.D, cfg.S, cfg.L, cfg.TB, safe_same)
    if key not in _CACHE:
        _CACHE[key] = build(cfg, safe_same=safe_same)[0]
    nc = _CACHE[key]
    in_maps = prep_inputs(cfg, **inputs)
    res = run_bass_kernel_spmd(nc, in_maps, core_ids=list(range(8)))
    out = np.empty((cfg.B, cfg.S, cfg.D), np.float32)
    for core in range(8):
        b, hf = core // 2, core % 2
        out[b, hf * cfg.T:(hf + 1) * cfg.T, :] = res.results[core]["outT"].T
    return out


def kernel(**inputs):
    return run(Cfg(), inputs)
```

```python
import math
import numpy as np
import ml_dtypes
import concourse.bass as bass
import concourse.mybir as mybir
from concourse.bass_utils import run_bass_kernel_spmd

F32 = mybir.dt.float32
BF16 = mybir.dt.bfloat16
ACTF = mybir.ActivationFunctionType
ALU = mybir.AluOpType
AX = mybir.AxisListType
EPS = 1e-6
AG_MODE = "2stage"
CC_MAX_OUT = 4 * 1024 * 1024


def chunk_rows(rows, rowbytes, nr, mult=1, limit=CC_MAX_OUT):
    best = None
    for cr in range(mult, rows + 1, mult):
        if rows % cr == 0 and cr * rowbytes * nr <= limit:
            best = cr
    assert best is not None, (rows, rowbytes, nr, mult, limit)
    return best


def ag_row(local, r, cr, nr):
    return (local // cr) * nr * cr + r * cr + (local % cr)


def weight_perm(rr, cra, crb):
    shard = [np.arange(rr) + c * rr for c in range(8)]
    mid = []
    for c in range(8):
        a, b = shard[c % 4], shard[c % 4 + 4]
        mid.append(np.concatenate([np.concatenate([a[i * cra:(i + 1) * cra], b[i * cra:(i + 1) * cra]]) for i in range(rr // cra)]))
    full = []
    for c in range(8):
        g = (c // 4) * 4
        full.append(np.concatenate([np.concatenate([mid[g + r][j * crb:(j + 1) * crb] for r in range(4)]) for j in range(2 * rr // crb)]))
    for c in range(1, 8):
        assert np.array_equal(full[c], full[0])
    return full[0]
NREL = 32
RELMAX = 128


class Cfg:
    def __init__(self, D=4096, S=2048, L=4, TB=512, cc_max=CC_MAX_OUT):
        self.D, self.S, self.L, self.TB = D, S, L, TB
        self.cc_max = cc_max
        self.B = 4
        self.T = S // 2
        self.DC = D // 128
        self.NF = D // 4
        self.GC = D // 16
        self.GCN = self.GC // 128
        self.NFB = self.NF // 128
        self.H = (D - self.NF) // 256
        self.DA = self.H * 256
        self.NAB = self.DA // 128
        self.DIN = self.NF + 3 * self.DA
        self.DFF = 2 * D
        self.FB = self.DFF // 128
        self.NTB = self.T // TB
        self.TT = self.T // 128
        self.KCH = 2 * self.T // 128
        self.LB = 3 * self.T
        self.WL = 3 * self.T - 128
        self.NPART = self.FB // 16
        self.mats = {
            "w_in": (D, self.DIN), "w_gate": (D, 2 * D), "w_fo": (self.NF, D), "w_ao": (self.DA, D),
            "w_o": (D, D), "w_up": (D, 2 * self.DFF), "w_down": (self.DFF, D),
        }
        assert self.GC % 128 == 0 and self.T % TB == 0 and self.FB % 16 == 0

    def kc(self, K):
        n = K // 128
        for c in (8, 4, 2, 1):
            if n % c == 0:
                return c

    def rows(self, name):
        K, N = self.mats[name]
        tot = K * N // 2048
        assert tot % 8 == 0
        return tot // 8

    def wchunks(self, name):
        rr = self.rows(name)
        cra = chunk_rows(rr, 4096, 2, limit=self.cc_max)
        crb = chunk_rows(2 * rr, 4096, 4, limit=self.cc_max)
        return rr, cra, crb

    def np_cols(self):
        c = self
        o = {}
        p = 0
        for nm, n in (("g1", c.L * c.DC), ("g2", c.L * c.DC), ("gf", c.DC), ("bg", c.L * 2 * c.DC),
                      ("cw", c.L * 3 * c.FB), ("cb", c.L * c.FB), ("sg", c.L * 2), ("mk", 2)):
            o[nm] = p
            p += n
        o["_n"] = p
        return o


class Buf:
    __slots__ = ("name", "lw", "rd", "rdd", "lastdma", "dcount")

    def __init__(self, name):
        self.name = name
        self.lw = None
        self.rd = {}
        self.rdd = []
        self.lastdma = {}
        self.dcount = {}


class Op:
    __slots__ = ("idx", "eng", "fn", "deps", "kind", "semkey", "val", "sig", "epoch", "n")


class Sched:
    def __init__(self, nc, safe_same=True):
        self.nc = nc
        self.ops = []
        self.epoch = 0
        self.safe_same = safe_same
        self.sems = {}
        self.cccount = 0

    def _add(self, o, r, w):
        deps = {}

        def add(d):
            if d is None:
                return
            if d.kind == "c":
                k = d.eng
                if k not in deps or deps[k].idx < d.idx:
                    deps[k] = d
            else:
                deps[("x", d.idx)] = d
        for b in r:
            add(b.lw)
        for b in w:
            add(b.lw)
            for d in b.rd.values():
                add(d)
            for d in b.rdd:
                add(d)
        return deps

    def _fin(self, o, deps, r, w):
        o.deps = list(deps.values())
        o.idx = len(self.ops)
        o.sig = False
        o.epoch = self.epoch
        for b in r:
            if o.kind == "c":
                b.rd[o.eng] = o
            else:
                b.rdd.append(o)
        for b in w:
            b.lw = o
            b.rd = {}
            b.rdd = []
        self.ops.append(o)
        return o

    def op(self, eng, fn, r=(), w=()):
        o = Op()
        o.eng, o.fn, o.kind = eng, fn, "c"
        deps = self._add(o, r, w)
        o.idx = len(self.ops)
        return self._fin(o, deps, r, w)

    def dma(self, eng, fn, dbuf, r=(), w=(), n=1):
        o = Op()
        o.eng, o.fn, o.kind, o.n = eng, fn, "d", n
        deps = self._add(o, r, w)
        prev = dbuf.lastdma.get(eng)
        if prev is not None:
            deps[("x", prev.idx)] = prev
        dbuf.lastdma[eng] = o
        dbuf.dcount[eng] = dbuf.dcount.get(eng, 0) + 16 * n
        o.semkey = ("d", id(dbuf), eng)
        o.val = dbuf.dcount[eng]
        return self._fin(o, deps, r, w)

    def cc(self, fn, r=(), w=()):
        o = Op()
        o.eng, o.fn, o.kind = "pool", fn, "cc"
        deps = self._add(o, r, w)
        self.cccount += 1
        o.semkey = ("cc",)
        o.val = self.cccount
        return self._fin(o, deps, r, w)

    def sem(self, key):
        if key not in self.sems:
            self.sems[key] = self.nc.alloc_semaphore("sm%d" % len(self.sems))
        return self.sems[key]

    def emit(self):
        nc = self.nc
        engs = {"pe": nc.tensor, "act": nc.scalar, "dve": nc.vector, "pool": nc.gpsimd, "sp": nc.sync}
        for o in self.ops:
            for d in o.deps:
                if d.kind == "c":
                    if d.eng == o.eng and o.kind == "c" and (d.eng == "pe" or not self.safe_same):
                        continue
                    d.sig = True
        cnt = {}
        for o in self.ops:
            if o.kind == "c" and o.sig:
                k = ("c", o.eng, o.epoch)
                cnt[k] = cnt.get(k, 0) + 1
                o.semkey = k
                o.val = cnt[k]
        waited = {}
        nwait = 0
        for o in self.ops:
            E = engs[o.eng]
            for d in sorted(o.deps, key=lambda d: -(d.val if (d.kind != "c" or d.sig) else 0)):
                if d.kind == "c" and not d.sig:
                    continue
                if d.kind == "c" and d.eng == o.eng and o.kind == "c" and (d.eng == "pe" or not self.safe_same):
                    continue
                wk = (o.eng, d.semkey)
                if waited.get(wk, 0) >= d.val:
                    continue
                E.wait_ge(self.sem(d.semkey), d.val)
                waited[wk] = d.val
                nwait += 1
            if o.fn is None:
                continue
            res = o.fn(E)
            if o.kind == "c":
                if o.sig:
                    inst = res[-1] if isinstance(res, (list, tuple)) else res
                    inst.then_inc(self.sem(o.semkey), 1)
            elif o.kind == "d":
                lst = res if isinstance(res, (list, tuple)) else [res]
                assert len(lst) == o.n
                for inst in lst:
                    inst.then_inc(self.sem(o.semkey), 16)
            else:
                res.then_inc(self.sem(o.semkey))
        return nwait


def build(cfg, safe_same=True):
    c = cfg
    D, T, L, TB, DC, NTB, TT = c.D, c.T, c.L, c.TB, c.DC, c.NTB, c.TT
    NF, NFB, GC, GCN, H, DA, NAB, DIN, DFF, FB = c.NF, c.NFB, c.GC, c.GCN, c.H, c.DA, c.NAB, c.DIN, c.DFF, c.FB
    KCH, LB, WL = c.KCH, c.LB, c.WL
    pc = c.np_cols()
    NP = pc["_n"]
    nc = bass.Bass("TRN2", target_bir_lowering=False)
    S = Sched(nc, safe_same=safe_same)

    xT_in = nc.dram_tensor("xT", [D, T], F32, kind="ExternalInput")
    outT = nc.dram_tensor("outT", [D, T], F32, kind="ExternalOutput")
    win = {nm: nc.dram_tensor(nm, [L * c.rows(nm), 2048], F32, kind="ExternalInput") for nm in c.mats}
    params_d = nc.dram_tensor("params", [128, NP], F32, kind="ExternalInput")
    lamb_d = nc.dram_tensor("lamb", [128, L * 512], F32, kind="ExternalInput")
    csc_d = nc.dram_tensor("csc", [128, GCN * 2 * GC], BF16, kind="ExternalInput")
    ct_d = nc.dram_tensor("ct", [2 * 2 * T, T], BF16, kind="ExternalInput")
    tr_d = nc.dram_tensor("tr", [H, LB], F32, kind="ExternalInput")

    XS = nc.dram_tensor("XS", [D, T], F32)
    wsh = {(l, nm): nc.dram_tensor("wsh_%d_%s" % (l, nm), [c.rows(nm), 2048], BF16) for l in range(L) for nm in c.mats}
    wfull = {(l, nm): nc.dram_tensor("wf_%d_%s" % (l, nm), [8 * c.rows(nm), 2048], BF16) for l in range(L) for nm in c.mats}
    wmid = {(l, nm): nc.dram_tensor("wm_%d_%s" % (l, nm), [2 * c.rows(nm), 2048], BF16) for l in range(L) for nm in c.mats}
    QX = nc.dram_tensor("QX", [NAB * 128, T], BF16)
    KX = nc.dram_tensor("KX", [NAB * 128, T], BF16)
    KG = nc.dram_tensor("KG", [2 * NAB * 128, T], BF16)
    VX = nc.dram_tensor("VX", [T, DA], BF16)
    VG = nc.dram_tensor("VG", [2 * T, DA], BF16)
    AFX = nc.dram_tensor("AFX", [T, 2 * NF], BF16)
    AFG = nc.dram_tensor("AFG", [2 * T, 2 * NF], BF16)
    GS = nc.dram_tensor("GS", [2 * DC * 128, T], BF16)
    HX = nc.dram_tensor("HX", [128, 2 * DC], BF16)
    HG = nc.dram_tensor("HG", [256, 2 * DC], BF16)
    MB = nc.dram_tensor("MB", [H * 128, LB], F32)

    b_XS = [Buf("XS%d" % i) for i in range(DC)]
    b_wfull = {}
    crK = chunk_rows(NAB * 128, T * 2, 2, mult=128, limit=c.cc_max)
    crV = chunk_rows(T, DA * 2, 2, mult=128, limit=c.cc_max)
    crA = chunk_rows(T, 2 * NF * 2, 2, mult=128, limit=c.cc_max)
    b_wsh = {k: Buf("wsh%s" % (k,)) for k in wsh}
    b_QX, b_KX, b_VX, b_AFX = (Buf(n) for n in ("QX", "KX", "VX", "AFX"))
    b_KG = [Buf("KG%d" % i) for i in range(NAB * 128 // crK)]
    b_VG = [Buf("VG%d" % i) for i in range(T // crV)]
    b_AFG = [Buf("AFG%d" % i) for i in range(T // crA)]
    b_GS = [Buf("GS%d" % i) for i in range(2 * DC)]
    b_HX, b_HG, b_MB, b_out = Buf("HX"), [Buf("HG")], [Buf("MB%d" % h) for h in range(H)], [Buf("out%d" % i) for i in range(DC)]

    def sb(name, shape, dt):
        return nc.alloc_sbuf_tensor("s_" + name, shape, dt)
    R1 = sb("R1", [128, DC, T], BF16)
    b_R1 = [Buf("R1_%d" % i) for i in range(DC)]
    R2 = sb("R2", [128, 16, T], BF16)
    b_R2 = [Buf("R2_%d" % i) for i in range(16)]
    NSLOT = 5
    SLOTE = 4096
    slots = [sb("slot%d" % i, [128, SLOTE], BF16) for i in range(NSLOT)]
    b_slots = [Buf("slot%d" % i) for i in range(NSLOT)]
    slot_ctr = [0]

    def next_slot():
        i = slot_ctr[0] % NSLOT
        slot_ctr[0] += 1
        return slots[i], b_slots[i]
    params = sb("params", [128, NP], F32)
    b_params = Buf("params")
    ones_bf = sb("ones", [128, 128], BF16)
    b_ones = Buf("ones")
    csc = sb("csc", [128, GCN, 2 * GC], BF16)
    b_csc = Buf("csc")
    NXS = 3
    XW = sb("XW", [128, NXS * T], F32)
    xst = [XW[:, i * T:(i + 1) * T] for i in range(NXS)]
    b_xst = [Buf("xst%d" % i) for i in range(NXS)]
    xctr = [0]

    def next_xst():
        i = xctr[0] % NXS
        xctr[0] += 1
        return xst[i], b_xst[i]
    NST = 4
    st16 = [sb("st16_%d" % i, [128, T], BF16) for i in range(NST)]
    b_st16 = [Buf("st16_%d" % i) for i in range(NST)]
    sctr = [0]

    def next_st16():
        i = sctr[0] % NST
        sctr[0] += 1
        return st16[i], b_st16[i]
    gt = [sb("gt%d" % i, [128, T], F32) for i in range(6)]
    b_gt = [Buf("gt%d" % i) for i in range(6)]
    rstd = gt[5]
    b_rstd = b_gt[5]
    Wb = XW
    assert LB == NXS * T
    lamt = sb("lamt", [128, 512], F32)
    b_lamt = Buf("lamt")
    lsc = sb("lsc", [128, 8], F32)
    b_lsc = Buf("lsc")
    ltmp = sb("ltmp", [128, 128], F32)
    b_ltmp = Buf("ltmp")
    tmpS = [gt[4], gt[5]]
    b_tmpS = [b_gt[4], b_gt[5]]
    Pt = [sb("Pt%d" % i, [128, TB], BF16) for i in range(3)]
    b_Pt = [Buf("Pt%d" % i) for i in range(3)]
    rs = sb("rs", [128, TB], F32)
    b_rs = Buf("rs")
    on = [[gt[m * 2 + j] for j in range(2)] for m in range(2)]
    b_on = [[b_gt[m * 2 + j] for j in range(2)] for m in range(2)]
    sq16 = [sb("sq16_%d" % i, [128, TB], BF16) for i in range(2)]
    b_sq16 = [Buf("sq16_%d" % i) for i in range(2)]
    tk16 = [sb("tk16_%d" % i, [128, 512], BF16) for i in range(3)]
    b_tk16 = [Buf("tk16_%d" % i) for i in range(3)]
    tkc = [0]

    def next_tk():
        i = tkc[0] % 3
        tkc[0] += 1
        return tk16[i], b_tk16[i]
    assert NFB <= 16
    ufT = R2
    b_ufT = b_R2
    yf32 = gt[0:4]
    b_yf32 = b_gt[0:4]
    gsb = sb("gsb", [128, T + 2], F32)
    b_gsb = Buf("gsb")
    cva, b_cva = gt[4], b_gt[4]
    cvb, b_cvb = gt[5], b_gt[5]
    gg = st16
    b_gg = b_st16
    hho = sb("hho", [128, 2, DC], BF16)
    b_hho = Buf("hho")
    hh2 = sb("hh2", [128, 2, DC], BF16)
    b_hh2 = Buf("hh2")
    ghal = sb("ghal", [128, FB, 2], F32)
    b_ghal = Buf("ghal")

    banks = [nc.alloc_psum_tensor("bank%d" % i, [128, 512], F32) for i in range(8)]
    b_banks = [Buf("bank%d" % i) for i in range(8)]

    def pcol(nm, off, n=1):
        return params[:, pc[nm] + off: pc[nm] + off + n]

    evac_rr = [0]

    def evac_eng():
        evac_rr[0] += 1
        return "act" if evac_rr[0] % 2 == 0 else "dve"

    def copy_op(eng, out, in_, r, w, scale=None):
        if eng == "act":
            if scale is None:
                S.op("act", lambda E: E.activation(out, in_, ACTF.Copy), r=r, w=w)
            else:
                S.op("act", lambda E: E.activation(out, in_, ACTF.Copy, scale=scale), r=r, w=w)
        else:
            if scale is None:
                S.op("dve", lambda E: E.tensor_copy(out, in_), r=r, w=w)
            else:
                S.op("dve", lambda E: E.tensor_scalar(out, in_, scale, None, ALU.mult), r=r, w=w)

    def slab_src(l, nm, ng, kg):
        K, N = c.mats[nm]
        kc = c.kc(K)
        KG = K // 128 // kc
        srows = kc * 32
        r0 = (ng * KG + kg) * srows
        t = wfull[(l, nm)]
        return bass.AP(t, r0 * 2048, [(kc * 512, 128), (1, kc * 512)]), kc

    def load_slab(l, nm, ng, kg):
        src, kc = slab_src(l, nm, ng, kg)
        sl, bsl = next_slot()
        dst = sl[:, 0:kc * 512]
        S.dma("sp", lambda E: E.dma_start(out=dst, in_=src), bsl, r=b_wfull[(l, nm)], w=[bsl])
        return sl, bsl, kc

    def proj_group(l, nm, ng, kgs, rhs_fn, rbufs_fn):
        K, N = c.mats[nm]
        first = True
        for kgi, kg in enumerate(kgs):
            sl, bsl, kc = load_slab(l, nm, ng, kg)
            slv = sl[:, 0:kc * 512].rearrange("p (c n) -> p c n", c=kc)
            for b in range(4):
                for tb in range(NTB):
                    bk = banks[b * NTB + tb]
                    bb = b_banks[b * NTB + tb]

                    def fn(E, b=b, tb=tb, bk=bk, slv=slv, kc=kc, kg=kg, kgi=kgi):
                        last = None
                        for cc in range(kc):
                            cg = kg * kc + cc
                            last = E.matmul(bk[:, 0:TB], lhsT=slv[:, cc, b * 128:(b + 1) * 128], rhs=rhs_fn(cg, tb),
                                            start=(kgi == 0 and cc == 0), stop=(kgi == len(kgs) - 1 and cc == kc - 1))
                        return last
                    rb = [bsl]
                    for cc in range(kc):
                        rb += rbufs_fn(kg * kc + cc)
                    S.op("pe", fn, r=rb, w=[bb])

    S.dma("sp", lambda E: E.dma_start(out=params[:, :], in_=params_d[:, :]), b_params, w=[b_params])
    S.dma("sp", lambda E: E.dma_start(out=csc[:, :, :], in_=csc_d[:, :].rearrange("p (c n) -> p c n", c=GCN)), b_csc, w=[b_csc])
    S.op("dve", lambda E: E.memset(ones_bf[:, :], 1.0), w=[b_ones])
    S.dma("sp", lambda E: E.dma_start(out=XS[:, :], in_=xT_in[:, :]), b_XS[0], w=b_XS)
    for h in range(H):
        S.dma("sp", lambda E, h=h: E.dma_start(out=Wb[:, :], in_=tr_d[h:h + 1, :].partition_broadcast(128)), b_xst[0], w=b_xst)
        S.dma("sp", lambda E, h=h: E.dma_start(out=MB[h * 128:(h + 1) * 128, :], in_=Wb[:, :]), b_xst[0], r=b_xst, w=[b_MB[h]])

    def chunked_ag(src, dst, rows, cr, groups, nr, rbufs, wbufs):
        assert rows % cr == 0 and len(wbufs) == rows // cr
        for i in range(rows // cr):
            S.cc(lambda E, i=i: E.collective_compute("AllGather", ALU.bypass, replica_groups=groups,
                                                     ins=[src[i * cr:(i + 1) * cr, :]], outs=[dst[i * nr * cr:(i + 1) * nr * cr, :]]),
                 r=rbufs, w=[wbufs[i]])

    castsem = [Buf("castsem%d" % i) for i in range(2)]
    castctr = [0]

    def prologue(l, names):
        for nm in names:
            rr = c.rows(nm)
            step = 1024
            pieces = []
            for r0 in range(0, rr, step):
                r1 = min(rr, r0 + step)
                bpiece = Buf("wshp")
                pieces.append(bpiece)
                cs_ = castsem[castctr[0] % 2]
                castctr[0] += 1
                S.dma("pool", lambda E, nm=nm, r0=r0, r1=r1: E.dma_start(
                    out=wsh[(l, nm)][r0:r1, :], in_=win[nm][l * c.rows(nm) + r0: l * c.rows(nm) + r1, :]),
                    cs_, w=[bpiece])
            if AG_MODE == "8":
                b_wfull[(l, nm)] = [Buf("wf")]
                S.cc(lambda E, nm=nm: E.collective_compute("AllGather", ALU.bypass, replica_groups=[list(range(8))],
                                                           ins=[wsh[(l, nm)].ap().opt()], outs=[wfull[(l, nm)].ap().opt()]),
                     r=pieces, w=b_wfull[(l, nm)])
            else:
                _, cra, crb = c.wchunks(nm)
                bmid = [Buf("wmid") for _ in range(rr // cra)]
                chunked_ag(wsh[(l, nm)], wmid[(l, nm)], rr, cra, [[0, 4], [1, 5], [2, 6], [3, 7]], 2, pieces, bmid)
                b_wfull[(l, nm)] = [Buf("wf") for _ in range(2 * rr // crb)]
                chunked_ag(wmid[(l, nm)], wfull[(l, nm)], 2 * rr, crb, [[0, 1, 2, 3], [4, 5, 6, 7]], 4, bmid, b_wfull[(l, nm)])

    PAIRS = [[0, 1], [2, 3], [4, 5], [6, 7]]

    def pair_ag(src, dst, bsrc, bdst, rows, cr):
        chunked_ag(src, dst, rows, cr, PAIRS, 2, [bsrc], bdst)

    def rmsnorm(gain_col, dst_ap_fn, dst_buf_fn, to_dram=None):
        for ci in range(DC):
            xt, bxt = next_xst()
            S.dma("sp", lambda E, ci=ci, xt=xt: E.dma_start(out=xt[:, :], in_=XS[ci * 128:(ci + 1) * 128, :]), bxt,
                  r=[b_XS[ci]], w=[bxt])
            sq, bsq = next_st16()
            S.op("act", lambda E, xt=xt, sq=sq: E.activation(sq[:, :], xt[:, :], ACTF.Square), r=[bxt], w=[bsq])
            for tb in range(NTB):
                S.op("pe", lambda E, sq=sq, tb=tb, ci=ci: E.matmul(banks[tb][:, 0:TB], lhsT=ones_bf[:, :], rhs=sq[:, tb * TB:(tb + 1) * TB],
                                                                start=(ci == 0), stop=(ci == DC - 1)),
                     r=[bsq, b_ones], w=[b_banks[tb]])
        for tb in range(NTB):
            S.op("act", lambda E, tb=tb: E.activation(rstd[:, tb * TB:(tb + 1) * TB], banks[tb][:, 0:TB], ACTF.Sqrt,
                                                      bias=eps_t[:, 0:1], scale=1.0 / D), r=[b_banks[tb], b_eps], w=[b_rstd])
        S.op("dve", lambda E: E.reciprocal(rstd[:, :], rstd[:, :]), r=[b_rstd], w=[b_rstd])
        for ci in range(DC):
            xt, bxt = next_xst()
            S.dma("sp", lambda E, ci=ci, xt=xt: E.dma_start(out=xt[:, :], in_=XS[ci * 128:(ci + 1) * 128, :]), bxt,
                  r=[b_XS[ci]], w=[bxt])
            if to_dram is None:
                S.op("dve", lambda E, ci=ci, xt=xt: E.scalar_tensor_tensor(dst_ap_fn(ci), xt[:, :], gain_col(ci), rstd[:, :],
                                                                         ALU.mult, ALU.mult),
                     r=[bxt, b_rstd, b_params], w=[dst_buf_fn(ci)])
            else:
                S.op("dve", lambda E, ci=ci, xt=xt: E.scalar_tensor_tensor(xt[:, :], xt[:, :], gain_col(ci), rstd[:, :],
                                                                         ALU.mult, ALU.mult),
                     r=[bxt, b_rstd, b_params], w=[bxt])
                S.dma("act", lambda E, ci=ci, xt=xt: E.dma_start(out=to_dram[ci * 128:(ci + 1) * 128, :], in_=xt[:, :]), bxt,
                      r=[bxt], w=[b_out[ci]])

    eps_t = sb("eps_t", [128, 1], F32)
    b_eps = Buf("eps")
    S.op("dve", lambda E: E.memset(eps_t[:, :], EPS), w=[b_eps])

    def x_rmw(j, bank_ids):
        xt, bxt = next_xst()
        S.dma("sp", lambda E, xt=xt: E.dma_start(out=xt[:, :], in_=XS[j * 128:(j + 1) * 128, :]), bxt, r=[b_XS[j]], w=[bxt])
        for tb in range(NTB):
            bi = bank_ids[tb]
            S.op("dve", lambda E, xt=xt, tb=tb, bi=bi: E.tensor_tensor(xt[:, tb * TB:(tb + 1) * TB], xt[:, tb * TB:(tb + 1) * TB],
                                                                     banks[bi][:, 0:TB], ALU.add), r=[bxt, b_banks[bi]], w=[bxt])
        S.dma("act", lambda E, xt=xt: E.dma_start(out=XS[j * 128:(j + 1) * 128, :], in_=xt[:, :]), bxt, r=[bxt], w=[b_XS[j]])

    prologue(0, ["w_in", "w_gate"])

    for l in range(L):
        S.epoch = l
        lam_init = 0.8 - 0.6 * math.exp(-0.3 * l)
        S.dma("sp", lambda E, l=l: E.dma_start(out=lamt[:, :], in_=lamb_d[:, l * 512:(l + 1) * 512]), b_lamt, w=[b_lamt])
        for i in range(2):
            S.op("dve", lambda E, i=i: E.tensor_tensor(ltmp[:, :], lamt[:, (2 * i) * 128:(2 * i + 1) * 128],
                                                     lamt[:, (2 * i + 1) * 128:(2 * i + 2) * 128], ALU.mult), r=[b_lamt], w=[b_ltmp])
            S.op("dve", lambda E, i=i: E.tensor_reduce(lsc[:, i:i + 1], ltmp[:, :], AX.X, ALU.add), r=[b_ltmp], w=[b_lsc])
        S.op("act", lambda E: E.activation(lsc[:, 2:4], lsc[:, 0:2], ACTF.Exp), r=[b_lsc], w=[b_lsc])
        S.op("dve", lambda E, li=lam_init: E.scalar_tensor_tensor(lsc[:, 4:5], lsc[:, 3:4], -li, lsc[:, 2:3], ALU.add, ALU.subtract),
             r=[b_lsc], w=[b_lsc])
        S.op("dve", lambda E, l=l, li=lam_init: E.tensor_scalar(lsc[:, 5:7], pcol("sg", l * 2, 2), 1.0 - li, None, ALU.mult),
             r=[b_params], w=[b_lsc])

        rmsnorm(lambda ci, l=l: pcol("g1", l * DC + ci), lambda ci: R1[:, ci, :], lambda ci: b_R1[ci])

        hr = lambda cg, tb: R1[:, cg, tb * TB:(tb + 1) * TB]
        hb = lambda cg: [b_R1[cg]]
        KGin = DC // c.kc(D)
        allkg = list(range(KGin))
        n_uf, n_q = NF // 512, DA // 512
        for ng in range(DIN // 512):
            if ng < n_uf + 2 * n_q:
                proj_group(l, "w_in", ng, allkg, hr, hb)
                for b in range(4):
                    if ng < n_uf:
                        fbk = ng * 4 + b
                        for tb in range(NTB):
                            bi = b * NTB + tb
                            copy_op(evac_eng(), ufT[:, fbk, tb * TB:(tb + 1) * TB], banks[bi][:, 0:TB], [b_banks[bi]], [b_ufT[fbk]])
                    else:
                        isq = ng < n_uf + n_q
                        blk = (ng - n_uf - (0 if isq else n_q)) * 4 + b
                        st, bst = next_st16()
                        for tb in range(NTB):
                            bi = b * NTB + tb
                            copy_op(evac_eng(), st[:, tb * TB:(tb + 1) * TB], banks[bi][:, 0:TB], [b_banks[bi]], [bst],
                                    scale=(128 ** -0.5) if isq else None)
                        dstT = QX if isq else KX
                        S.dma("act", lambda E, st=st, blk=blk, dstT=dstT: E.dma_start(out=dstT[blk * 128:(blk + 1) * 128, :], in_=st[:, :]),
                              bst, r=[bst], w=[b_QX if isq else b_KX])
                if ng == n_uf - 1:
                    for tt in range(TT):
                        for g in range(4):
                            bi = (tt * 4 + g) % 8

                            def fn(E, tt=tt, g=g, bi=bi):
                                last = None
                                for cc in range(GCN):
                                    last = E.matmul(banks[bi][:, 0:2 * GC], lhsT=ufT[:, g * GCN + cc, tt * 128:(tt + 1) * 128],
                                                    rhs=csc[:, cc, :], start=(cc == 0), stop=(cc == GCN - 1))
                                return last
                            S.op("pe", fn, r=[b_ufT[g * GCN + cc] for cc in range(GCN)] + [b_csc], w=[b_banks[bi]])
                            tk, btk = next_tk()
                            copy_op(evac_eng(), tk[:, 0:2 * GC], banks[bi][:, 0:2 * GC], [b_banks[bi]], [btk])
                            S.dma("act", lambda E, tk=tk, tt=tt, g=g: E.dma_start(
                                out=AFX[tt * 128:(tt + 1) * 128, g * 2 * GC:(g + 1) * 2 * GC], in_=tk[:, 0:2 * GC]), btk,
                                r=[btk], w=[b_AFX])
                    pair_ag(AFX, AFG, b_AFX, b_AFG, T, crA)
                if ng == n_uf + 2 * n_q - 1:
                    pair_ag(KX, KG, b_KX, b_KG, NAB * 128, crK)
            else:
                gv = ng - n_uf - 2 * n_q
                for kgi, kg in enumerate(allkg):
                    sl, bsl, kc = load_slab(l, "w_in", ng, kg)
                    slv = sl[:, 0:kc * 512].rearrange("p (c n) -> p c n", c=kc)
                    for tt in range(TT):
                        def fn(E, tt=tt, slv=slv, kc=kc, kg=kg, kgi=kgi):
                            last = None
                            for cc in range(kc):
                                cg = kg * kc + cc
                                last = E.matmul(banks[tt][:, :], lhsT=R1[:, cg, tt * 128:(tt + 1) * 128], rhs=slv[:, cc, :],
                                                start=(kgi == 0 and cc == 0), stop=(kgi == len(allkg) - 1 and cc == kc - 1))
                            return last
                        S.op("pe", fn, r=[bsl] + [b_R1[kg * kc + cc] for cc in range(kc)], w=[b_banks[tt]])
                for tt in range(TT):
                    tk, btk = next_tk()
                    copy_op(evac_eng(), tk[:, :], banks[tt][:, :], [b_banks[tt]], [btk])
                    S.dma("act", lambda E, tk=tk, tt=tt, gv=gv: E.dma_start(out=VX[tt * 128:(tt + 1) * 128, gv * 512:(gv + 1) * 512], in_=tk[:, :]),
                          btk, r=[btk], w=[b_VX])
        pair_ag(VX, VG, b_VX, b_VG, T, crV)
        prologue(l, ["w_fo", "w_ao", "w_o", "w_up", "w_down"] if l == 0 else ["w_up", "w_down"])

        for ng in range(2 * D // 512):
            proj_group(l, "w_gate", ng, allkg, hr, hb)
            for b in range(4):
                blk = ng * 4 + b
                st, bst = next_st16()
                for tb in range(NTB):
                    bi = b * NTB + tb
                    S.op("act", lambda E, st=st, tb=tb, bi=bi, blk=blk, l=l: E.activation(
                        st[:, tb * TB:(tb + 1) * TB], banks[bi][:, 0:TB], ACTF.Sigmoid, bias=pcol("bg", l * 2 * DC + blk)),
                        r=[b_banks[bi], b_params], w=[bst])
                S.dma("act", lambda E, st=st, blk=blk: E.dma_start(out=GS[blk * 128:(blk + 1) * 128, :], in_=st[:, :]), bst,
                      r=[bst], w=[b_GS[blk]])


        for sb_ in range(NTB):
            for sc in range(KCH):
                sl, bsl = next_slot()
                afv = sl[:, 0:2 * NF]
                arow = ag_row((sc % TT) * 128, sc // TT, crA, 2)
                S.dma("sp", lambda E, afv=afv, arow=arow: E.dma_start(out=afv, in_=AFG[arow:arow + 128, :]), bsl, r=b_AFG, w=[bsl])
                tl, btl = next_slot()
                tv = tl[:, 0:2 * TB].rearrange("p (a n) -> p a n", a=2)
                src = bass.AP(ct_d, sc * 128 * T + sb_ * TB, [(T, 128), (2 * T * T, 2), (1, TB)])
                S.dma("sp", lambda E, tv=tv, src=src: E.dma_start(out=tv, in_=src), btl, w=[btl])
                for fb in range(NFB):
                    g, j = fb // GCN, fb % GCN
                    c0 = g * 2 * GC + j * 128

                    def fn(E, fb=fb, c0=c0, afv=afv, tv=tv, sc=sc):
                        E.matmul(banks[fb][:, 0:TB], lhsT=afv[:, c0:c0 + 128], rhs=tv[:, 0, :], start=(sc == 0), stop=False)
                        return E.matmul(banks[fb][:, 0:TB], lhsT=afv[:, c0 + GC:c0 + GC + 128], rhs=tv[:, 1, :], start=False,
                                        stop=(sc == KCH - 1))
                    S.op("pe", fn, r=[bsl, btl], w=[b_banks[fb]])
            for fb in range(NFB):
                copy_op(evac_eng(), R1[:, NAB + fb, sb_ * TB:(sb_ + 1) * TB], banks[fb][:, 0:TB], [b_banks[fb]], [b_R1[NAB + fb]])

        for h in range(H):
            S.dma("sp", lambda E, h=h: E.dma_start(out=Wb[:, 0:WL], in_=bass.AP(MB, h * 128 * LB + 127, [(LB - 1, 128), (1, WL)])),
                  b_xst[0], r=[b_MB[h]], w=b_xst)
            ksl, bks = next_slot()
            kv = ksl[:, 0:2 * 2 * T].rearrange("p (m r t) -> p m r t", m=2, r=2)
            for m in range(2):
                krow = ag_row((2 * h + m) * 128, 0, crK, 2)
                src = bass.AP(KG, krow * T, [(T, 128), (crK * T, 2), (1, T)])
                S.dma("sp", lambda E, m=m, src=src, kv=kv: E.dma_start(out=kv[:, m, :, :], in_=src), bks, r=b_KG, w=[bks])
            kfl = ksl[:, 0:2 * 2 * T].rearrange("p (m k) -> p m k", m=2)
            vsl, bvs = next_slot()
            vv = vsl[:, 0:KCH * 256].rearrange("p (k e) -> p k e", k=KCH)
            nvc = crV // 128

            def vload(E, h=h, vv=vv):
                res = []
                for r_ in range(2):
                    for i_ in range(T // crV):
                        vrow = ag_row(i_ * crV, r_, crV, 2)
                        k0 = r_ * TT + i_ * nvc
                        res.append(E.dma_start(out=vv[:, k0:k0 + nvc, :],
                                               in_=bass.AP(VG, vrow * DA + h * 256, [(DA, 128), (128 * DA, nvc), (1, 256)])))
                return res
            S.dma("sp", vload, bvs, r=b_VG, w=[bvs], n=2 * (T // crV))
            qsl, bqs = next_slot()
            qv = qsl[:, 0:2 * T].rearrange("p (m t) -> p m t", m=2)
            S.dma("sp", lambda E, h=h, qv=qv: E.dma_start(out=qv, in_=bass.AP(QX, 2 * h * 128 * T, [(T, 128), (128 * T, 2), (1, T)])),
                  bqs, r=[b_QX], w=[bqs])
            for qb in range(NTB):
                for m in range(2):
                    acc = [m * 3 + 0, m * 3 + 1, m * 3 + 2]
                    pend = None
                    for kc_ in range(KCH + 1):
                        if kc_ < KCH:
                            sbk = 6 + (kc_ % 2)
                            S.op("pe", lambda E, m=m, kc_=kc_, sbk=sbk, qb=qb, kfl=kfl, qv=qv: E.matmul(
                                banks[sbk][:, 0:TB], lhsT=kfl[:, m, kc_ * 128:(kc_ + 1) * 128], rhs=qv[:, m, qb * TB:(qb + 1) * TB],
                                start=True, stop=True), r=[bks, bqs], w=[b_banks[sbk]])
                            ti = kc_ % 2
                            jb = 2 * T - kc_ * 128 + qb * TB - 128
                            S.op("dve", lambda E, ti=ti, sbk=sbk, jb=jb: E.tensor_tensor(tmpS[ti][:, 0:TB], banks[sbk][:, 0:TB],
                                                                                        Wb[:, jb:jb + TB], ALU.add),
                                 r=[b_banks[sbk]] + b_xst, w=[b_tmpS[ti]])
                            pi = kc_ % 3
                            S.op("act", lambda E, ti=ti, pi=pi: E.activation(Pt[pi][:, :], tmpS[ti][:, 0:TB], ACTF.Exp),
                                 r=[b_tmpS[ti]], w=[b_Pt[pi]])
                        if pend is not None:
                            pk, ppi = pend

                            def fn(E, pk=pk, ppi=ppi, acc=acc, vv=vv):
                                E.matmul(banks[acc[0]][:, 0:TB], lhsT=vv[:, pk, 0:128], rhs=Pt[ppi][:, :], start=(pk == 0), stop=(pk == KCH - 1))
                                E.matmul(banks[acc[1]][:, 0:TB], lhsT=vv[:, pk, 128:256], rhs=Pt[ppi][:, :], start=(pk == 0), stop=(pk == KCH - 1))
                                return E.matmul(banks[acc[2]][:, 0:TB], lhsT=ones_bf[:, :], rhs=Pt[ppi][:, :], start=(pk == 0), stop=(pk == KCH - 1))
                            S.op("pe", fn, r=[bvs, b_Pt[ppi], b_ones], w=[b_banks[a] for a in acc])
                        pend = (kc_, kc_ % 3) if kc_ < KCH else None
                    S.op("dve", lambda E, acc=acc: E.reciprocal(rs[:, :], banks[acc[2]][:, 0:TB]), r=[b_banks[acc[2]]], w=[b_rs])
                    for j in range(2):
                        S.op("dve", lambda E, m=m, j=j, acc=acc: E.tensor_tensor(on[m][j][:, 0:TB], banks[acc[j]][:, 0:TB], rs[:, :], ALU.mult),
                             r=[b_banks[acc[j]], b_rs], w=[b_on[m][j]])
                for j in range(2):
                    S.op("dve", lambda E, j=j: E.scalar_tensor_tensor(on[0][j][:, 0:TB], on[1][j][:, 0:TB], lsc[:, 4:5], on[0][j][:, 0:TB],
                                                                    ALU.mult, ALU.add), r=[b_on[1][j], b_on[0][j], b_lsc], w=[b_on[0][j]])
                    S.op("act", lambda E, j=j: E.activation(sq16[j][:, :], on[0][j][:, 0:TB], ACTF.Square), r=[b_on[0][j]], w=[b_sq16[j]])
                    S.op("pe", lambda E, j=j: E.matmul(banks[2][:, 0:TB], lhsT=ones_bf[:, :], rhs=sq16[j][:, :], start=(j == 0), stop=(j == 1)),
                         r=[b_sq16[j], b_ones], w=[b_banks[2]])
                S.op("act", lambda E: E.activation(rs[:, :], banks[2][:, 0:TB], ACTF.Sqrt, bias=eps_t[:, 0:1], scale=1.0 / 256),
                     r=[b_banks[2], b_eps], w=[b_rs])
                S.op("dve", lambda E: E.reciprocal(rs[:, :], rs[:, :]), r=[b_rs], w=[b_rs])
                for j in range(2):
                    S.op("dve", lambda E, j=j, h=h, qb=qb: E.scalar_tensor_tensor(R1[:, 2 * h + j, qb * TB:(qb + 1) * TB], on[0][j][:, 0:TB],
                                                                               lsc[:, 5 + j:6 + j], rs[:, :], ALU.mult, ALU.mult),
                         r=[b_on[0][j], b_rs, b_lsc], w=[b_R1[2 * h + j]])

        KCo = c.kc(D)
        KGo = DC // KCo
        hk = DC // 2
        for kh in range(2):
            for jg in range(hk // 4):
                ng = kh * (hk // 4) + jg
                fr = lambda cg, tb: R1[:, NAB + cg, tb * TB:(tb + 1) * TB]
                fbuf = lambda cg: [b_R1[NAB + cg]]
                proj_group(l, "w_fo", ng, list(range(NFB // c.kc(NF))), fr, fbuf)
                for b in range(4):
                    j = ng * 4 + b
                    st, bst = next_st16()
                    S.dma("sp", lambda E, st=st, j=j: E.dma_start(out=st[:, :], in_=GS[j * 128:(j + 1) * 128, :]), bst, r=[b_GS[j]], w=[bst])
                    for tb in range(NTB):
                        bi = b * NTB + tb
                        S.op("dve", lambda E, b=b, tb=tb, bi=bi, st=st: E.tensor_tensor(yf32[b][:, tb * TB:(tb + 1) * TB], banks[bi][:, 0:TB],
                                                                                      st[:, tb * TB:(tb + 1) * TB], ALU.mult),
                             r=[b_banks[bi], bst], w=[b_yf32[b]])
                ar = lambda cg, tb: R1[:, cg, tb * TB:(tb + 1) * TB]
                abuf = lambda cg: [b_R1[cg]]
                proj_group(l, "w_ao", ng, list(range(NAB // c.kc(DA))), ar, abuf)
                for b in range(4):
                    j = ng * 4 + b
                    st, bst = next_st16()
                    S.dma("sp", lambda E, st=st, j=j: E.dma_start(out=st[:, :], in_=GS[(DC + j) * 128:(DC + j + 1) * 128, :]), bst,
                          r=[b_GS[DC + j]], w=[bst])
                    for tb in range(NTB):
                        bi = b * NTB + tb
                        S.op("dve", lambda E, tb=tb, bi=bi, st=st: E.tensor_tensor(cva[:, tb * TB:(tb + 1) * TB], banks[bi][:, 0:TB],
                                                                                 st[:, tb * TB:(tb + 1) * TB], ALU.mult),
                             r=[b_banks[bi], bst], w=[b_cva])
                        S.op("dve", lambda E, b=b, tb=tb, jg=jg: E.tensor_tensor(R2[:, jg * 4 + b, tb * TB:(tb + 1) * TB], cva[:, tb * TB:(tb + 1) * TB],
                                                                               yf32[b][:, tb * TB:(tb + 1) * TB], ALU.add),
                             r=[b_cva, b_yf32[b]], w=[b_R2[jg * 4 + b]])
            mr = lambda cg, tb, kh=kh: R2[:, cg - kh * hk, tb * TB:(tb + 1) * TB]
            mbuf = lambda cg, kh=kh: [b_R2[cg - kh * hk]]
            kgs = [kh * (hk // KCo) + i for i in range(hk // KCo)]
            for og in range(D // 512):
                proj_group(l, "w_o", og, kgs, mr, mbuf)
                for b in range(4):
                    x_rmw(og * 4 + b, [b * NTB + tb for tb in range(NTB)])

        rmsnorm(lambda ci, l=l: pcol("g2", l * DC + ci), lambda ci: R1[:, ci, :], lambda ci: b_R1[ci])
        S.op("dve", lambda E: E.tensor_copy(hho[:, 0, :], R1[:, :, 0]), r=b_R1, w=[b_hho])
        S.op("dve", lambda E: E.tensor_copy(hho[:, 1, :], R1[:, :, T - 1]), r=b_R1, w=[b_hho])
        S.dma("act", lambda E: E.dma_start(out=HX[:, :], in_=hho[:, :, :].rearrange("p a c -> p (a c)")), b_hho, r=[b_hho], w=[b_HX])
        pair_ag(HX, HG, b_HX, b_HG, 128, 128)
        S.dma("sp", lambda E: [E.dma_start(out=hh2[:, 0, :], in_=HG[0:128, DC:2 * DC]),
                               E.dma_start(out=hh2[:, 1, :], in_=HG[128:256, 0:DC])], b_hh2, r=b_HG, w=[b_hh2], n=2)
        if l + 1 < L:
            prologue(l + 1, ["w_in", "w_gate", "w_fo", "w_ao", "w_o"])

        first_f0 = [True]
        for ng in range(DFF // 512):
            for kgi, kg in enumerate(allkg):
                sl, bsl, kc = load_slab(l, "w_up", ng, kg)
                slv = sl[:, 0:kc * 512].rearrange("p (c n) -> p c n", c=kc)

                def fn(E, ng=ng, kg=kg, kc=kc, slv=slv, first_f0=first_f0):
                    last = None
                    for b in range(4):
                        blk = ng * 4 + b
                        for cc in range(kc):
                            cg = kg * kc + cc
                            st_ = first_f0[0]
                            first_f0[0] = False
                            last = E.matmul(banks[7][:, blk * 2:blk * 2 + 2], lhsT=slv[:, cc, b * 128:(b + 1) * 128], rhs=hh2[:, :, cg],
                                            start=st_, stop=(cg == DC - 1), skip_group_check=True)
                    return last
                S.op("pe", fn, r=[bsl, b_hh2], w=[b_banks[7]])
        bk7 = banks[7][:, 0:FB * 2].rearrange("p (f a) -> p f a", a=2)
        for a in range(2):
            S.op("dve", lambda E, a=a: E.tensor_scalar(ghal[:, :, a], bk7[:, :, a], pcol("mk", a), None, ALU.mult),
                 r=[b_banks[7], b_params], w=[b_ghal])

        KCd = c.kc(DFF)
        for part in range(c.NPART):
            for fg in range(4):
                ngg = part * 4 + fg
                proj_group(l, "w_up", ngg, allkg, hr, hb)
                for b in range(4):
                    fblk = ngg * 4 + b
                    for tb in range(NTB):
                        bi = b * NTB + tb
                        S.op("act", lambda E, tb=tb, bi=bi: E.activation(gsb[:, 1 + tb * TB:1 + (tb + 1) * TB], banks[bi][:, 0:TB], ACTF.Copy),
                             r=[b_banks[bi]], w=[b_gsb])
                    S.op("dve", lambda E, fblk=fblk: E.tensor_copy(gsb[:, 0:1], ghal[:, fblk, 0:1]), r=[b_ghal], w=[b_gsb])
                    S.op("dve", lambda E, fblk=fblk: E.tensor_copy(gsb[:, T + 1:T + 2], ghal[:, fblk, 1:2]), r=[b_ghal], w=[b_gsb])
                    cw = lambda j, fblk=fblk, l=l: pcol("cw", (l * 3 + j) * FB + fblk)
                    S.op("dve", lambda E, fblk=fblk, l=l, cw=cw: E.tensor_scalar(cva[:, :], gsb[:, 0:T], cw(0), pcol("cb", l * FB + fblk),
                                                                               ALU.mult, ALU.add), r=[b_gsb, b_params], w=[b_cva])
                    S.op("dve", lambda E, cw=cw: E.scalar_tensor_tensor(cvb[:, :], gsb[:, 1:T + 1], cw(1), cva[:, :], ALU.mult, ALU.add),
                         r=[b_gsb, b_cva, b_params], w=[b_cvb])
                    S.op("dve", lambda E, cw=cw: E.scalar_tensor_tensor(cva[:, :], gsb[:, 2:T + 2], cw(2), cvb[:, :], ALU.mult, ALU.add),
                         r=[b_gsb, b_cvb, b_params], w=[b_cva])
                    S.op("act", lambda E, b=b: E.activation(gg[b][:, :], cva[:, :], ACTF.Gelu), r=[b_cva], w=[b_gg[b]])
                ngv = DFF // 512 + ngg
                proj_group(l, "w_up", ngv, allkg, hr, hb)
                for b in range(4):
                    for tb in range(NTB):
                        bi = b * NTB + tb
                        S.op("dve", lambda E, b=b, tb=tb, bi=bi, fg=fg: E.tensor_tensor(R2[:, fg * 4 + b, tb * TB:(tb + 1) * TB], banks[bi][:, 0:TB],
                                                                                      gg[b][:, tb * TB:(tb + 1) * TB], ALU.mult),
                             r=[b_banks[bi], b_gg[b]], w=[b_R2[fg * 4 + b]])
            ar2 = lambda cg, tb, part=part: R2[:, cg - part * 16, tb * TB:(tb + 1) * TB]
            ab2 = lambda cg, part=part: [b_R2[cg - part * 16]]
            kgs = [part * (16 // KCd) + i for i in range(16 // KCd)]
            for og in range(D // 512):
                proj_group(l, "w_down", og, kgs, ar2, ab2)
                for b in range(4):
                    x_rmw(og * 4 + b, [b * NTB + tb for tb in range(NTB)])

    S.epoch = L
    rmsnorm(lambda ci: pcol("gf", ci), None, None, to_dram=outT)
    S.op("sp", None, r=b_out + b_XS, w=[])
    S.op("pool", None, r=b_out + b_XS, w=[])
    nw = S.emit()
    return nc, (len(S.ops), nw)


def rel_bucket_np(rel):
    n = -rel
    half = NREL // 2
    ret = (n < 0).astype(np.int32) * half
    n = np.abs(n)
    max_exact = half // 2
    is_small = n < max_exact
    nf = np.maximum(n, 1).astype(np.float32)
    large = max_exact + (np.log(nf / np.float32(max_exact)) / np.float32(math.log(RELMAX / max_exact))
                         * np.float32(half - max_exact)).astype(np.int32)
    large = np.minimum(large, half - 1)
    return ret + np.where(is_small, n, large)


def stream_layout(cfg, W, K, N):
    kc = cfg.kc(K)
    KG = K // 128 // kc
    NG = N // 512
    a = W.reshape(KG, kc, 128, NG, 512).transpose(3, 0, 2, 1, 4)
    return np.ascontiguousarray(a).reshape(8, -1, 2048)


def prep_inputs(cfg, x, norm1_gain, w_in, w_fourier_out, lambdas, subln_gain, rel_bias_table, w_attn_out,
                w_gate, b_gate, w_o, norm2_gain, w_up, conv_w, conv_b, w_down, final_norm_gain):
    c = cfg
    L, D, T, DC, FB = c.L, c.D, c.T, c.DC, c.FB
    f32 = np.float32
    x = np.asarray(x, f32)
    src = {"w_in": w_in, "w_gate": w_gate, "w_fo": w_fourier_out, "w_ao": w_attn_out, "w_o": w_o, "w_up": w_up, "w_down": w_down}
    wsh = {}
    for nm, (K, N) in c.mats.items():
        rr, cra, crb = c.wchunks(nm)
        perm = weight_perm(rr, cra, crb) if AG_MODE != "8" else np.arange(8 * rr)
        per_layer = []
        for l in range(L):
            st = stream_layout(c, np.asarray(src[nm][l], f32), K, N).reshape(8 * rr, 2048)
            sh = np.empty_like(st)
            sh[perm] = st
            per_layer.append(sh.reshape(8, rr, 2048))
        wsh[nm] = np.stack(per_layer, axis=1)
    pc = c.np_cols()

    def fm(a, nblk):
        a = np.asarray(a, f32)
        lead = a.shape[:-1]
        a = a.reshape(*lead, nblk, 128)
        return np.moveaxis(a, -1, 0).reshape(128, -1)
    base = np.zeros((128, pc["_n"]), f32)
    base[:, pc["g1"]:pc["g1"] + L * DC] = fm(norm1_gain, DC)
    base[:, pc["g2"]:pc["g2"] + L * DC] = fm(norm2_gain, DC)
    base[:, pc["gf"]:pc["gf"] + DC] = fm(final_norm_gain, DC)
    base[:, pc["bg"]:pc["bg"] + L * 2 * DC] = fm(b_gate, 2 * DC)
    base[:, pc["cw"]:pc["cw"] + L * 3 * FB] = fm(conv_w, FB)
    base[:, pc["cb"]:pc["cb"] + L * FB] = fm(conv_b, FB)
    base[:, pc["sg"]:pc["sg"] + L * 2] = fm(subln_gain, 2)
    lamb = np.ascontiguousarray(np.broadcast_to(np.asarray(lambdas, f32).reshape(1, L * 512), (128, L * 512)))
    GC, GCN, S_ = c.GC, c.GCN, c.S
    cidx = np.arange(GC)
    ang = 2.0 * np.pi * ((cidx[:, None] * cidx[None, :]) % GC) / GC
    cs = np.concatenate([np.cos(ang), np.sin(ang)], axis=1)
    csc = cs.reshape(GCN, 128, 2 * GC).transpose(1, 0, 2).reshape(128, GCN * 2 * GC).astype(ml_dtypes.bfloat16)
    tbl = np.asarray(rel_bias_table, f32)
    in_maps = []
    for core in range(8):
        b, hf = core // 2, core % 2
        m = {}
        m["xT"] = np.ascontiguousarray(x[b, hf * T:(hf + 1) * T, :].T)
        for nm in c.mats:
            m[nm] = wsh[nm][core].reshape(-1, 2048)
        p = base.copy()
        p[:, pc["mk"]] = 1.0 if hf == 1 else 0.0
        p[:, pc["mk"] + 1] = 1.0 if hf == 0 else 0.0
        m["params"] = p
        m["lamb"] = lamb
        m["csc"] = csc
        s = np.arange(2 * T)
        sp = hf * T + np.arange(T)
        a2 = 2.0 * np.pi * ((s[:, None] * sp[None, :]) % S_) / S_
        nrm = 1.0 / math.sqrt(S_ * GC)
        m["ct"] = np.concatenate([np.cos(a2) * nrm, -np.sin(a2) * nrm], axis=0).astype(ml_dtypes.bfloat16)
        N_ = 3 * T - 1
        i = np.arange(N_)
        rel = i - (T - 1) - hf * T
        TD = tbl[rel_bucket_np(rel)]
        TRv = np.zeros((c.H, c.LB), f32)
        TRv[:, :N_] = TD[::-1, :].T
        m["tr"] = TRv
        in_maps.append(m)
    return in_maps


_CACHE = {}


def run(cfg, inputs, safe_same=True):
    key = (cfg.D, cfg.S, cfg.L, cfg.TB, safe_same)
    if key not in _CACHE:
        _CACHE[key] = build(cfg, safe_same=safe_same)[0]
    nc = _CACHE[key]
    in_maps = prep_inputs(cfg, **inputs)
    res = run_bass_kernel_spmd(nc, in_maps, core_ids=list(range(8)))
    out = np.empty((cfg.B, cfg.S, cfg.D), np.float32)
    for core in range(8):
        b, hf = core // 2, core % 2
        out[b, hf * cfg.T:(hf + 1) * cfg.T, :] = res.results[core]["outT"].T
    return out


def kernel(**inputs):
    return run(Cfg(), inputs)
```

```python
import math
import numpy as np
import ml_dtypes
import concourse.bass as bass
import concourse.mybir as mybir
from concourse.bass_utils import run_bass_kernel_spmd

F32 = mybir.dt.float32
BF16 = mybir.dt.bfloat16
ACTF = mybir.ActivationFunctionType
ALU = mybir.AluOpType
AX = mybir.AxisListType
EPS = 1e-6
AG_MODE = "2stage"
CC_MAX_OUT = 4 * 1024 * 1024


def chunk_rows(rows, rowbytes, nr, mult=1, limit=CC_MAX_OUT):
    best = None
    for cr in range(mult, rows + 1, mult):
        if rows % cr == 0 and cr * rowbytes * nr <= limit:
            best = cr
    assert best is not None, (rows, rowbytes, nr, mult, limit)
    return best


def ag_row(local, r, cr, nr):
    return (local // cr) * nr * cr + r * cr + (local % cr)


def weight_perm(rr, cra, crb):
    shard = [np.arange(rr) + c * rr for c in range(8)]
    mid = []
    for c in range(8):
        a, b = shard[c % 4], shard[c % 4 + 4]
        mid.append(np.concatenate([np.concatenate([a[i * cra:(i + 1) * cra], b[i * cra:(i + 1) * cra]]) for i in range(rr // cra)]))
    full = []
    for c in range(8):
        g = (c // 4) * 4
        full.append(np.concatenate([np.concatenate([mid[g + r][j * crb:(j + 1) * crb] for r in range(4)]) for j in range(2 * rr // crb)]))
    for c in range(1, 8):
        assert np.array_equal(full[c], full[0])
    return full[0]
NREL = 32
RELMAX = 128


class Cfg:
    def __init__(self, D=4096, S=2048, L=4, TB=512, cc_max=CC_MAX_OUT):
        self.D, self.S, self.L, self.TB = D, S, L, TB
        self.cc_max = cc_max
        self.B = 4
        self.T = S // 2
        self.DC = D // 128
        self.NF = D // 4
        self.GC = D // 16
        self.GCN = self.GC // 128
        self.NFB = self.NF // 128
        self.H = (D - self.NF) // 256
        self.DA = self.H * 256
        self.NAB = self.DA // 128
        self.DIN = self.NF + 3 * self.DA
        self.DFF = 2 * D
        self.FB = self.DFF // 128
        self.NTB = self.T // TB
        self.TT = self.T // 128
        self.KCH = 2 * self.T // 128
        self.LB = 3 * self.T
        self.WL = 3 * self.T - 128
        self.NPART = self.FB // 16
        self.mats = {
            "w_in": (D, self.DIN), "w_gate": (D, 2 * D), "w_fo": (self.NF, D), "w_ao": (self.DA, D),
            "w_o": (D, D), "w_up": (D, 2 * self.DFF), "w_down": (self.DFF, D),
        }
        assert self.GC % 128 == 0 and self.T % TB == 0 and self.FB % 16 == 0

    def kc(self, K):
        n = K // 128
        for c in (8, 4, 2, 1):
            if n % c == 0:
                return c

    def rows(self, name):
        K, N = self.mats[name]
        tot = K * N // 2048
        assert tot % 8 == 0
        return tot // 8

    def wchunks(self, name):
        rr = self.rows(name)
        cra = chunk_rows(rr, 4096, 2, limit=self.cc_max)
        crb = chunk_rows(2 * rr, 4096, 4, limit=self.cc_max)
        return rr, cra, crb

    def np_cols(self):
        c = self
        o = {}
        p = 0
        for nm, n in (("g1", c.L * c.DC), ("g2", c.L * c.DC), ("gf", c.DC), ("bg", c.L * 2 * c.DC),
                      ("cw", c.L * 3 * c.FB), ("cb", c.L * c.FB), ("sg", c.L * 2), ("mk", 2)):
            o[nm] = p
            p += n
        o["_n"] = p
        return o


class Buf:
    __slots__ = ("name", "lw", "rd", "rdd", "lastdma", "dcount")

    def __init__(self, name):
        self.name = name
        self.lw = None
        self.rd = {}
        self.rdd = []
        self.lastdma = {}
        self.dcount = {}


class Op:
    __slots__ = ("idx", "eng", "fn", "deps", "kind", "semkey", "val", "sig", "epoch", "n")


class Sched:
    def __init__(self, nc, safe_same=True):
        self.nc = nc
        self.ops = []
        self.epoch = 0
        self.safe_same = safe_same
        self.sems = {}
        self.cccount = 0

    def _add(self, o, r, w):
        deps = {}

        def add(d):
            if d is None:
                return
            if d.kind == "c":
                k = d.eng
                if k not in deps or deps[k].idx < d.idx:
                    deps[k] = d
            else:
                deps[("x", d.idx)] = d
        for b in r:
            add(b.lw)
        for b in w:
            add(b.lw)
            for d in b.rd.values():
                add(d)
            for d in b.rdd:
                add(d)
        return deps

    def _fin(self, o, deps, r, w):
        o.deps = list(deps.values())
        o.idx = len(self.ops)
        o.sig = False
        o.epoch = self.epoch
        for b in r:
            if o.kind == "c":
                b.rd[o.eng] = o
            else:
                b.rdd.append(o)
        for b in w:
            b.lw = o
            b.rd = {}
            b.rdd = []
        self.ops.append(o)
        return o

    def op(self, eng, fn, r=(), w=()):
        o = Op()
        o.eng, o.fn, o.kind = eng, fn, "c"
        deps = self._add(o, r, w)
        o.idx = len(self.ops)
        return self._fin(o, deps, r, w)

    def dma(self, eng, fn, dbuf, r=(), w=(), n=1):
        o = Op()
        o.eng, o.fn, o.kind, o.n = eng, fn, "d", n
        deps = self._add(o, r, w)
        prev = dbuf.lastdma.get(eng)
        if prev is not None:
            deps[("x", prev.idx)] = prev
        dbuf.lastdma[eng] = o
        dbuf.dcount[eng] = dbuf.dcount.get(eng, 0) + 16 * n
        o.semkey = ("d", id(dbuf), eng)
        o.val = dbuf.dcount[eng]
        return self._fin(o, deps, r, w)

    def cc(self, fn, r=(), w=()):
        o = Op()
        o.eng, o.fn, o.kind = "pool", fn, "cc"
        deps = self._add(o, r, w)
        self.cccount += 1
        o.semkey = ("cc",)
        o.val = self.cccount
        return self._fin(o, deps, r, w)

    def sem(self, key):
        if key not in self.sems:
            self.sems[key] = self.nc.alloc_semaphore("sm%d" % len(self.sems))
        return self.sems[key]

    def emit(self):
        nc = self.nc
        engs = {"pe": nc.tensor, "act": nc.scalar, "dve": nc.vector, "pool": nc.gpsimd, "sp": nc.sync}
        for o in self.ops:
            for d in o.deps:
                if d.kind == "c":
                    if d.eng == o.eng and o.kind == "c" and (d.eng == "pe" or not self.safe_same):
                        continue
                    d.sig = True
        cnt = {}
        for o in self.ops:
            if o.kind == "c" and o.sig:
                k = ("c", o.eng, o.epoch)
                cnt[k] = cnt.get(k, 0) + 1
                o.semkey = k
                o.val = cnt[k]
        waited = {}
        nwait = 0
        for o in self.ops:
            E = engs[o.eng]
            for d in sorted(o.deps, key=lambda d: -(d.val if (d.kind != "c" or d.sig) else 0)):
                if d.kind == "c" and not d.sig:
                    continue
                if d.kind == "c" and d.eng == o.eng and o.kind == "c" and (d.eng == "pe" or not self.safe_same):
                    continue
                wk = (o.eng, d.semkey)
                if waited.get(wk, 0) >= d.val:
                    continue
                E.wait_ge(self.sem(d.semkey), d.val)
                waited[wk] = d.val
                nwait += 1
            if o.fn is None:
                continue
            res = o.fn(E)
            if o.kind == "c":
                if o.sig:
                    inst = res[-1] if isinstance(res, (list, tuple)) else res
                    inst.then_inc(self.sem(o.semkey), 1)
            elif o.kind == "d":
                lst = res if isinstance(res, (list, tuple)) else [res]
                assert len(lst) == o.n
                for inst in lst:
                    inst.then_inc(self.sem(o.semkey), 16)
            else:
                res.then_inc(self.sem(o.semkey))
        return nwait


def build(cfg, safe_same=True):
    c = cfg
    D, T, L, TB, DC, NTB, TT = c.D, c.T, c.L, c.TB, c.DC, c.NTB, c.TT
    NF, NFB, GC, GCN, H, DA, NAB, DIN, DFF, FB = c.NF, c.NFB, c.GC, c.GCN, c.H, c.DA, c.NAB, c.DIN, c.DFF, c.FB
    KCH, LB, WL = c.KCH, c.LB, c.WL
    pc = c.np_cols()
    NP = pc["_n"]
    nc = bass.Bass("TRN2", target_bir_lowering=False)
    S = Sched(nc, safe_same=safe_same)

    xT_in = nc.dram_tensor("xT", [D, T], F32, kind="ExternalInput")
    outT = nc.dram_tensor("outT", [D, T], F32, kind="ExternalOutput")
    win = {nm: nc.dram_tensor(nm, [L * c.rows(nm), 2048], F32, kind="ExternalInput") for nm in c.mats}
    params_d = nc.dram_tensor("params", [128, NP], F32, kind="ExternalInput")
    lamb_d = nc.dram_tensor("lamb", [128, L * 512], F32, kind="ExternalInput")
    csc_d = nc.dram_tensor("csc", [128, GCN * 2 * GC], BF16, kind="ExternalInput")
    ct_d = nc.dram_tensor("ct", [2 * 2 * T, T], BF16, kind="ExternalInput")
    tr_d = nc.dram_tensor("tr", [H, LB], F32, kind="ExternalInput")

    XS = nc.dram_tensor("XS", [D, T], F32)
    wsh = {(l, nm): nc.dram_tensor("wsh_%d_%s" % (l, nm), [c.rows(nm), 2048], BF16) for l in range(L) for nm in c.mats}
    wfull = {(l, nm): nc.dram_tensor("wf_%d_%s" % (l, nm), [8 * c.rows(nm), 2048], BF16) for l in range(L) for nm in c.mats}
    wmid = {(l, nm): nc.dram_tensor("wm_%d_%s" % (l, nm), [2 * c.rows(nm), 2048], BF16) for l in range(L) for nm in c.mats}
    QX = nc.dram_tensor("QX", [NAB * 128, T], BF16)
    KX = nc.dram_tensor("KX", [NAB * 128, T], BF16)
    KG = nc.dram_tensor("KG", [2 * NAB * 128, T], BF16)
    VX = nc.dram_tensor("VX", [T, DA], BF16)
    VG = nc.dram_tensor("VG", [2 * T, DA], BF16)
    AFX = nc.dram_tensor("AFX", [T, 2 * NF], BF16)
    AFG = nc.dram_tensor("AFG", [2 * T, 2 * NF], BF16)
    GS = nc.dram_tensor("GS", [2 * DC * 128, T], BF16)
    HX = nc.dram_tensor("HX", [128, 2 * DC], BF16)
    HG = nc.dram_tensor("HG", [256, 2 * DC], BF16)
    MB = nc.dram_tensor("MB", [H * 128, LB], F32)

    b_XS = [Buf("XS%d" % i) for i in range(DC)]
    b_wfull = {}
    crK = chunk_rows(NAB * 128, T * 2, 2, mult=128, limit=c.cc_max)
    crV = chunk_rows(T, DA * 2, 2, mult=128, limit=c.cc_max)
    crA = chunk_rows(T, 2 * NF * 2, 2, mult=128, limit=c.cc_max)
    b_wsh = {k: Buf("wsh%s" % (k,)) for k in wsh}
    b_QX, b_KX, b_VX, b_AFX = (Buf(n) for n in ("QX", "KX", "VX", "AFX"))
    b_KG = [Buf("KG%d" % i) for i in range(NAB * 128 // crK)]
    b_VG = [Buf("VG%d" % i) for i in range(T // crV)]
    b_AFG = [Buf("AFG%d" % i) for i in range(T // crA)]
    b_GS = [Buf("GS%d" % i) for i in range(2 * DC)]
    b_HX, b_HG, b_MB, b_out = Buf("HX"), [Buf("HG")], [Buf("MB%d" % h) for h in range(H)], [Buf("out%d" % i) for i in range(DC)]

    def sb(name, shape, dt):
        return nc.alloc_sbuf_tensor("s_" + name, shape, dt)
    R1 = sb("R1", [128, DC, T], BF16)
    b_R1 = [Buf("R1_%d" % i) for i in range(DC)]
    R2 = sb("R2", [128, 16, T], BF16)
    b_R2 = [Buf("R2_%d" % i) for i in range(16)]
    NSLOT = 5
    SLOTE = 4096
    slots = [sb("slot%d" % i, [128, SLOTE], BF16) for i in range(NSLOT)]
    b_slots = [Buf("slot%d" % i) for i in range(NSLOT)]
    slot_ctr = [0]

    def next_slot():
        i = slot_ctr[0] % NSLOT
        slot_ctr[0] += 1
        return slots[i], b_slots[i]
    params = sb("params", [128, NP], F32)
    b_params = Buf("params")
    ones_bf = sb("ones", [128, 128], BF16)
    b_ones = Buf("ones")
    csc = sb("csc", [128, GCN, 2 * GC], BF16)
    b_csc = Buf("csc")
    NXS = 3
    XW = sb("XW", [128, NXS * T], F32)
    xst = [XW[:, i * T:(i + 1) * T] for i in range(NXS)]
    b_xst = [Buf("xst%d" % i) for i in range(NXS)]
    xctr = [0]

    def next_xst():
        i = xctr[0] % NXS
        xctr[0] += 1
        return xst[i], b_xst[i]
    NST = 4
    st16 = [sb("st16_%d" % i, [128, T], BF16) for i in range(NST)]
    b_st16 = [Buf("st16_%d" % i) for i in range(NST)]
    sctr = [0]

    def next_st16():
        i = sctr[0] % NST
        sctr[0] += 1
        return st16[i], b_st16[i]
    gt = [sb("gt%d" % i, [128, T], F32) for i in range(6)]
    b_gt = [Buf("gt%d" % i) for i in range(6)]
    rstd = gt[5]
    b_rstd = b_gt[5]
    Wb = XW
    assert LB == NXS * T
    lamt = sb("lamt", [128, 512], F32)
    b_lamt = Buf("lamt")
    lsc = sb("lsc", [128, 8], F32)
    b_lsc = Buf("lsc")
    ltmp = sb("ltmp", [128, 128], F32)
    b_ltmp = Buf("ltmp")
    tmpS = [gt[4], gt[5]]
    b_tmpS = [b_gt[4], b_gt[5]]
    Pt = [sb("Pt%d" % i, [128, TB], BF16) for i in range(3)]
    b_Pt = [Buf("Pt%d" % i) for i in range(3)]
    rs = sb("rs", [128, TB], F32)
    b_rs = Buf("rs")
    on = [[gt[m * 2 + j] for j in range(2)] for m in range(2)]
    b_on = [[b_gt[m * 2 + j] for j in range(2)] for m in range(2)]
    sq16 = [sb("sq16_%d" % i, [128, TB], BF16) for i in range(2)]
    b_sq16 = [Buf("sq16_%d" % i) for i in range(2)]
    tk16 = [sb("tk16_%d" % i, [128, 512], BF16) for i in range(3)]
    b_tk16 = [Buf("tk16_%d" % i) for i in range(3)]
    tkc = [0]

    def next_tk():
        i = tkc[0] % 3
        tkc[0] += 1
        return tk16[i], b_tk16[i]
    assert NFB <= 16
    ufT = R2
    b_ufT = b_R2
    yf32 = gt[0:4]
    b_yf32 = b_gt[0:4]
    gsb = sb("gsb", [128, T + 2], F32)
    b_gsb = Buf("gsb")
    cva, b_cva = gt[4], b_gt[4]
    cvb, b_cvb = gt[5], b_gt[5]
    gg = st16
    b_gg = b_st16
    hho = sb("hho", [128, 2, DC], BF16)
    b_hho = Buf("hho")
    hh2 = sb("hh2", [128, 2, DC], BF16)
    b_hh2 = Buf("hh2")
    ghal = sb("ghal", [128, FB, 2], F32)
    b_ghal = Buf("ghal")

    banks = [nc.alloc_psum_tensor("bank%d" % i, [128, 512], F32) for i in range(8)]
    b_banks = [Buf("bank%d" % i) for i in range(8)]

    def pcol(nm, off, n=1):
        return params[:, pc[nm] + off: pc[nm] + off + n]

    evac_rr = [0]

    def evac_eng():
        evac_rr[0] += 1
        return "act" if evac_rr[0] % 2 == 0 else "dve"

    def copy_op(eng, out, in_, r, w, scale=None):
        if eng == "act":
            if scale is None:
                S.op("act", lambda E: E.activation(out, in_, ACTF.Copy), r=r, w=w)
            else:
                S.op("act", lambda E: E.activation(out, in_, ACTF.Copy, scale=scale), r=r, w=w)
        else:
            if scale is None:
                S.op("dve", lambda E: E.tensor_copy(out, in_), r=r, w=w)
            else:
                S.op("dve", lambda E: E.tensor_scalar(out, in_, scale, None, ALU.mult), r=r, w=w)

    def slab_src(l, nm, ng, kg):
        K, N = c.mats[nm]
        kc = c.kc(K)
        KG = K // 128 // kc
        srows = kc * 32
        r0 = (ng * KG + kg) * srows
        t = wfull[(l, nm)]
        return bass.AP(t, r0 * 2048, [(kc * 512, 128), (1, kc * 512)]), kc

    def load_slab(l, nm, ng, kg):
        src, kc = slab_src(l, nm, ng, kg)
        sl, bsl = next_slot()
        dst = sl[:, 0:kc * 512]
        S.dma("sp", lambda E: E.dma_start(out=dst, in_=src), bsl, r=b_wfull[(l, nm)], w=[bsl])
        return sl, bsl, kc

    def proj_group(l, nm, ng, kgs, rhs_fn, rbufs_fn, split=False):
        K, N = c.mats[nm]
        assert len(kgs) <= NSLOT - 1
        loaded = [load_slab(l, nm, ng, kg) for kg in kgs] if split else None
        order = [(half, kgi) for half in range(2) for kgi in range(len(kgs))] if split else [(None, kgi) for kgi in range(len(kgs))]
        for half, kgi in order:
            kg = kgs[kgi]
            sl, bsl, kc = loaded[kgi] if split else load_slab(l, nm, ng, kg)
            slv = sl[:, 0:kc * 512].rearrange("p (c n) -> p c n", c=kc)
            for b in (range(4) if half is None else (2 * half, 2 * half + 1)):
                for tb in range(NTB):
                    bk = banks[b * NTB + tb]
                    bb = b_banks[b * NTB + tb]

                    def fn(E, b=b, tb=tb, bk=bk, slv=slv, kc=kc, kg=kg, kgi=kgi):
                        last = None
                        for cc in range(kc):
                            cg = kg * kc + cc
                            last = E.matmul(bk[:, 0:TB], lhsT=slv[:, cc, b * 128:(b + 1) * 128], rhs=rhs_fn(cg, tb),
                                            start=(kgi == 0 and cc == 0), stop=(kgi == len(kgs) - 1 and cc == kc - 1))
                        return last
                    rb = [bsl]
                    for cc in range(kc):
                        rb += rbufs_fn(kg * kc + cc)
                    S.op("pe", fn, r=rb, w=[bb])

    S.dma("sp", lambda E: E.dma_start(out=params[:, :], in_=params_d[:, :]), b_params, w=[b_params])
    S.dma("sp", lambda E: E.dma_start(out=csc[:, :, :], in_=csc_d[:, :].rearrange("p (c n) -> p c n", c=GCN)), b_csc, w=[b_csc])
    S.op("dve", lambda E: E.memset(ones_bf[:, :], 1.0), w=[b_ones])
    S.dma("sp", lambda E: E.dma_start(out=XS[:, :], in_=xT_in[:, :]), b_XS[0], w=b_XS)
    for h in range(H):
        S.dma("sp", lambda E, h=h: E.dma_start(out=Wb[:, :], in_=tr_d[h:h + 1, :].partition_broadcast(128)), b_xst[0], w=b_xst)
        S.dma("sp", lambda E, h=h: E.dma_start(out=MB[h * 128:(h + 1) * 128, :], in_=Wb[:, :]), b_xst[0], r=b_xst, w=[b_MB[h]])

    def chunked_ag(src, dst, rows, cr, groups, nr, rbufs, wbufs):
        assert rows % cr == 0 and len(wbufs) == rows // cr
        for i in range(rows // cr):
            S.cc(lambda E, i=i: E.collective_compute("AllGather", ALU.bypass, replica_groups=groups,
                                                     ins=[src[i * cr:(i + 1) * cr, :]], outs=[dst[i * nr * cr:(i + 1) * nr * cr, :]]),
                 r=rbufs, w=[wbufs[i]])

    castsem = [Buf("castsem%d" % i) for i in range(2)]
    castctr = [0]

    def prologue(l, names):
        for nm in names:
            rr = c.rows(nm)
            step = 1024
            pieces = []
            for r0 in range(0, rr, step):
                r1 = min(rr, r0 + step)
                bpiece = Buf("wshp")
                pieces.append(bpiece)
                cs_ = castsem[castctr[0] % 2]
                castctr[0] += 1
                S.dma("pool", lambda E, nm=nm, r0=r0, r1=r1: E.dma_start(
                    out=wsh[(l, nm)][r0:r1, :], in_=win[nm][l * c.rows(nm) + r0: l * c.rows(nm) + r1, :]),
                    cs_, w=[bpiece])
            if AG_MODE == "8":
                b_wfull[(l, nm)] = [Buf("wf")]
                S.cc(lambda E, nm=nm: E.collective_compute("AllGather", ALU.bypass, replica_groups=[list(range(8))],
                                                           ins=[wsh[(l, nm)].ap().opt()], outs=[wfull[(l, nm)].ap().opt()]),
                     r=pieces, w=b_wfull[(l, nm)])
            else:
                _, cra, crb = c.wchunks(nm)
                bmid = [Buf("wmid") for _ in range(rr // cra)]
                chunked_ag(wsh[(l, nm)], wmid[(l, nm)], rr, cra, [[0, 4], [1, 5], [2, 6], [3, 7]], 2, pieces, bmid)
                b_wfull[(l, nm)] = [Buf("wf") for _ in range(2 * rr // crb)]
                chunked_ag(wmid[(l, nm)], wfull[(l, nm)], 2 * rr, crb, [[0, 1, 2, 3], [4, 5, 6, 7]], 4, bmid, b_wfull[(l, nm)])

    PAIRS = [[0, 1], [2, 3], [4, 5], [6, 7]]

    def pair_ag(src, dst, bsrc, bdst, rows, cr):
        chunked_ag(src, dst, rows, cr, PAIRS, 2, [bsrc], bdst)

    def rmsnorm(gain_col, dst_ap_fn, dst_buf_fn, to_dram=None):
        for ci in range(DC):
            xt, bxt = next_xst()
            S.dma("sp", lambda E, ci=ci, xt=xt: E.dma_start(out=xt[:, :], in_=XS[ci * 128:(ci + 1) * 128, :]), bxt,
                  r=[b_XS[ci]], w=[bxt])
            sq, bsq = next_st16()
            S.op("act", lambda E, xt=xt, sq=sq: E.activation(sq[:, :], xt[:, :], ACTF.Square), r=[bxt], w=[bsq])
            for tb in range(NTB):
                S.op("pe", lambda E, sq=sq, tb=tb, ci=ci: E.matmul(banks[tb][:, 0:TB], lhsT=ones_bf[:, :], rhs=sq[:, tb * TB:(tb + 1) * TB],
                                                                start=(ci == 0), stop=(ci == DC - 1)),
                     r=[bsq, b_ones], w=[b_banks[tb]])
        for tb in range(NTB):
            S.op("act", lambda E, tb=tb: E.activation(rstd[:, tb * TB:(tb + 1) * TB], banks[tb][:, 0:TB], ACTF.Sqrt,
                                                      bias=eps_t[:, 0:1], scale=1.0 / D), r=[b_banks[tb], b_eps], w=[b_rstd])
        S.op("dve", lambda E: E.reciprocal(rstd[:, :], rstd[:, :]), r=[b_rstd], w=[b_rstd])
        for ci in range(DC):
            xt, bxt = next_xst()
            S.dma("sp", lambda E, ci=ci, xt=xt: E.dma_start(out=xt[:, :], in_=XS[ci * 128:(ci + 1) * 128, :]), bxt,
                  r=[b_XS[ci]], w=[bxt])
            if to_dram is None:
                S.op("dve", lambda E, ci=ci, xt=xt: E.scalar_tensor_tensor(dst_ap_fn(ci), xt[:, :], gain_col(ci), rstd[:, :],
                                                                         ALU.mult, ALU.mult),
                     r=[bxt, b_rstd, b_params], w=[dst_buf_fn(ci)])
            else:
                S.op("dve", lambda E, ci=ci, xt=xt: E.scalar_tensor_tensor(xt[:, :], xt[:, :], gain_col(ci), rstd[:, :],
                                                                         ALU.mult, ALU.mult),
                     r=[bxt, b_rstd, b_params], w=[bxt])
                S.dma("act", lambda E, ci=ci, xt=xt: E.dma_start(out=to_dram[ci * 128:(ci + 1) * 128, :], in_=xt[:, :]), bxt,
                      r=[bxt], w=[b_out[ci]])

    eps_t = sb("eps_t", [128, 1], F32)
    b_eps = Buf("eps")
    S.op("dve", lambda E: E.memset(eps_t[:, :], EPS), w=[b_eps])

    def x_rmw(j, bank_ids):
        xt, bxt = next_xst()
        S.dma("sp", lambda E, xt=xt: E.dma_start(out=xt[:, :], in_=XS[j * 128:(j + 1) * 128, :]), bxt, r=[b_XS[j]], w=[bxt])
        for tb in range(NTB):
            bi = bank_ids[tb]
            S.op("dve", lambda E, xt=xt, tb=tb, bi=bi: E.tensor_tensor(xt[:, tb * TB:(tb + 1) * TB], xt[:, tb * TB:(tb + 1) * TB],
                                                                     banks[bi][:, 0:TB], ALU.add), r=[bxt, b_banks[bi]], w=[bxt])
        S.dma("act", lambda E, xt=xt: E.dma_start(out=XS[j * 128:(j + 1) * 128, :], in_=xt[:, :]), bxt, r=[bxt], w=[b_XS[j]])

    prologue(0, ["w_in", "w_gate"])

    for l in range(L):
        S.epoch = l
        lam_init = 0.8 - 0.6 * math.exp(-0.3 * l)
        S.dma("sp", lambda E, l=l: E.dma_start(out=lamt[:, :], in_=lamb_d[:, l * 512:(l + 1) * 512]), b_lamt, w=[b_lamt])
        for i in range(2):
            S.op("dve", lambda E, i=i: E.tensor_tensor(ltmp[:, :], lamt[:, (2 * i) * 128:(2 * i + 1) * 128],
                                                     lamt[:, (2 * i + 1) * 128:(2 * i + 2) * 128], ALU.mult), r=[b_lamt], w=[b_ltmp])
            S.op("dve", lambda E, i=i: E.tensor_reduce(lsc[:, i:i + 1], ltmp[:, :], AX.X, ALU.add), r=[b_ltmp], w=[b_lsc])
        S.op("act", lambda E: E.activation(lsc[:, 2:4], lsc[:, 0:2], ACTF.Exp), r=[b_lsc], w=[b_lsc])
        S.op("dve", lambda E, li=lam_init: E.scalar_tensor_tensor(lsc[:, 4:5], lsc[:, 3:4], -li, lsc[:, 2:3], ALU.add, ALU.subtract),
             r=[b_lsc], w=[b_lsc])
        S.op("dve", lambda E, l=l, li=lam_init: E.tensor_scalar(lsc[:, 5:7], pcol("sg", l * 2, 2), 1.0 - li, None, ALU.mult),
             r=[b_params], w=[b_lsc])

        rmsnorm(lambda ci, l=l: pcol("g1", l * DC + ci), lambda ci: R1[:, ci, :], lambda ci: b_R1[ci])

        hr = lambda cg, tb: R1[:, cg, tb * TB:(tb + 1) * TB]
        hb = lambda cg: [b_R1[cg]]
        KGin = DC // c.kc(D)
        allkg = list(range(KGin))
        n_uf, n_q = NF // 512, DA // 512
        for ng in range(DIN // 512):
            if ng < n_uf + 2 * n_q:
                proj_group(l, "w_in", ng, allkg, hr, hb)
                for b in range(4):
                    if ng < n_uf:
                        fbk = ng * 4 + b
                        for tb in range(NTB):
                            bi = b * NTB + tb
                            copy_op(evac_eng(), ufT[:, fbk, tb * TB:(tb + 1) * TB], banks[bi][:, 0:TB], [b_banks[bi]], [b_ufT[fbk]])
                    else:
                        isq = ng < n_uf + n_q
                        blk = (ng - n_uf - (0 if isq else n_q)) * 4 + b
                        st, bst = next_st16()
                        for tb in range(NTB):
                            bi = b * NTB + tb
                            copy_op(evac_eng(), st[:, tb * TB:(tb + 1) * TB], banks[bi][:, 0:TB], [b_banks[bi]], [bst],
                                    scale=(128 ** -0.5) if isq else None)
                        dstT = QX if isq else KX
                        S.dma("act", lambda E, st=st, blk=blk, dstT=dstT: E.dma_start(out=dstT[blk * 128:(blk + 1) * 128, :], in_=st[:, :]),
                              bst, r=[bst], w=[b_QX if isq else b_KX])
                if ng == n_uf - 1:
                    for tt in range(TT):
                        for g in range(4):
                            bi = (tt * 4 + g) % 8

                            def fn(E, tt=tt, g=g, bi=bi):
                                last = None
                                for cc in range(GCN):
                                    last = E.matmul(banks[bi][:, 0:2 * GC], lhsT=ufT[:, g * GCN + cc, tt * 128:(tt + 1) * 128],
                                                    rhs=csc[:, cc, :], start=(cc == 0), stop=(cc == GCN - 1))
                                return last
                            S.op("pe", fn, r=[b_ufT[g * GCN + cc] for cc in range(GCN)] + [b_csc], w=[b_banks[bi]])
                            tk, btk = next_tk()
                            copy_op(evac_eng(), tk[:, 0:2 * GC], banks[bi][:, 0:2 * GC], [b_banks[bi]], [btk])
                            S.dma("act", lambda E, tk=tk, tt=tt, g=g: E.dma_start(
                                out=AFX[tt * 128:(tt + 1) * 128, g * 2 * GC:(g + 1) * 2 * GC], in_=tk[:, 0:2 * GC]), btk,
                                r=[btk], w=[b_AFX])
                    pair_ag(AFX, AFG, b_AFX, b_AFG, T, crA)
                if ng == n_uf + 2 * n_q - 1:
                    pair_ag(KX, KG, b_KX, b_KG, NAB * 128, crK)
            else:
                gv = ng - n_uf - 2 * n_q
                for kgi, kg in enumerate(allkg):
                    sl, bsl, kc = load_slab(l, "w_in", ng, kg)
                    slv = sl[:, 0:kc * 512].rearrange("p (c n) -> p c n", c=kc)
                    for tt in range(TT):
                        def fn(E, tt=tt, slv=slv, kc=kc, kg=kg, kgi=kgi):
                            last = None
                            for cc in range(kc):
                                cg = kg * kc + cc
                                last = E.matmul(banks[tt][:, :], lhsT=R1[:, cg, tt * 128:(tt + 1) * 128], rhs=slv[:, cc, :],
                                                start=(kgi == 0 and cc == 0), stop=(kgi == len(allkg) - 1 and cc == kc - 1))
                            return last
                        S.op("pe", fn, r=[bsl] + [b_R1[kg * kc + cc] for cc in range(kc)], w=[b_banks[tt]])
                for tt in range(TT):
                    tk, btk = next_tk()
                    copy_op(evac_eng(), tk[:, :], banks[tt][:, :], [b_banks[tt]], [btk])
                    S.dma("act", lambda E, tk=tk, tt=tt, gv=gv: E.dma_start(out=VX[tt * 128:(tt + 1) * 128, gv * 512:(gv + 1) * 512], in_=tk[:, :]),
                          btk, r=[btk], w=[b_VX])
        pair_ag(VX, VG, b_VX, b_VG, T, crV)
        prologue(l, ["w_fo", "w_ao", "w_o", "w_up", "w_down"] if l == 0 else ["w_up", "w_down"])

        for ng in range(2 * D // 512):
            proj_group(l, "w_gate", ng, allkg, hr, hb)
            for b in range(4):
                blk = ng * 4 + b
                st, bst = next_st16()
                for tb in range(NTB):
                    bi = b * NTB + tb
                    S.op("act", lambda E, st=st, tb=tb, bi=bi, blk=blk, l=l: E.activation(
                        st[:, tb * TB:(tb + 1) * TB], banks[bi][:, 0:TB], ACTF.Sigmoid, bias=pcol("bg", l * 2 * DC + blk)),
                        r=[b_banks[bi], b_params], w=[bst])
                S.dma("act", lambda E, st=st, blk=blk: E.dma_start(out=GS[blk * 128:(blk + 1) * 128, :], in_=st[:, :]), bst,
                      r=[bst], w=[b_GS[blk]])


        for sb_ in range(NTB):
            for sc in range(KCH):
                sl, bsl = next_slot()
                afv = sl[:, 0:2 * NF]
                arow = ag_row((sc % TT) * 128, sc // TT, crA, 2)
                S.dma("sp", lambda E, afv=afv, arow=arow: E.dma_start(out=afv, in_=AFG[arow:arow + 128, :]), bsl, r=b_AFG, w=[bsl])
                tl, btl = next_slot()
                tv = tl[:, 0:2 * TB].rearrange("p (a n) -> p a n", a=2)
                src = bass.AP(ct_d, sc * 128 * T + sb_ * TB, [(T, 128), (2 * T * T, 2), (1, TB)])
                S.dma("sp", lambda E, tv=tv, src=src: E.dma_start(out=tv, in_=src), btl, w=[btl])
                for fb in range(NFB):
                    g, j = fb // GCN, fb % GCN
                    c0 = g * 2 * GC + j * 128

                    def fn(E, fb=fb, c0=c0, afv=afv, tv=tv, sc=sc):
                        E.matmul(banks[fb][:, 0:TB], lhsT=afv[:, c0:c0 + 128], rhs=tv[:, 0, :], start=(sc == 0), stop=False)
                        return E.matmul(banks[fb][:, 0:TB], lhsT=afv[:, c0 + GC:c0 + GC + 128], rhs=tv[:, 1, :], start=False,
                                        stop=(sc == KCH - 1))
                    S.op("pe", fn, r=[bsl, btl], w=[b_banks[fb]])
            for fb in range(NFB):
                copy_op(evac_eng(), R1[:, NAB + fb, sb_ * TB:(sb_ + 1) * TB], banks[fb][:, 0:TB], [b_banks[fb]], [b_R1[NAB + fb]])

        for h in range(H):
            S.dma("sp", lambda E, h=h: E.dma_start(out=Wb[:, 0:WL], in_=bass.AP(MB, h * 128 * LB + 127, [(LB - 1, 128), (1, WL)])),
                  b_xst[0], r=[b_MB[h]], w=b_xst)
            ksl, bks = next_slot()
            kv = ksl[:, 0:2 * 2 * T].rearrange("p (m r t) -> p m r t", m=2, r=2)
            for m in range(2):
                krow = ag_row((2 * h + m) * 128, 0, crK, 2)
                src = bass.AP(KG, krow * T, [(T, 128), (crK * T, 2), (1, T)])
                S.dma("sp", lambda E, m=m, src=src, kv=kv: E.dma_start(out=kv[:, m, :, :], in_=src), bks, r=b_KG, w=[bks])
            kfl = ksl[:, 0:2 * 2 * T].rearrange("p (m k) -> p m k", m=2)
            vsl, bvs = next_slot()
            vv = vsl[:, 0:KCH * 256].rearrange("p (k e) -> p k e", k=KCH)
            nvc = crV // 128

            def vload(E, h=h, vv=vv):
                res = []
                for r_ in range(2):
                    for i_ in range(T // crV):
                        vrow = ag_row(i_ * crV, r_, crV, 2)
                        k0 = r_ * TT + i_ * nvc
                        res.append(E.dma_start(out=vv[:, k0:k0 + nvc, :],
                                               in_=bass.AP(VG, vrow * DA + h * 256, [(DA, 128), (128 * DA, nvc), (1, 256)])))
                return res
            S.dma("sp", vload, bvs, r=b_VG, w=[bvs], n=2 * (T // crV))
            qsl, bqs = next_slot()
            qv = qsl[:, 0:2 * T].rearrange("p (m t) -> p m t", m=2)
            S.dma("sp", lambda E, h=h, qv=qv: E.dma_start(out=qv, in_=bass.AP(QX, 2 * h * 128 * T, [(T, 128), (128 * T, 2), (1, T)])),
                  bqs, r=[b_QX], w=[bqs])
            for qb in range(NTB):
                for m in range(2):
                    acc = [m * 3 + 0, m * 3 + 1, m * 3 + 2]
                    pend = None
                    for kc_ in range(KCH + 1):
                        if kc_ < KCH:
                            sbk = 6 + (kc_ % 2)
                            S.op("pe", lambda E, m=m, kc_=kc_, sbk=sbk, qb=qb, kfl=kfl, qv=qv: E.matmul(
                                banks[sbk][:, 0:TB], lhsT=kfl[:, m, kc_ * 128:(kc_ + 1) * 128], rhs=qv[:, m, qb * TB:(qb + 1) * TB],
                                start=True, stop=True), r=[bks, bqs], w=[b_banks[sbk]])
                            ti = kc_ % 2
                            jb = 2 * T - kc_ * 128 + qb * TB - 128
                            S.op("dve", lambda E, ti=ti, sbk=sbk, jb=jb: E.tensor_tensor(tmpS[ti][:, 0:TB], banks[sbk][:, 0:TB],
                                                                                        Wb[:, jb:jb + TB], ALU.add),
                                 r=[b_banks[sbk]] + b_xst, w=[b_tmpS[ti]])
                            pi = kc_ % 3
                            S.op("act", lambda E, ti=ti, pi=pi: E.activation(Pt[pi][:, :], tmpS[ti][:, 0:TB], ACTF.Exp),
                                 r=[b_tmpS[ti]], w=[b_Pt[pi]])
                        if pend is not None:
                            pk, ppi = pend

                            def fn(E, pk=pk, ppi=ppi, acc=acc, vv=vv):
                                E.matmul(banks[acc[0]][:, 0:TB], lhsT=vv[:, pk, 0:128], rhs=Pt[ppi][:, :], start=(pk == 0), stop=(pk == KCH - 1))
                                E.matmul(banks[acc[1]][:, 0:TB], lhsT=vv[:, pk, 128:256], rhs=Pt[ppi][:, :], start=(pk == 0), stop=(pk == KCH - 1))
                                return E.matmul(banks[acc[2]][:, 0:TB], lhsT=ones_bf[:, :], rhs=Pt[ppi][:, :], start=(pk == 0), stop=(pk == KCH - 1))
                            S.op("pe", fn, r=[bvs, b_Pt[ppi], b_ones], w=[b_banks[a] for a in acc])
                        pend = (kc_, kc_ % 3) if kc_ < KCH else None
                    S.op("act", lambda E, acc=acc: E.activation(rs[:, :], banks[acc[2]][:, 0:TB], ACTF.Ln), r=[b_banks[acc[2]]], w=[b_rs])
                    S.op("act", lambda E: E.activation(rs[:, :], rs[:, :], ACTF.Exp, scale=-1.0), r=[b_rs], w=[b_rs])
                    for j in range(2):
                        S.op("dve", lambda E, m=m, j=j, acc=acc: E.tensor_tensor(on[m][j][:, 0:TB], banks[acc[j]][:, 0:TB], rs[:, :], ALU.mult),
                             r=[b_banks[acc[j]], b_rs], w=[b_on[m][j]])
                for j in range(2):
                    S.op("dve", lambda E, j=j: E.scalar_tensor_tensor(on[0][j][:, 0:TB], on[1][j][:, 0:TB], lsc[:, 4:5], on[0][j][:, 0:TB],
                                                                    ALU.mult, ALU.add), r=[b_on[1][j], b_on[0][j], b_lsc], w=[b_on[0][j]])
                    S.op("act", lambda E, j=j: E.activation(sq16[j][:, :], on[0][j][:, 0:TB], ACTF.Square), r=[b_on[0][j]], w=[b_sq16[j]])
                    S.op("pe", lambda E, j=j: E.matmul(banks[2][:, 0:TB], lhsT=ones_bf[:, :], rhs=sq16[j][:, :], start=(j == 0), stop=(j == 1)),
                         r=[b_sq16[j], b_ones], w=[b_banks[2]])
                S.op("act", lambda E: E.activation(rs[:, :], banks[2][:, 0:TB], ACTF.Ln, bias=eps_t[:, 0:1], scale=1.0 / 256),
                     r=[b_banks[2], b_eps], w=[b_rs])
                S.op("act", lambda E: E.activation(rs[:, :], rs[:, :], ACTF.Exp, scale=-0.5), r=[b_rs], w=[b_rs])
                for j in range(2):
                    S.op("dve", lambda E, j=j, h=h, qb=qb: E.scalar_tensor_tensor(R1[:, 2 * h + j, qb * TB:(qb + 1) * TB], on[0][j][:, 0:TB],
                                                                               lsc[:, 5 + j:6 + j], rs[:, :], ALU.mult, ALU.mult),
                         r=[b_on[0][j], b_rs, b_lsc], w=[b_R1[2 * h + j]])

        KCo = c.kc(D)
        KGo = DC // KCo
        hk = DC // 2
        for kh in range(2):
            for jg in range(hk // 4):
                ng = kh * (hk // 4) + jg
                fr = lambda cg, tb: R1[:, NAB + cg, tb * TB:(tb + 1) * TB]
                fbuf = lambda cg: [b_R1[NAB + cg]]
                proj_group(l, "w_fo", ng, list(range(NFB // c.kc(NF))), fr, fbuf, split=True)
                for b in range(4):
                    j = ng * 4 + b
                    st, bst = next_st16()
                    S.dma("sp", lambda E, st=st, j=j: E.dma_start(out=st[:, :], in_=GS[j * 128:(j + 1) * 128, :]), bst, r=[b_GS[j]], w=[bst])
                    for tb in range(NTB):
                        bi = b * NTB + tb
                        S.op("dve", lambda E, b=b, tb=tb, bi=bi, st=st: E.tensor_tensor(yf32[b][:, tb * TB:(tb + 1) * TB], banks[bi][:, 0:TB],
                                                                                      st[:, tb * TB:(tb + 1) * TB], ALU.mult),
                             r=[b_banks[bi], bst], w=[b_yf32[b]])
                ar = lambda cg, tb: R1[:, cg, tb * TB:(tb + 1) * TB]
                abuf = lambda cg: [b_R1[cg]]
                proj_group(l, "w_ao", ng, list(range(NAB // c.kc(DA))), ar, abuf, split=True)
                for b in range(4):
                    j = ng * 4 + b
                    st, bst = next_st16()
                    S.dma("sp", lambda E, st=st, j=j: E.dma_start(out=st[:, :], in_=GS[(DC + j) * 128:(DC + j + 1) * 128, :]), bst,
                          r=[b_GS[DC + j]], w=[bst])
                    for tb in range(NTB):
                        bi = b * NTB + tb
                        S.op("dve", lambda E, tb=tb, bi=bi, st=st: E.tensor_tensor(cva[:, tb * TB:(tb + 1) * TB], banks[bi][:, 0:TB],
                                                                                 st[:, tb * TB:(tb + 1) * TB], ALU.mult),
                             r=[b_banks[bi], bst], w=[b_cva])
                        S.op("dve", lambda E, b=b, tb=tb, jg=jg: E.tensor_tensor(R2[:, jg * 4 + b, tb * TB:(tb + 1) * TB], cva[:, tb * TB:(tb + 1) * TB],
                                                                               yf32[b][:, tb * TB:(tb + 1) * TB], ALU.add),
                             r=[b_cva, b_yf32[b]], w=[b_R2[jg * 4 + b]])
            mr = lambda cg, tb, kh=kh: R2[:, cg - kh * hk, tb * TB:(tb + 1) * TB]
            mbuf = lambda cg, kh=kh: [b_R2[cg - kh * hk]]
            kgs = [kh * (hk // KCo) + i for i in range(hk // KCo)]
            for og in range(D // 512):
                proj_group(l, "w_o", og, kgs, mr, mbuf, split=True)
                for b in range(4):
                    x_rmw(og * 4 + b, [b * NTB + tb for tb in range(NTB)])

        rmsnorm(lambda ci, l=l: pcol("g2", l * DC + ci), lambda ci: R1[:, ci, :], lambda ci: b_R1[ci])
        S.op("dve", lambda E: E.tensor_copy(hho[:, 0, :], R1[:, :, 0]), r=b_R1, w=[b_hho])
        S.op("dve", lambda E: E.tensor_copy(hho[:, 1, :], R1[:, :, T - 1]), r=b_R1, w=[b_hho])
        S.dma("act", lambda E: E.dma_start(out=HX[:, :], in_=hho[:, :, :].rearrange("p a c -> p (a c)")), b_hho, r=[b_hho], w=[b_HX])
        pair_ag(HX, HG, b_HX, b_HG, 128, 128)
        S.dma("sp", lambda E: [E.dma_start(out=hh2[:, 0, :], in_=HG[0:128, DC:2 * DC]),
                               E.dma_start(out=hh2[:, 1, :], in_=HG[128:256, 0:DC])], b_hh2, r=b_HG, w=[b_hh2], n=2)
        if l + 1 < L:
            prologue(l + 1, ["w_in", "w_gate", "w_fo", "w_ao", "w_o"])

        first_f0 = [True]
        for ng in range(DFF // 512):
            for kgi, kg in enumerate(allkg):
                sl, bsl, kc = load_slab(l, "w_up", ng, kg)
                slv = sl[:, 0:kc * 512].rearrange("p (c n) -> p c n", c=kc)

                def fn(E, ng=ng, kg=kg, kc=kc, slv=slv, first_f0=first_f0):
                    last = None
                    for b in range(4):
                        blk = ng * 4 + b
                        for cc in range(kc):
                            cg = kg * kc + cc
                            st_ = first_f0[0]
                            first_f0[0] = False
                            last = E.matmul(banks[7][:, blk * 2:blk * 2 + 2], lhsT=slv[:, cc, b * 128:(b + 1) * 128], rhs=hh2[:, :, cg],
                                            start=st_, stop=(cg == DC - 1), skip_group_check=True)
                    return last
                S.op("pe", fn, r=[bsl, b_hh2], w=[b_banks[7]])
        bk7 = banks[7][:, 0:FB * 2].rearrange("p (f a) -> p f a", a=2)
        for a in range(2):
            S.op("dve", lambda E, a=a: E.tensor_scalar(ghal[:, :, a], bk7[:, :, a], pcol("mk", a), None, ALU.mult),
                 r=[b_banks[7], b_params], w=[b_ghal])

        KCd = c.kc(DFF)
        for part in range(c.NPART):
            for fg in range(4):
                ngg = part * 4 + fg
                proj_group(l, "w_up", ngg, allkg, hr, hb)
                for b in range(4):
                    fblk = ngg * 4 + b
                    for tb in range(NTB):
                        bi = b * NTB + tb
                        S.op("act", lambda E, tb=tb, bi=bi: E.activation(gsb[:, 1 + tb * TB:1 + (tb + 1) * TB], banks[bi][:, 0:TB], ACTF.Copy),
                             r=[b_banks[bi]], w=[b_gsb])
                    S.op("dve", lambda E, fblk=fblk: E.tensor_copy(gsb[:, 0:1], ghal[:, fblk, 0:1]), r=[b_ghal], w=[b_gsb])
                    S.op("dve", lambda E, fblk=fblk: E.tensor_copy(gsb[:, T + 1:T + 2], ghal[:, fblk, 1:2]), r=[b_ghal], w=[b_gsb])
                    cw = lambda j, fblk=fblk, l=l: pcol("cw", (l * 3 + j) * FB + fblk)
                    S.op("dve", lambda E, fblk=fblk, l=l, cw=cw: E.tensor_scalar(cva[:, :], gsb[:, 0:T], cw(0), pcol("cb", l * FB + fblk),
                                                                               ALU.mult, ALU.add), r=[b_gsb, b_params], w=[b_cva])
                    S.op("dve", lambda E, cw=cw: E.scalar_tensor_tensor(cvb[:, :], gsb[:, 1:T + 1], cw(1), cva[:, :], ALU.mult, ALU.add),
                         r=[b_gsb, b_cva, b_params], w=[b_cvb])
                    S.op("dve", lambda E, cw=cw: E.scalar_tensor_tensor(cva[:, :], gsb[:, 2:T + 2], cw(2), cvb[:, :], ALU.mult, ALU.add),
                         r=[b_gsb, b_cvb, b_params], w=[b_cva])
                    S.op("act", lambda E, b=b: E.activation(gg[b][:, :], cva[:, :], ACTF.Gelu), r=[b_cva], w=[b_gg[b]])
                ngv = DFF // 512 + ngg
                proj_group(l, "w_up", ngv, allkg, hr, hb)
                for b in range(4):
                    for tb in range(NTB):
                        bi = b * NTB + tb
                        S.op("dve", lambda E, b=b, tb=tb, bi=bi, fg=fg: E.tensor_tensor(R2[:, fg * 4 + b, tb * TB:(tb + 1) * TB], banks[bi][:, 0:TB],
                                                                                      gg[b][:, tb * TB:(tb + 1) * TB], ALU.mult),
                             r=[b_banks[bi], b_gg[b]], w=[b_R2[fg * 4 + b]])
            ar2 = lambda cg, tb, part=part: R2[:, cg - part * 16, tb * TB:(tb + 1) * TB]
            ab2 = lambda cg, part=part: [b_R2[cg - part * 16]]
            kgs = [part * (16 // KCd) + i for i in range(16 // KCd)]
            for og in range(D // 512):
                proj_group(l, "w_down", og, kgs, ar2, ab2, split=True)
                for b in range(4):
                    x_rmw(og * 4 + b, [b * NTB + tb for tb in range(NTB)])

    S.epoch = L
    rmsnorm(lambda ci: pcol("gf", ci), None, None, to_dram=outT)
    S.op("sp", None, r=b_out + b_XS, w=[])
    S.op("pool", None, r=b_out + b_XS, w=[])
    nw = S.emit()
    return nc, (len(S.ops), nw)


def rel_bucket_np(rel):
    n = -rel
    half = NREL // 2
    ret = (n < 0).astype(np.int32) * half
    n = np.abs(n)
    max_exact = half // 2
    is_small = n < max_exact
    nf = np.maximum(n, 1).astype(np.float32)
    large = max_exact + (np.log(nf / np.float32(max_exact)) / np.float32(math.log(RELMAX / max_exact))
                         * np.float32(half - max_exact)).astype(np.int32)
    large = np.minimum(large, half - 1)
    return ret + np.where(is_small, n, large)


def stream_layout(cfg, W, K, N):
    kc = cfg.kc(K)
    KG = K // 128 // kc
    NG = N // 512
    a = W.reshape(KG, kc, 128, NG, 512).transpose(3, 0, 2, 1, 4)
    return np.ascontiguousarray(a).reshape(8, -1, 2048)


def prep_inputs(cfg, x, norm1_gain, w_in, w_fourier_out, lambdas, subln_gain, rel_bias_table, w_attn_out,
                w_gate, b_gate, w_o, norm2_gain, w_up, conv_w, conv_b, w_down, final_norm_gain):
    c = cfg
    L, D, T, DC, FB = c.L, c.D, c.T, c.DC, c.FB
    f32 = np.float32
    x = np.asarray(x, f32)
    src = {"w_in": w_in, "w_gate": w_gate, "w_fo": w_fourier_out, "w_ao": w_attn_out, "w_o": w_o, "w_up": w_up, "w_down": w_down}
    wsh = {}
    for nm, (K, N) in c.mats.items():
        rr, cra, crb = c.wchunks(nm)
        perm = weight_perm(rr, cra, crb) if AG_MODE != "8" else np.arange(8 * rr)
        per_layer = []
        for l in range(L):
            st = stream_layout(c, np.asarray(src[nm][l], f32), K, N).reshape(8 * rr, 2048)
            sh = np.empty_like(st)
            sh[perm] = st
            per_layer.append(sh.reshape(8, rr, 2048))
        wsh[nm] = np.stack(per_layer, axis=1)
    pc = c.np_cols()

    def fm(a, nblk):
        a = np.asarray(a, f32)
        lead = a.shape[:-1]
        a = a.reshape(*lead, nblk, 128)
        return np.moveaxis(a, -1, 0).reshape(128, -1)
    base = np.zeros((128, pc["_n"]), f32)
    base[:, pc["g1"]:pc["g1"] + L * DC] = fm(norm1_gain, DC)
    base[:, pc["g2"]:pc["g2"] + L * DC] = fm(norm2_gain, DC)
    base[:, pc["gf"]:pc["gf"] + DC] = fm(final_norm_gain, DC)
    base[:, pc["bg"]:pc["bg"] + L * 2 * DC] = fm(b_gate, 2 * DC)
    base[:, pc["cw"]:pc["cw"] + L * 3 * FB] = fm(conv_w, FB)
    base[:, pc["cb"]:pc["cb"] + L * FB] = fm(conv_b, FB)
    base[:, pc["sg"]:pc["sg"] + L * 2] = fm(subln_gain, 2)
    lamb = np.ascontiguousarray(np.broadcast_to(np.asarray(lambdas, f32).reshape(1, L * 512), (128, L * 512)))
    GC, GCN, S_ = c.GC, c.GCN, c.S
    cidx = np.arange(GC)
    ang = 2.0 * np.pi * ((cidx[:, None] * cidx[None, :]) % GC) / GC
    cs = np.concatenate([np.cos(ang), np.sin(ang)], axis=1)
    csc = cs.reshape(GCN, 128, 2 * GC).transpose(1, 0, 2).reshape(128, GCN * 2 * GC).astype(ml_dtypes.bfloat16)
    tbl = np.asarray(rel_bias_table, f32)
    in_maps = []
    for core in range(8):
        b, hf = core // 2, core % 2
        m = {}
        m["xT"] = np.ascontiguousarray(x[b, hf * T:(hf + 1) * T, :].T)
        for nm in c.mats:
            m[nm] = wsh[nm][core].reshape(-1, 2048)
        p = base.copy()
        p[:, pc["mk"]] = 1.0 if hf == 1 else 0.0
        p[:, pc["mk"] + 1] = 1.0 if hf == 0 else 0.0
        m["params"] = p
        m["lamb"] = lamb
        m["csc"] = csc
        s = np.arange(2 * T)
        sp = hf * T + np.arange(T)
        a2 = 2.0 * np.pi * ((s[:, None] * sp[None, :]) % S_) / S_
        nrm = 1.0 / math.sqrt(S_ * GC)
        m["ct"] = np.concatenate([np.cos(a2) * nrm, -np.sin(a2) * nrm], axis=0).astype(ml_dtypes.bfloat16)
        N_ = 3 * T - 1
        i = np.arange(N_)
        rel = i - (T - 1) - hf * T
        TD = tbl[rel_bucket_np(rel)]
        TRv = np.zeros((c.H, c.LB), f32)
        TRv[:, :N_] = TD[::-1, :].T
        m["tr"] = TRv
        in_maps.append(m)
    return in_maps


_CACHE = {}


def run(cfg, inputs, safe_same=True):
    key = (cfg.D, cfg.S, cfg.L, cfg.TB, safe_same)
    if key not in _CACHE:
        _CACHE[key] = build(cfg, safe_same=safe_same)[0]
    nc = _CACHE[key]
    in_maps = prep_inputs(cfg, **inputs)
    res = run_bass_kernel_spmd(nc, in_maps, core_ids=list(range(8)))
    out = np.empty((cfg.B, cfg.S, cfg.D), np.float32)
    for core in range(8):
        b, hf = core // 2, core % 2
        out[b, hf * cfg.T:(hf + 1) * cfg.T, :] = res.results[core]["outT"].T
    return out


def kernel(**inputs):
    return run(Cfg(), inputs)
```

```python
import math
import numpy as np
import ml_dtypes
import concourse.bass as bass
import concourse.mybir as mybir
from concourse.bass_utils import run_bass_kernel_spmd

F32 = mybir.dt.float32
BF16 = mybir.dt.bfloat16
ACTF = mybir.ActivationFunctionType
ALU = mybir.AluOpType
AX = mybir.AxisListType
EPS = 1e-6
AG_MODE = "2stage"
CC_MAX_OUT = 4 * 1024 * 1024


def chunk_rows(rows, rowbytes, nr, mult=1, limit=CC_MAX_OUT):
    best = None
    for cr in range(mult, rows + 1, mult):
        if rows % cr == 0 and cr * rowbytes * nr <= limit:
            best = cr
    assert best is not None, (rows, rowbytes, nr, mult, limit)
    return best


def ag_row(local, r, cr, nr):
    return (local // cr) * nr * cr + r * cr + (local % cr)


def weight_perm(rr, cra, crb):
    shard = [np.arange(rr) + c * rr for c in range(8)]
    mid = []
    for c in range(8):
        a, b = shard[c % 4], shard[c % 4 + 4]
        mid.append(np.concatenate([np.concatenate([a[i * cra:(i + 1) * cra], b[i * cra:(i + 1) * cra]]) for i in range(rr // cra)]))
    full = []
    for c in range(8):
        g = (c // 4) * 4
        full.append(np.concatenate([np.concatenate([mid[g + r][j * crb:(j + 1) * crb] for r in range(4)]) for j in range(2 * rr // crb)]))
    for c in range(1, 8):
        assert np.array_equal(full[c], full[0])
    return full[0]
NREL = 32
RELMAX = 128


class Cfg:
    def __init__(self, D=4096, S=2048, L=4, TB=512, cc_max=CC_MAX_OUT):
        self.D, self.S, self.L, self.TB = D, S, L, TB
        self.cc_max = cc_max
        self.B = 4
        self.T = S // 2
        self.DC = D // 128
        self.NF = D // 4
        self.GC = D // 16
        self.GCN = self.GC // 128
        self.NFB = self.NF // 128
        self.H = (D - self.NF) // 256
        self.DA = self.H * 256
        self.NAB = self.DA // 128
        self.DIN = self.NF + 3 * self.DA
        self.DFF = 2 * D
        self.FB = self.DFF // 128
        self.NTB = self.T // TB
        self.TT = self.T // 128
        self.KCH = 2 * self.T // 128
        self.LB = 3 * self.T
        self.WL = 3 * self.T - 128
        self.NPART = self.FB // 16
        self.mats = {
            "w_in": (D, self.DIN), "w_gate": (D, 2 * D), "w_fo": (self.NF, D), "w_ao": (self.DA, D),
            "w_o": (D, D), "w_up": (D, 2 * self.DFF), "w_down": (self.DFF, D),
        }
        assert self.GC % 128 == 0 and self.T % TB == 0 and self.FB % 16 == 0

    def kc(self, K):
        n = K // 128
        for c in (8, 4, 2, 1):
            if n % c == 0:
                return c

    def rows(self, name):
        K, N = self.mats[name]
        tot = K * N // 2048
        assert tot % 8 == 0
        return tot // 8

    def wchunks(self, name):
        rr = self.rows(name)
        cra = chunk_rows(rr, 4096, 2, limit=self.cc_max)
        crb = chunk_rows(2 * rr, 4096, 4, limit=self.cc_max)
        return rr, cra, crb

    def np_cols(self):
        c = self
        o = {}
        p = 0
        for nm, n in (("g1", c.L * c.DC), ("g2", c.L * c.DC), ("gf", c.DC), ("bg", c.L * 2 * c.DC),
                      ("cw", c.L * 3 * c.FB), ("cb", c.L * c.FB), ("sg", c.L * 2), ("mk", 2)):
            o[nm] = p
            p += n
        o["_n"] = p
        return o


class Buf:
    __slots__ = ("name", "lw", "rd", "rdd", "lastdma", "dcount")

    def __init__(self, name):
        self.name = name
        self.lw = None
        self.rd = {}
        self.rdd = []
        self.lastdma = {}
        self.dcount = {}


class Op:
    __slots__ = ("idx", "eng", "fn", "deps", "kind", "semkey", "val", "sig", "epoch", "n")


class Sched:
    def __init__(self, nc, safe_same=True):
        self.nc = nc
        self.ops = []
        self.epoch = 0
        self.safe_same = safe_same
        self.sems = {}
        self.cccount = 0

    def _add(self, o, r, w):
        deps = {}

        def add(d):
            if d is None:
                return
            if d.kind == "c":
                k = d.eng
                if k not in deps or deps[k].idx < d.idx:
                    deps[k] = d
            else:
                deps[("x", d.idx)] = d
        for b in r:
            add(b.lw)
        for b in w:
            add(b.lw)
            for d in b.rd.values():
                add(d)
            for d in b.rdd:
                add(d)
        return deps

    def _fin(self, o, deps, r, w):
        o.deps = list(deps.values())
        o.idx = len(self.ops)
        o.sig = False
        o.epoch = self.epoch
        for b in r:
            if o.kind == "c":
                b.rd[o.eng] = o
            else:
                b.rdd.append(o)
        for b in w:
            b.lw = o
            b.rd = {}
            b.rdd = []
        self.ops.append(o)
        return o

    def op(self, eng, fn, r=(), w=()):
        o = Op()
        o.eng, o.fn, o.kind = eng, fn, "c"
        deps = self._add(o, r, w)
        o.idx = len(self.ops)
        return self._fin(o, deps, r, w)

    def dma(self, eng, fn, dbuf, r=(), w=(), n=1):
        o = Op()
        o.eng, o.fn, o.kind, o.n = eng, fn, "d", n
        deps = self._add(o, r, w)
        prev = dbuf.lastdma.get(eng)
        if prev is not None:
            deps[("x", prev.idx)] = prev
        dbuf.lastdma[eng] = o
        dbuf.dcount[eng] = dbuf.dcount.get(eng, 0) + 16 * n
        o.semkey = ("d", id(dbuf), eng)
        o.val = dbuf.dcount[eng]
        return self._fin(o, deps, r, w)

    def cc(self, fn, r=(), w=()):
        o = Op()
        o.eng, o.fn, o.kind = "pool", fn, "cc"
        deps = self._add(o, r, w)
        self.cccount += 1
        o.semkey = ("cc",)
        o.val = self.cccount
        return self._fin(o, deps, r, w)

    def sem(self, key):
        if key not in self.sems:
            self.sems[key] = self.nc.alloc_semaphore("sm%d" % len(self.sems))
        return self.sems[key]

    def emit(self):
        nc = self.nc
        engs = {"pe": nc.tensor, "act": nc.scalar, "dve": nc.vector, "pool": nc.gpsimd, "sp": nc.sync}
        for o in self.ops:
            for d in o.deps:
                if d.kind == "c":
                    if d.eng == o.eng and o.kind == "c" and (d.eng == "pe" or not self.safe_same):
                        continue
                    d.sig = True
        cnt = {}
        for o in self.ops:
            if o.kind == "c" and o.sig:
                k = ("c", o.eng, o.epoch)
                cnt[k] = cnt.get(k, 0) + 1
                o.semkey = k
                o.val = cnt[k]
        waited = {}
        nwait = 0
        for o in self.ops:
            E = engs[o.eng]
            for d in sorted(o.deps, key=lambda d: -(d.val if (d.kind != "c" or d.sig) else 0)):
                if d.kind == "c" and not d.sig:
                    continue
                if d.kind == "c" and d.eng == o.eng and o.kind == "c" and (d.eng == "pe" or not self.safe_same):
                    continue
                wk = (o.eng, d.semkey)
                if waited.get(wk, 0) >= d.val:
                    continue
                E.wait_ge(self.sem(d.semkey), d.val)
                waited[wk] = d.val
                nwait += 1
            if o.fn is None:
                continue
            res = o.fn(E)
            if o.kind == "c":
                if o.sig:
                    inst = res[-1] if isinstance(res, (list, tuple)) else res
                    inst.then_inc(self.sem(o.semkey), 1)
            elif o.kind == "d":
                lst = res if isinstance(res, (list, tuple)) else [res]
                assert len(lst) == o.n
                for inst in lst:
                    inst.then_inc(self.sem(o.semkey), 16)
            else:
                res.then_inc(self.sem(o.semkey))
        return nwait


def build(cfg, safe_same=True):
    c = cfg
    D, T, L, TB, DC, NTB, TT = c.D, c.T, c.L, c.TB, c.DC, c.NTB, c.TT
    NF, NFB, GC, GCN, H, DA, NAB, DIN, DFF, FB = c.NF, c.NFB, c.GC, c.GCN, c.H, c.DA, c.NAB, c.DIN, c.DFF, c.FB
    KCH, LB, WL = c.KCH, c.LB, c.WL
    pc = c.np_cols()
    NP = pc["_n"]
    nc = bass.Bass("TRN2", target_bir_lowering=False)
    S = Sched(nc, safe_same=safe_same)

    xT_in = nc.dram_tensor("xT", [D, T], F32, kind="ExternalInput")
    outT = nc.dram_tensor("outT", [D, T], F32, kind="ExternalOutput")
    win = {nm: nc.dram_tensor(nm, [L * c.rows(nm), 2048], F32, kind="ExternalInput") for nm in c.mats}
    params_d = nc.dram_tensor("params", [128, NP], F32, kind="ExternalInput")
    lamb_d = nc.dram_tensor("lamb", [128, L * 512], F32, kind="ExternalInput")
    csc_d = nc.dram_tensor("csc", [128, GCN * 2 * GC], BF16, kind="ExternalInput")
    ct_d = nc.dram_tensor("ct", [2 * 2 * T, T], BF16, kind="ExternalInput")
    tr_d = nc.dram_tensor("tr", [H, LB], F32, kind="ExternalInput")

    XS = nc.dram_tensor("XS", [D, T], F32)
    wsh = {(l, nm): nc.dram_tensor("wsh_%d_%s" % (l, nm), [c.rows(nm), 2048], BF16) for l in range(L) for nm in c.mats}
    wfull = {(l, nm): nc.dram_tensor("wf_%d_%s" % (l, nm), [8 * c.rows(nm), 2048], BF16) for l in range(L) for nm in c.mats}
    wmid = {(l, nm): nc.dram_tensor("wm_%d_%s" % (l, nm), [2 * c.rows(nm), 2048], BF16) for l in range(L) for nm in c.mats}
    QX = nc.dram_tensor("QX", [NAB * 128, T], BF16)
    KX = nc.dram_tensor("KX", [NAB * 128, T], BF16)
    KG = nc.dram_tensor("KG", [2 * NAB * 128, T], BF16)
    VX = nc.dram_tensor("VX", [T, DA], BF16)
    VG = nc.dram_tensor("VG", [2 * T, DA], BF16)
    AFX = nc.dram_tensor("AFX", [T, 2 * NF], BF16)
    AFG = nc.dram_tensor("AFG", [2 * T, 2 * NF], BF16)
    GS = nc.dram_tensor("GS", [2 * DC * 128, T], BF16)
    HX = nc.dram_tensor("HX", [128, 2 * DC], BF16)
    HG = nc.dram_tensor("HG", [256, 2 * DC], BF16)
    MB = nc.dram_tensor("MB", [H * 128, LB], F32)

    b_XS = [Buf("XS%d" % i) for i in range(DC)]
    b_wfull = {}
    crK = chunk_rows(NAB * 128, T * 2, 2, mult=128, limit=c.cc_max)
    crV = chunk_rows(T, DA * 2, 2, mult=128, limit=c.cc_max)
    crA = chunk_rows(T, 2 * NF * 2, 2, mult=128, limit=c.cc_max)
    b_wsh = {k: Buf("wsh%s" % (k,)) for k in wsh}
    b_QX, b_KX, b_VX, b_AFX = (Buf(n) for n in ("QX", "KX", "VX", "AFX"))
    b_KG = [Buf("KG%d" % i) for i in range(NAB * 128 // crK)]
    b_VG = [Buf("VG%d" % i) for i in range(T // crV)]
    b_AFG = [Buf("AFG%d" % i) for i in range(T // crA)]
    b_GS = [Buf("GS%d" % i) for i in range(2 * DC)]
    b_HX, b_HG, b_MB, b_out = Buf("HX"), [Buf("HG")], [Buf("MB%d" % h) for h in range(H)], [Buf("out%d" % i) for i in range(DC)]

    def sb(name, shape, dt):
        return nc.alloc_sbuf_tensor("s_" + name, shape, dt)
    R1 = sb("R1", [128, DC, T], BF16)
    b_R1 = [Buf("R1_%d" % i) for i in range(DC)]
    R2 = sb("R2", [128, 16, T], BF16)
    b_R2 = [Buf("R2_%d" % i) for i in range(16)]
    NSLOT = 5
    SLOTE = 4096
    slots = [sb("slot%d" % i, [128, SLOTE], BF16) for i in range(NSLOT)]
    b_slots = [Buf("slot%d" % i) for i in range(NSLOT)]
    slot_ctr = [0]

    def next_slot():
        i = slot_ctr[0] % NSLOT
        slot_ctr[0] += 1
        return slots[i], b_slots[i]
    params = sb("params", [128, NP], F32)
    b_params = Buf("params")
    ones_bf = sb("ones", [128, 128], BF16)
    b_ones = Buf("ones")
    csc = sb("csc", [128, GCN, 2 * GC], BF16)
    b_csc = Buf("csc")
    NXS = 3
    XW = sb("XW", [128, NXS * T], F32)
    xst = [XW[:, i * T:(i + 1) * T] for i in range(NXS)]
    b_xst = [Buf("xst%d" % i) for i in range(NXS)]
    xctr = [0]

    def next_xst():
        i = xctr[0] % NXS
        xctr[0] += 1
        return xst[i], b_xst[i]
    NST = 4
    st16 = [sb("st16_%d" % i, [128, T], BF16) for i in range(NST)]
    b_st16 = [Buf("st16_%d" % i) for i in range(NST)]
    sctr = [0]

    def next_st16():
        i = sctr[0] % NST
        sctr[0] += 1
        return st16[i], b_st16[i]
    gt = [sb("gt%d" % i, [128, T], F32) for i in range(6)]
    b_gt = [Buf("gt%d" % i) for i in range(6)]
    rstd = gt[5]
    b_rstd = b_gt[5]
    Wb = XW
    assert LB == NXS * T
    lamt = sb("lamt", [128, 512], F32)
    b_lamt = Buf("lamt")
    lsc = sb("lsc", [128, 8], F32)
    b_lsc = Buf("lsc")
    ltmp = sb("ltmp", [128, 128], F32)
    b_ltmp = Buf("ltmp")
    tmpS = [gt[4], gt[5]]
    b_tmpS = [b_gt[4], b_gt[5]]
    Pt = [sb("Pt%d" % i, [128, TB], BF16) for i in range(3)]
    b_Pt = [Buf("Pt%d" % i) for i in range(3)]
    rs = sb("rs", [128, TB], F32)
    b_rs = Buf("rs")
    on = [[gt[m * 2 + j] for j in range(2)] for m in range(2)]
    b_on = [[b_gt[m * 2 + j] for j in range(2)] for m in range(2)]
    sq16 = [sb("sq16_%d" % i, [128, TB], BF16) for i in range(2)]
    b_sq16 = [Buf("sq16_%d" % i) for i in range(2)]
    tk16 = [sb("tk16_%d" % i, [128, 512], BF16) for i in range(3)]
    b_tk16 = [Buf("tk16_%d" % i) for i in range(3)]
    tkc = [0]

    def next_tk():
        i = tkc[0] % 3
        tkc[0] += 1
        return tk16[i], b_tk16[i]
    assert NFB <= 16
    ufT = R2
    b_ufT = b_R2
    yf32 = gt[0:4]
    b_yf32 = b_gt[0:4]
    gsb = sb("gsb", [128, T + 2], F32)
    b_gsb = Buf("gsb")
    cva, b_cva = gt[4], b_gt[4]
    cvb, b_cvb = gt[5], b_gt[5]
    gg = st16
    b_gg = b_st16
    hho = sb("hho", [128, 2, DC], BF16)
    b_hho = Buf("hho")
    hh2 = sb("hh2", [128, 2, DC], BF16)
    b_hh2 = Buf("hh2")
    ghal = sb("ghal", [128, FB, 2], F32)
    b_ghal = Buf("ghal")

    banks = [nc.alloc_psum_tensor("bank%d" % i, [128, 512], F32) for i in range(8)]
    b_banks = [Buf("bank%d" % i) for i in range(8)]

    def pcol(nm, off, n=1):
        return params[:, pc[nm] + off: pc[nm] + off + n]

    evac_rr = [0]

    def evac_eng():
        evac_rr[0] += 1
        return "act" if evac_rr[0] % 2 == 0 else "dve"

    def copy_op(eng, out, in_, r, w, scale=None):
        if eng == "act":
            if scale is None:
                S.op("act", lambda E: E.activation(out, in_, ACTF.Copy), r=r, w=w)
            else:
                S.op("act", lambda E: E.activation(out, in_, ACTF.Copy, scale=scale), r=r, w=w)
        else:
            if scale is None:
                S.op("dve", lambda E: E.tensor_copy(out, in_), r=r, w=w)
            else:
                S.op("dve", lambda E: E.tensor_scalar(out, in_, scale, None, ALU.mult), r=r, w=w)

    def slab_src(l, nm, ng, kg):
        K, N = c.mats[nm]
        kc = c.kc(K)
        KG = K // 128 // kc
        srows = kc * 32
        r0 = (ng * KG + kg) * srows
        t = wfull[(l, nm)]
        return bass.AP(t, r0 * 2048, [(kc * 512, 128), (1, kc * 512)]), kc

    def load_slab(l, nm, ng, kg):
        src, kc = slab_src(l, nm, ng, kg)
        sl, bsl = next_slot()
        dst = sl[:, 0:kc * 512]
        S.dma("sp", lambda E: E.dma_start(out=dst, in_=src), bsl, r=b_wfull[(l, nm)], w=[bsl])
        return sl, bsl, kc

    def proj_group(l, nm, ng, kgs, rhs_fn, rbufs_fn, split=False):
        K, N = c.mats[nm]
        assert len(kgs) <= NSLOT - 1
        loaded = [load_slab(l, nm, ng, kg) for kg in kgs] if split else None
        order = [(half, kgi) for half in range(2) for kgi in range(len(kgs))] if split else [(None, kgi) for kgi in range(len(kgs))]
        for half, kgi in order:
            kg = kgs[kgi]
            sl, bsl, kc = loaded[kgi] if split else load_slab(l, nm, ng, kg)
            slv = sl[:, 0:kc * 512].rearrange("p (c n) -> p c n", c=kc)
            for b in (range(4) if half is None else (2 * half, 2 * half + 1)):
                for tb in range(NTB):
                    bk = banks[b * NTB + tb]
                    bb = b_banks[b * NTB + tb]

                    def fn(E, b=b, tb=tb, bk=bk, slv=slv, kc=kc, kg=kg, kgi=kgi):
                        last = None
                        for cc in range(kc):
                            cg = kg * kc + cc
                            last = E.matmul(bk[:, 0:TB], lhsT=slv[:, cc, b * 128:(b + 1) * 128], rhs=rhs_fn(cg, tb),
                                            start=(kgi == 0 and cc == 0), stop=(kgi == len(kgs) - 1 and cc == kc - 1))
                        return last
                    rb = [bsl]
                    for cc in range(kc):
                        rb += rbufs_fn(kg * kc + cc)
                    S.op("pe", fn, r=rb, w=[bb])

    S.dma("sp", lambda E: E.dma_start(out=params[:, :], in_=params_d[:, :]), b_params, w=[b_params])
    S.dma("sp", lambda E: E.dma_start(out=csc[:, :, :], in_=csc_d[:, :].rearrange("p (c n) -> p c n", c=GCN)), b_csc, w=[b_csc])
    S.op("dve", lambda E: E.memset(ones_bf[:, :], 1.0), w=[b_ones])
    S.dma("sp", lambda E: E.dma_start(out=XS[:, :], in_=xT_in[:, :]), b_XS[0], w=b_XS)
    for h in range(H):
        S.dma("sp", lambda E, h=h: E.dma_start(out=Wb[:, :], in_=tr_d[h:h + 1, :].partition_broadcast(128)), b_xst[0], w=b_xst)
        S.dma("sp", lambda E, h=h: E.dma_start(out=MB[h * 128:(h + 1) * 128, :], in_=Wb[:, :]), b_xst[0], r=b_xst, w=[b_MB[h]])

    def chunked_ag(src, dst, rows, cr, groups, nr, rbufs, wbufs, qos=None):
        assert rows % cr == 0 and len(wbufs) == rows // cr
        for i in range(rows // cr):
            S.cc(lambda E, i=i: E.collective_compute("AllGather", ALU.bypass, replica_groups=groups,
                                                     ins=[src[i * cr:(i + 1) * cr, :]], outs=[dst[i * nr * cr:(i + 1) * nr * cr, :]],
                                                     dma_qos=qos),
                 r=rbufs, w=[wbufs[i]])

    castsem = [Buf("castsem%d" % i) for i in range(2)]
    castctr = [0]

    def prologue(l, names):
        for nm in names:
            rr = c.rows(nm)
            step = 1024
            pieces = []
            for r0 in range(0, rr, step):
                r1 = min(rr, r0 + step)
                bpiece = Buf("wshp")
                pieces.append(bpiece)
                cs_ = castsem[castctr[0] % 2]
                castctr[0] += 1
                S.dma("pool", lambda E, nm=nm, r0=r0, r1=r1: E.dma_start(
                    out=wsh[(l, nm)][r0:r1, :], in_=win[nm][l * c.rows(nm) + r0: l * c.rows(nm) + r1, :]),
                    cs_, w=[bpiece])
            if AG_MODE == "8":
                b_wfull[(l, nm)] = [Buf("wf")]
                S.cc(lambda E, nm=nm: E.collective_compute("AllGather", ALU.bypass, replica_groups=[list(range(8))],
                                                           ins=[wsh[(l, nm)].ap().opt()], outs=[wfull[(l, nm)].ap().opt()]),
                     r=pieces, w=b_wfull[(l, nm)])
            else:
                _, cra, crb = c.wchunks(nm)
                bmid = [Buf("wmid") for _ in range(rr // cra)]
                chunked_ag(wsh[(l, nm)], wmid[(l, nm)], rr, cra, [[0, 4], [1, 5], [2, 6], [3, 7]], 2, pieces, bmid, qos="P2")
                b_wfull[(l, nm)] = [Buf("wf") for _ in range(2 * rr // crb)]
                chunked_ag(wmid[(l, nm)], wfull[(l, nm)], 2 * rr, crb, [[0, 1, 2, 3], [4, 5, 6, 7]], 4, bmid, b_wfull[(l, nm)], qos="P2")

    PAIRS = [[0, 1], [2, 3], [4, 5], [6, 7]]

    def pair_ag(src, dst, bsrc, bdst, rows, cr):
        chunked_ag(src, dst, rows, cr, PAIRS, 2, [bsrc], bdst)

    def rmsnorm(gain_col, dst_ap_fn, dst_buf_fn, to_dram=None):
        for ci in range(DC):
            xt, bxt = next_xst()
            S.dma("sp", lambda E, ci=ci, xt=xt: E.dma_start(out=xt[:, :], in_=XS[ci * 128:(ci + 1) * 128, :]), bxt,
                  r=[b_XS[ci]], w=[bxt])
            sq, bsq = next_st16()
            S.op("act", lambda E, xt=xt, sq=sq: E.activation(sq[:, :], xt[:, :], ACTF.Square), r=[bxt], w=[bsq])
            for tb in range(NTB):
                S.op("pe", lambda E, sq=sq, tb=tb, ci=ci: E.matmul(banks[tb][:, 0:TB], lhsT=ones_bf[:, :], rhs=sq[:, tb * TB:(tb + 1) * TB],
                                                                start=(ci == 0), stop=(ci == DC - 1)),
                     r=[bsq, b_ones], w=[b_banks[tb]])
        for tb in range(NTB):
            S.op("act", lambda E, tb=tb: E.activation(rstd[:, tb * TB:(tb + 1) * TB], banks[tb][:, 0:TB], ACTF.Sqrt,
                                                      bias=eps_t[:, 0:1], scale=1.0 / D), r=[b_banks[tb], b_eps], w=[b_rstd])
        S.op("dve", lambda E: E.reciprocal(rstd[:, :], rstd[:, :]), r=[b_rstd], w=[b_rstd])
        for ci in range(DC):
            xt, bxt = next_xst()
            S.dma("sp", lambda E, ci=ci, xt=xt: E.dma_start(out=xt[:, :], in_=XS[ci * 128:(ci + 1) * 128, :]), bxt,
                  r=[b_XS[ci]], w=[bxt])
            if to_dram is None:
                S.op("dve", lambda E, ci=ci, xt=xt: E.scalar_tensor_tensor(dst_ap_fn(ci), xt[:, :], gain_col(ci), rstd[:, :],
                                                                         ALU.mult, ALU.mult),
                     r=[bxt, b_rstd, b_params], w=[dst_buf_fn(ci)])
            else:
                S.op("dve", lambda E, ci=ci, xt=xt: E.scalar_tensor_tensor(xt[:, :], xt[:, :], gain_col(ci), rstd[:, :],
                                                                         ALU.mult, ALU.mult),
                     r=[bxt, b_rstd, b_params], w=[bxt])
                S.dma("act", lambda E, ci=ci, xt=xt: E.dma_start(out=to_dram[ci * 128:(ci + 1) * 128, :], in_=xt[:, :]), bxt,
                      r=[bxt], w=[b_out[ci]])

    eps_t = sb("eps_t", [128, 1], F32)
    b_eps = Buf("eps")
    S.op("dve", lambda E: E.memset(eps_t[:, :], EPS), w=[b_eps])

    def x_rmw(j, bank_ids):
        xt, bxt = next_xst()
        S.dma("sp", lambda E, xt=xt: E.dma_start(out=xt[:, :], in_=XS[j * 128:(j + 1) * 128, :]), bxt, r=[b_XS[j]], w=[bxt])
        for tb in range(NTB):
            bi = bank_ids[tb]
            S.op("dve", lambda E, xt=xt, tb=tb, bi=bi: E.tensor_tensor(xt[:, tb * TB:(tb + 1) * TB], xt[:, tb * TB:(tb + 1) * TB],
                                                                     banks[bi][:, 0:TB], ALU.add), r=[bxt, b_banks[bi]], w=[bxt])
        S.dma("act", lambda E, xt=xt: E.dma_start(out=XS[j * 128:(j + 1) * 128, :], in_=xt[:, :]), bxt, r=[bxt], w=[b_XS[j]])

    prologue(0, ["w_in", "w_gate"])

    for l in range(L):
        S.epoch = l
        lam_init = 0.8 - 0.6 * math.exp(-0.3 * l)
        S.dma("sp", lambda E, l=l: E.dma_start(out=lamt[:, :], in_=lamb_d[:, l * 512:(l + 1) * 512]), b_lamt, w=[b_lamt])
        for i in range(2):
            S.op("dve", lambda E, i=i: E.tensor_tensor(ltmp[:, :], lamt[:, (2 * i) * 128:(2 * i + 1) * 128],
                                                     lamt[:, (2 * i + 1) * 128:(2 * i + 2) * 128], ALU.mult), r=[b_lamt], w=[b_ltmp])
            S.op("dve", lambda E, i=i: E.tensor_reduce(lsc[:, i:i + 1], ltmp[:, :], AX.X, ALU.add), r=[b_ltmp], w=[b_lsc])
        S.op("act", lambda E: E.activation(lsc[:, 2:4], lsc[:, 0:2], ACTF.Exp), r=[b_lsc], w=[b_lsc])
        S.op("dve", lambda E, li=lam_init: E.scalar_tensor_tensor(lsc[:, 4:5], lsc[:, 3:4], -li, lsc[:, 2:3], ALU.add, ALU.subtract),
             r=[b_lsc], w=[b_lsc])
        S.op("dve", lambda E, l=l, li=lam_init: E.tensor_scalar(lsc[:, 5:7], pcol("sg", l * 2, 2), 1.0 - li, None, ALU.mult),
             r=[b_params], w=[b_lsc])

        rmsnorm(lambda ci, l=l: pcol("g1", l * DC + ci), lambda ci: R1[:, ci, :], lambda ci: b_R1[ci])

        hr = lambda cg, tb: R1[:, cg, tb * TB:(tb + 1) * TB]
        hb = lambda cg: [b_R1[cg]]
        KGin = DC // c.kc(D)
        allkg = list(range(KGin))
        n_uf, n_q = NF // 512, DA // 512
        for ng in range(DIN // 512):
            if ng < n_uf + 2 * n_q:
                proj_group(l, "w_in", ng, allkg, hr, hb)
                for b in range(4):
                    if ng < n_uf:
                        fbk = ng * 4 + b
                        for tb in range(NTB):
                            bi = b * NTB + tb
                            copy_op(evac_eng(), ufT[:, fbk, tb * TB:(tb + 1) * TB], banks[bi][:, 0:TB], [b_banks[bi]], [b_ufT[fbk]])
                    else:
                        isq = ng < n_uf + n_q
                        blk = (ng - n_uf - (0 if isq else n_q)) * 4 + b
                        st, bst = next_st16()
                        for tb in range(NTB):
                            bi = b * NTB + tb
                            copy_op(evac_eng(), st[:, tb * TB:(tb + 1) * TB], banks[bi][:, 0:TB], [b_banks[bi]], [bst],
                                    scale=(128 ** -0.5) if isq else None)
                        dstT = QX if isq else KX
                        S.dma("act", lambda E, st=st, blk=blk, dstT=dstT: E.dma_start(out=dstT[blk * 128:(blk + 1) * 128, :], in_=st[:, :]),
                              bst, r=[bst], w=[b_QX if isq else b_KX])
                if ng == n_uf - 1:
                    for tt in range(TT):
                        for g in range(4):
                            bi = (tt * 4 + g) % 8

                            def fn(E, tt=tt, g=g, bi=bi):
                                last = None
                                for cc in range(GCN):
                                    last = E.matmul(banks[bi][:, 0:2 * GC], lhsT=ufT[:, g * GCN + cc, tt * 128:(tt + 1) * 128],
                                                    rhs=csc[:, cc, :], start=(cc == 0), stop=(cc == GCN - 1))
                                return last
                            S.op("pe", fn, r=[b_ufT[g * GCN + cc] for cc in range(GCN)] + [b_csc], w=[b_banks[bi]])
                            tk, btk = next_tk()
                            copy_op(evac_eng(), tk[:, 0:2 * GC], banks[bi][:, 0:2 * GC], [b_banks[bi]], [btk])
                            S.dma("act", lambda E, tk=tk, tt=tt, g=g: E.dma_start(
                                out=AFX[tt * 128:(tt + 1) * 128, g * 2 * GC:(g + 1) * 2 * GC], in_=tk[:, 0:2 * GC]), btk,
                                r=[btk], w=[b_AFX])
                    pair_ag(AFX, AFG, b_AFX, b_AFG, T, crA)
                if ng == n_uf + 2 * n_q - 1:
                    pair_ag(KX, KG, b_KX, b_KG, NAB * 128, crK)
            else:
                gv = ng - n_uf - 2 * n_q
                for kgi, kg in enumerate(allkg):
                    sl, bsl, kc = load_slab(l, "w_in", ng, kg)
                    slv = sl[:, 0:kc * 512].rearrange("p (c n) -> p c n", c=kc)
                    for tt in range(TT):
                        def fn(E, tt=tt, slv=slv, kc=kc, kg=kg, kgi=kgi):
                            last = None
                            for cc in range(kc):
                                cg = kg * kc + cc
                                last = E.matmul(banks[tt][:, :], lhsT=R1[:, cg, tt * 128:(tt + 1) * 128], rhs=slv[:, cc, :],
                                                start=(kgi == 0 and cc == 0), stop=(kgi == len(allkg) - 1 and cc == kc - 1))
                            return last
                        S.op("pe", fn, r=[bsl] + [b_R1[kg * kc + cc] for cc in range(kc)], w=[b_banks[tt]])
                for tt in range(TT):
                    tk, btk = next_tk()
                    copy_op(evac_eng(), tk[:, :], banks[tt][:, :], [b_banks[tt]], [btk])
                    S.dma("act", lambda E, tk=tk, tt=tt, gv=gv: E.dma_start(out=VX[tt * 128:(tt + 1) * 128, gv * 512:(gv + 1) * 512], in_=tk[:, :]),
                          btk, r=[btk], w=[b_VX])
        pair_ag(VX, VG, b_VX, b_VG, T, crV)
        prologue(l, ["w_fo", "w_ao", "w_o", "w_up", "w_down"] if l == 0 else ["w_up", "w_down"])

        for ng in range(2 * D // 512):
            proj_group(l, "w_gate", ng, allkg, hr, hb)
            for b in range(4):
                blk = ng * 4 + b
                st, bst = next_st16()
                for tb in range(NTB):
                    bi = b * NTB + tb
                    S.op("act", lambda E, st=st, tb=tb, bi=bi, blk=blk, l=l: E.activation(
                        st[:, tb * TB:(tb + 1) * TB], banks[bi][:, 0:TB], ACTF.Sigmoid, bias=pcol("bg", l * 2 * DC + blk)),
                        r=[b_banks[bi], b_params], w=[bst])
                S.dma("act", lambda E, st=st, blk=blk: E.dma_start(out=GS[blk * 128:(blk + 1) * 128, :], in_=st[:, :]), bst,
                      r=[bst], w=[b_GS[blk]])


        for sb_ in range(NTB):
            for sc in range(KCH):
                sl, bsl = next_slot()
                afv = sl[:, 0:2 * NF]
                arow = ag_row((sc % TT) * 128, sc // TT, crA, 2)
                S.dma("sp", lambda E, afv=afv, arow=arow: E.dma_start(out=afv, in_=AFG[arow:arow + 128, :]), bsl, r=b_AFG, w=[bsl])
                tl, btl = next_slot()
                tv = tl[:, 0:2 * TB].rearrange("p (a n) -> p a n", a=2)
                src = bass.AP(ct_d, sc * 128 * T + sb_ * TB, [(T, 128), (2 * T * T, 2), (1, TB)])
                S.dma("sp", lambda E, tv=tv, src=src: E.dma_start(out=tv, in_=src), btl, w=[btl])
                for fb in range(NFB):
                    g, j = fb // GCN, fb % GCN
                    c0 = g * 2 * GC + j * 128

                    def fn(E, fb=fb, c0=c0, afv=afv, tv=tv, sc=sc):
                        E.matmul(banks[fb][:, 0:TB], lhsT=afv[:, c0:c0 + 128], rhs=tv[:, 0, :], start=(sc == 0), stop=False)
                        return E.matmul(banks[fb][:, 0:TB], lhsT=afv[:, c0 + GC:c0 + GC + 128], rhs=tv[:, 1, :], start=False,
                                        stop=(sc == KCH - 1))
                    S.op("pe", fn, r=[bsl, btl], w=[b_banks[fb]])
            for fb in range(NFB):
                copy_op(evac_eng(), R1[:, NAB + fb, sb_ * TB:(sb_ + 1) * TB], banks[fb][:, 0:TB], [b_banks[fb]], [b_R1[NAB + fb]])

        for h in range(H):
            S.dma("sp", lambda E, h=h: E.dma_start(out=Wb[:, 0:WL], in_=bass.AP(MB, h * 128 * LB + 127, [(LB - 1, 128), (1, WL)])),
                  b_xst[0], r=[b_MB[h]], w=b_xst)
            ksl, bks = next_slot()
            kv = ksl[:, 0:2 * 2 * T].rearrange("p (m r t) -> p m r t", m=2, r=2)
            for m in range(2):
                krow = ag_row((2 * h + m) * 128, 0, crK, 2)
                src = bass.AP(KG, krow * T, [(T, 128), (crK * T, 2), (1, T)])
                S.dma("sp", lambda E, m=m, src=src, kv=kv: E.dma_start(out=kv[:, m, :, :], in_=src), bks, r=b_KG, w=[bks])
            kfl = ksl[:, 0:2 * 2 * T].rearrange("p (m k) -> p m k", m=2)
            vsl, bvs = next_slot()
            vv = vsl[:, 0:KCH * 256].rearrange("p (k e) -> p k e", k=KCH)
            nvc = crV // 128

            def vload(E, h=h, vv=vv):
                res = []
                for r_ in range(2):
                    for i_ in range(T // crV):
                        vrow = ag_row(i_ * crV, r_, crV, 2)
                        k0 = r_ * TT + i_ * nvc
                        res.append(E.dma_start(out=vv[:, k0:k0 + nvc, :],
                                               in_=bass.AP(VG, vrow * DA + h * 256, [(DA, 128), (128 * DA, nvc), (1, 256)])))
                return res
            S.dma("sp", vload, bvs, r=b_VG, w=[bvs], n=2 * (T // crV))
            qsl, bqs = next_slot()
            qv = qsl[:, 0:2 * T].rearrange("p (m t) -> p m t", m=2)
            S.dma("sp", lambda E, h=h, qv=qv: E.dma_start(out=qv, in_=bass.AP(QX, 2 * h * 128 * T, [(T, 128), (128 * T, 2), (1, T)])),
                  bqs, r=[b_QX], w=[bqs])
            for qb in range(NTB):
                for m in range(2):
                    acc = [m * 3 + 0, m * 3 + 1, m * 3 + 2]
                    pend = None
                    for kc_ in range(KCH + 1):
                        if kc_ < KCH:
                            sbk = 6 + (kc_ % 2)
                            S.op("pe", lambda E, m=m, kc_=kc_, sbk=sbk, qb=qb, kfl=kfl, qv=qv: E.matmul(
                                banks[sbk][:, 0:TB], lhsT=kfl[:, m, kc_ * 128:(kc_ + 1) * 128], rhs=qv[:, m, qb * TB:(qb + 1) * TB],
                                start=True, stop=True), r=[bks, bqs], w=[b_banks[sbk]])
                            ti = kc_ % 2
                            jb = 2 * T - kc_ * 128 + qb * TB - 128
                            S.op("dve", lambda E, ti=ti, sbk=sbk, jb=jb: E.tensor_tensor(tmpS[ti][:, 0:TB], banks[sbk][:, 0:TB],
                                                                                        Wb[:, jb:jb + TB], ALU.add),
                                 r=[b_banks[sbk]] + b_xst, w=[b_tmpS[ti]])
                            pi = kc_ % 3
                            S.op("act", lambda E, ti=ti, pi=pi: E.activation(Pt[pi][:, :], tmpS[ti][:, 0:TB], ACTF.Exp),
                                 r=[b_tmpS[ti]], w=[b_Pt[pi]])
                        if pend is not None:
                            pk, ppi = pend

                            def fn(E, pk=pk, ppi=ppi, acc=acc, vv=vv):
                                E.matmul(banks[acc[0]][:, 0:TB], lhsT=vv[:, pk, 0:128], rhs=Pt[ppi][:, :], start=(pk == 0), stop=(pk == KCH - 1))
                                E.matmul(banks[acc[1]][:, 0:TB], lhsT=vv[:, pk, 128:256], rhs=Pt[ppi][:, :], start=(pk == 0), stop=(pk == KCH - 1))
                                return E.matmul(banks[acc[2]][:, 0:TB], lhsT=ones_bf[:, :], rhs=Pt[ppi][:, :], start=(pk == 0), stop=(pk == KCH - 1))
                            S.op("pe", fn, r=[bvs, b_Pt[ppi], b_ones], w=[b_banks[a] for a in acc])
                        pend = (kc_, kc_ % 3) if kc_ < KCH else None
                    S.op("act", lambda E, acc=acc: E.activation(rs[:, :], banks[acc[2]][:, 0:TB], ACTF.Ln), r=[b_banks[acc[2]]], w=[b_rs])
                    S.op("act", lambda E: E.activation(rs[:, :], rs[:, :], ACTF.Exp, scale=-1.0), r=[b_rs], w=[b_rs])
                    for j in range(2):
                        S.op("dve", lambda E, m=m, j=j, acc=acc: E.tensor_tensor(on[m][j][:, 0:TB], banks[acc[j]][:, 0:TB], rs[:, :], ALU.mult),
                             r=[b_banks[acc[j]], b_rs], w=[b_on[m][j]])
                for j in range(2):
                    S.op("dve", lambda E, j=j: E.scalar_tensor_tensor(on[0][j][:, 0:TB], on[1][j][:, 0:TB], lsc[:, 4:5], on[0][j][:, 0:TB],
                                                                    ALU.mult, ALU.add), r=[b_on[1][j], b_on[0][j], b_lsc], w=[b_on[0][j]])
                    S.op("act", lambda E, j=j: E.activation(sq16[j][:, :], on[0][j][:, 0:TB], ACTF.Square), r=[b_on[0][j]], w=[b_sq16[j]])
                    S.op("pe", lambda E, j=j: E.matmul(banks[2][:, 0:TB], lhsT=ones_bf[:, :], rhs=sq16[j][:, :], start=(j == 0), stop=(j == 1)),
                         r=[b_sq16[j], b_ones], w=[b_banks[2]])
                S.op("act", lambda E: E.activation(rs[:, :], banks[2][:, 0:TB], ACTF.Ln, bias=eps_t[:, 0:1], scale=1.0 / 256),
                     r=[b_banks[2], b_eps], w=[b_rs])
                S.op("act", lambda E: E.activation(rs[:, :], rs[:, :], ACTF.Exp, scale=-0.5), r=[b_rs], w=[b_rs])
                for j in range(2):
                    S.op("dve", lambda E, j=j, h=h, qb=qb: E.scalar_tensor_tensor(R1[:, 2 * h + j, qb * TB:(qb + 1) * TB], on[0][j][:, 0:TB],
                                                                               lsc[:, 5 + j:6 + j], rs[:, :], ALU.mult, ALU.mult),
                         r=[b_on[0][j], b_rs, b_lsc], w=[b_R1[2 * h + j]])

        KCo = c.kc(D)
        KGo = DC // KCo
        hk = DC // 2
        for kh in range(2):
            for jg in range(hk // 4):
                ng = kh * (hk // 4) + jg
                fr = lambda cg, tb: R1[:, NAB + cg, tb * TB:(tb + 1) * TB]
                fbuf = lambda cg: [b_R1[NAB + cg]]
                proj_group(l, "w_fo", ng, list(range(NFB // c.kc(NF))), fr, fbuf, split=True)
                for b in range(4):
                    j = ng * 4 + b
                    st, bst = next_st16()
                    S.dma("sp", lambda E, st=st, j=j: E.dma_start(out=st[:, :], in_=GS[j * 128:(j + 1) * 128, :]), bst, r=[b_GS[j]], w=[bst])
                    for tb in range(NTB):
                        bi = b * NTB + tb
                        S.op("dve", lambda E, b=b, tb=tb, bi=bi, st=st: E.tensor_tensor(yf32[b][:, tb * TB:(tb + 1) * TB], banks[bi][:, 0:TB],
                                                                                      st[:, tb * TB:(tb + 1) * TB], ALU.mult),
                             r=[b_banks[bi], bst], w=[b_yf32[b]])
                ar = lambda cg, tb: R1[:, cg, tb * TB:(tb + 1) * TB]
                abuf = lambda cg: [b_R1[cg]]
                proj_group(l, "w_ao", ng, list(range(NAB // c.kc(DA))), ar, abuf, split=True)
                for b in range(4):
                    j = ng * 4 + b
                    st, bst = next_st16()
                    S.dma("sp", lambda E, st=st, j=j: E.dma_start(out=st[:, :], in_=GS[(DC + j) * 128:(DC + j + 1) * 128, :]), bst,
                          r=[b_GS[DC + j]], w=[bst])
                    for tb in range(NTB):
                        bi = b * NTB + tb
                        S.op("dve", lambda E, tb=tb, bi=bi, st=st: E.tensor_tensor(cva[:, tb * TB:(tb + 1) * TB], banks[bi][:, 0:TB],
                                                                                 st[:, tb * TB:(tb + 1) * TB], ALU.mult),
                             r=[b_banks[bi], bst], w=[b_cva])
                        S.op("dve", lambda E, b=b, tb=tb, jg=jg: E.tensor_tensor(R2[:, jg * 4 + b, tb * TB:(tb + 1) * TB], cva[:, tb * TB:(tb + 1) * TB],
                                                                               yf32[b][:, tb * TB:(tb + 1) * TB], ALU.add),
                             r=[b_cva, b_yf32[b]], w=[b_R2[jg * 4 + b]])
            mr = lambda cg, tb, kh=kh: R2[:, cg - kh * hk, tb * TB:(tb + 1) * TB]
            mbuf = lambda cg, kh=kh: [b_R2[cg - kh * hk]]
            kgs = [kh * (hk // KCo) + i for i in range(hk // KCo)]
            for og in range(D // 512):
                proj_group(l, "w_o", og, kgs, mr, mbuf, split=True)
                for b in range(4):
                    x_rmw(og * 4 + b, [b * NTB + tb for tb in range(NTB)])

        rmsnorm(lambda ci, l=l: pcol("g2", l * DC + ci), lambda ci: R1[:, ci, :], lambda ci: b_R1[ci])
        S.op("dve", lambda E: E.tensor_copy(hho[:, 0, :], R1[:, :, 0]), r=b_R1, w=[b_hho])
        S.op("dve", lambda E: E.tensor_copy(hho[:, 1, :], R1[:, :, T - 1]), r=b_R1, w=[b_hho])
        S.dma("act", lambda E: E.dma_start(out=HX[:, :], in_=hho[:, :, :].rearrange("p a c -> p (a c)")), b_hho, r=[b_hho], w=[b_HX])
        pair_ag(HX, HG, b_HX, b_HG, 128, 128)
        S.dma("sp", lambda E: [E.dma_start(out=hh2[:, 0, :], in_=HG[0:128, DC:2 * DC]),
                               E.dma_start(out=hh2[:, 1, :], in_=HG[128:256, 0:DC])], b_hh2, r=b_HG, w=[b_hh2], n=2)
        if l + 1 < L:
            prologue(l + 1, ["w_in", "w_gate", "w_fo", "w_ao", "w_o"])

        first_f0 = [True]
        for ng in range(DFF // 512):
            for kgi, kg in enumerate(allkg):
                sl, bsl, kc = load_slab(l, "w_up", ng, kg)
                slv = sl[:, 0:kc * 512].rearrange("p (c n) -> p c n", c=kc)

                def fn(E, ng=ng, kg=kg, kc=kc, slv=slv, first_f0=first_f0):
                    last = None
                    for b in range(4):
                        blk = ng * 4 + b
                        for cc in range(kc):
                            cg = kg * kc + cc
                            st_ = first_f0[0]
                            first_f0[0] = False
                            last = E.matmul(banks[7][:, blk * 2:blk * 2 + 2], lhsT=slv[:, cc, b * 128:(b + 1) * 128], rhs=hh2[:, :, cg],
                                            start=st_, stop=(cg == DC - 1), skip_group_check=True)
                    return last
                S.op("pe", fn, r=[bsl, b_hh2], w=[b_banks[7]])
        bk7 = banks[7][:, 0:FB * 2].rearrange("p (f a) -> p f a", a=2)
        for a in range(2):
            S.op("dve", lambda E, a=a: E.tensor_scalar(ghal[:, :, a], bk7[:, :, a], pcol("mk", a), None, ALU.mult),
                 r=[b_banks[7], b_params], w=[b_ghal])

        KCd = c.kc(DFF)
        for part in range(c.NPART):
            for fg in range(4):
                ngg = part * 4 + fg
                proj_group(l, "w_up", ngg, allkg, hr, hb)
                for b in range(4):
                    fblk = ngg * 4 + b
                    for tb in range(NTB):
                        bi = b * NTB + tb
                        S.op("act", lambda E, tb=tb, bi=bi: E.activation(gsb[:, 1 + tb * TB:1 + (tb + 1) * TB], banks[bi][:, 0:TB], ACTF.Copy),
                             r=[b_banks[bi]], w=[b_gsb])
                    S.op("dve", lambda E, fblk=fblk: E.tensor_copy(gsb[:, 0:1], ghal[:, fblk, 0:1]), r=[b_ghal], w=[b_gsb])
                    S.op("dve", lambda E, fblk=fblk: E.tensor_copy(gsb[:, T + 1:T + 2], ghal[:, fblk, 1:2]), r=[b_ghal], w=[b_gsb])
                    cw = lambda j, fblk=fblk, l=l: pcol("cw", (l * 3 + j) * FB + fblk)
                    S.op("dve", lambda E, fblk=fblk, l=l, cw=cw: E.tensor_scalar(cva[:, :], gsb[:, 0:T], cw(0), pcol("cb", l * FB + fblk),
                                                                               ALU.mult, ALU.add), r=[b_gsb, b_params], w=[b_cva])
                    S.op("dve", lambda E, cw=cw: E.scalar_tensor_tensor(cvb[:, :], gsb[:, 1:T + 1], cw(1), cva[:, :], ALU.mult, ALU.add),
                         r=[b_gsb, b_cva, b_params], w=[b_cvb])
                    S.op("dve", lambda E, cw=cw: E.scalar_tensor_tensor(cva[:, :], gsb[:, 2:T + 2], cw(2), cvb[:, :], ALU.mult, ALU.add),
                         r=[b_gsb, b_cvb, b_params], w=[b_cva])
                    S.op("act", lambda E, b=b: E.activation(gg[b][:, :], cva[:, :], ACTF.Gelu), r=[b_cva], w=[b_gg[b]])
                ngv = DFF // 512 + ngg
                proj_group(l, "w_up", ngv, allkg, hr, hb)
                for b in range(4):
                    for tb in range(NTB):
                        bi = b * NTB + tb
                        S.op("dve", lambda E, b=b, tb=tb, bi=bi, fg=fg: E.tensor_tensor(R2[:, fg * 4 + b, tb * TB:(tb + 1) * TB], banks[bi][:, 0:TB],
                                                                                      gg[b][:, tb * TB:(tb + 1) * TB], ALU.mult),
                             r=[b_banks[bi], b_gg[b]], w=[b_R2[fg * 4 + b]])
            ar2 = lambda cg, tb, part=part: R2[:, cg - part * 16, tb * TB:(tb + 1) * TB]
            ab2 = lambda cg, part=part: [b_R2[cg - part * 16]]
            kgs = [part * (16 // KCd) + i for i in range(16 // KCd)]
            for og in range(D // 512):
                proj_group(l, "w_down", og, kgs, ar2, ab2, split=True)
                for b in range(4):
                    x_rmw(og * 4 + b, [b * NTB + tb for tb in range(NTB)])

    S.epoch = L
    rmsnorm(lambda ci: pcol("gf", ci), None, None, to_dram=outT)
    S.op("sp", None, r=b_out + b_XS, w=[])
    S.op("pool", None, r=b_out + b_XS, w=[])
    nw = S.emit()
    return nc, (len(S.ops), nw)


def rel_bucket_np(rel):
    n = -rel
    half = NREL // 2
    ret = (n < 0).astype(np.int32) * half
    n = np.abs(n)
    max_exact = half // 2
    is_small = n < max_exact
    nf = np.maximum(n, 1).astype(np.float32)
    large = max_exact + (np.log(nf / np.float32(max_exact)) / np.float32(math.log(RELMAX / max_exact))
                         * np.float32(half - max_exact)).astype(np.int32)
    large = np.minimum(large, half - 1)
    return ret + np.where(is_small, n, large)


def stream_layout(cfg, W, K, N):
    kc = cfg.kc(K)
    KG = K // 128 // kc
    NG = N // 512
    a = W.reshape(KG, kc, 128, NG, 512).transpose(3, 0, 2, 1, 4)
    return np.ascontiguousarray(a).reshape(8, -1, 2048)


def prep_inputs(cfg, x, norm1_gain, w_in, w_fourier_out, lambdas, subln_gain, rel_bias_table, w_attn_out,
                w_gate, b_gate, w_o, norm2_gain, w_up, conv_w, conv_b, w_down, final_norm_gain):
    c = cfg
    L, D, T, DC, FB = c.L, c.D, c.T, c.DC, c.FB
    f32 = np.float32
    x = np.asarray(x, f32)
    src = {"w_in": w_in, "w_gate": w_gate, "w_fo": w_fourier_out, "w_ao": w_attn_out, "w_o": w_o, "w_up": w_up, "w_down": w_down}
    wsh = {}
    for nm, (K, N) in c.mats.items():
        rr, cra, crb = c.wchunks(nm)
        perm = weight_perm(rr, cra, crb) if AG_MODE != "8" else np.arange(8 * rr)
        per_layer = []
        for l in range(L):
            st = stream_layout(c, np.asarray(src[nm][l], f32), K, N).reshape(8 * rr, 2048)
            sh = np.empty_like(st)
            sh[perm] = st
            per_layer.append(sh.reshape(8, rr, 2048))
        wsh[nm] = np.stack(per_layer, axis=1)
    pc = c.np_cols()

    def fm(a, nblk):
        a = np.asarray(a, f32)
        lead = a.shape[:-1]
        a = a.reshape(*lead, nblk, 128)
        return np.moveaxis(a, -1, 0).reshape(128, -1)
    base = np.zeros((128, pc["_n"]), f32)
    base[:, pc["g1"]:pc["g1"] + L * DC] = fm(norm1_gain, DC)
    base[:, pc["g2"]:pc["g2"] + L * DC] = fm(norm2_gain, DC)
    base[:, pc["gf"]:pc["gf"] + DC] = fm(final_norm_gain, DC)
    base[:, pc["bg"]:pc["bg"] + L * 2 * DC] = fm(b_gate, 2 * DC)
    base[:, pc["cw"]:pc["cw"] + L * 3 * FB] = fm(conv_w, FB)
    base[:, pc["cb"]:pc["cb"] + L * FB] = fm(conv_b, FB)
    base[:, pc["sg"]:pc["sg"] + L * 2] = fm(subln_gain, 2)
    lamb = np.ascontiguousarray(np.broadcast_to(np.asarray(lambdas, f32).reshape(1, L * 512), (128, L * 512)))
    GC, GCN, S_ = c.GC, c.GCN, c.S
    cidx = np.arange(GC)
    ang = 2.0 * np.pi * ((cidx[:, None] * cidx[None, :]) % GC) / GC
    cs = np.concatenate([np.cos(ang), np.sin(ang)], axis=1)
    csc = cs.reshape(GCN, 128, 2 * GC).transpose(1, 0, 2).reshape(128, GCN * 2 * GC).astype(ml_dtypes.bfloat16)
    tbl = np.asarray(rel_bias_table, f32)
    in_maps = []
    for core in range(8):
        b, hf = core // 2, core % 2
        m = {}
        m["xT"] = np.ascontiguousarray(x[b, hf * T:(hf + 1) * T, :].T)
        for nm in c.mats:
            m[nm] = wsh[nm][core].reshape(-1, 2048)
        p = base.copy()
        p[:, pc["mk"]] = 1.0 if hf == 1 else 0.0
        p[:, pc["mk"] + 1] = 1.0 if hf == 0 else 0.0
        m["params"] = p
        m["lamb"] = lamb
        m["csc"] = csc
        s = np.arange(2 * T)
        sp = hf * T + np.arange(T)
        a2 = 2.0 * np.pi * ((s[:, None] * sp[None, :]) % S_) / S_
        nrm = 1.0 / math.sqrt(S_ * GC)
        m["ct"] = np.concatenate([np.cos(a2) * nrm, -np.sin(a2) * nrm], axis=0).astype(ml_dtypes.bfloat16)
        N_ = 3 * T - 1
        i = np.arange(N_)
        rel = i - (T - 1) - hf * T
        TD = tbl[rel_bucket_np(rel)]
        TRv = np.zeros((c.H, c.LB), f32)
        TRv[:, :N_] = TD[::-1, :].T
        m["tr"] = TRv
        in_maps.append(m)
    return in_maps


_CACHE = {}


def run(cfg, inputs, safe_same=True):
    key = (cfg.D, cfg.S, cfg.L, cfg.TB, safe_same)
    if key not in _CACHE:
        _CACHE[key] = build(cfg, safe_same=safe_same)[0]
    nc = _CACHE[key]
    in_maps = prep_inputs(cfg, **inputs)
    res = run_bass_kernel_spmd(nc, in_maps, core_ids=list(range(8)))
    out = np.empty((cfg.B, cfg.S, cfg.D), np.float32)
    for core in range(8):
        b, hf = core // 2, core % 2
        out[b, hf * cfg.T:(hf + 1) * cfg.T, :] = res.results[core]["outT"].T
    return out


def kernel(**inputs):
    return run(Cfg(), inputs)
```
